# Optimizing a Trainium2 kernel written in Bass

```python
import math
import jax, jax.numpy as jnp
from jax import lax
import numpy as np

D_MODEL = 1024
BATCH = 32
SEQ = 2048
DEPTH = 1
DEC_BATCH = 16
DEC_SEQ = 4096
PAST_LEN = 128

D_CONV = D_MODEL // 2
D_HYENA = D_MODEL // 2
CONV_KERNEL = 31
SHORT_KERNEL = 3
HYENA_ORDER = 2
N_BANDS = 16
FILTER_EMB = 1 + 2 * N_BANDS
FILTER_HIDDEN = 64
N_DIRS = 2
D_FF = 4 * D_MODEL
N_MOD = 6
LN_EPS = 1e-5
DECAY_TARGET = 1e-2
FAST_DECAY_PCT = 0.3
SLOW_DECAY_PCT = 1.5
MAX_DECAY = math.log(DECAY_TARGET) / FAST_DECAY_PCT
MIN_DECAY = math.log(DECAY_TARGET) / SLOW_DECAY_PCT
IN_WIDTH = 2 * D_CONV + (HYENA_ORDER + 1) * D_HYENA + 2 * D_MODEL
DEEPNORM_ALPHA = (2.0 * DEPTH) ** 0.25
DEEPNORM_BETA = (8.0 * DEPTH) ** -0.25

kernel_name = "conditioned_conformer_hyena_hybrid_encoder"


def _ln(x, g=None, b=None):
    xf = x.astype(jnp.float32)
    mu = jnp.mean(xf, axis=-1, keepdims=True)
    var = jnp.mean(jnp.square(xf - mu), axis=-1, keepdims=True)
    y = (xf - mu) * lax.rsqrt(var + LN_EPS)
    if g is not None:
        y = y * g.astype(jnp.float32) + b.astype(jnp.float32)
    return y.astype(x.dtype)


def _dwconv(x, w, b):
    k = w.shape[0]
    pad = k // 2
    y = lax.conv_general_dilated(x, w[:, None, :].astype(x.dtype), window_strides=(1,),
                                 padding=[(pad, pad)], dimension_numbers=('NWC', 'WIO', 'NWC'),
                                 feature_group_count=x.shape[-1])
    return y + b.astype(x.dtype)


def _implicit_filters(L, w1, b1, freq, w2, b2, w3):
    f32 = jnp.float32
    t = jnp.linspace(0.0, 1.0, L, dtype=f32)[:, None]
    w = (2.0 * math.pi / L) * jnp.arange(L, dtype=f32)[:, None]
    bands = jnp.linspace(1e-4, N_BANDS - 1, N_BANDS, dtype=f32)[None, :]
    z = jnp.concatenate([t, jnp.cos(w * bands), -jnp.sin(w * bands)], axis=-1)
    fr = freq.astype(f32)
    hdn = jnp.sin(fr * (z @ w1.astype(f32) + b1.astype(f32)))
    hdn = jnp.sin(fr * (hdn @ w2.astype(f32) + b2.astype(f32)))
    h = (hdn @ w3.astype(f32)).reshape(L, HYENA_ORDER, N_DIRS, D_HYENA)
    deltas = jnp.abs(jnp.linspace(MIN_DECAY, MAX_DECAY, D_HYENA, dtype=f32))
    decay_f = jnp.exp(-t * deltas)[:, None, :]
    decay_b = jnp.exp(-t * deltas[::-1])[:, None, :]
    h_f = h[:, :, 0] * decay_f
    h_b = h[:, :, 1] * decay_b
    k = jnp.concatenate([h_f, jnp.zeros((1, HYENA_ORDER, D_HYENA), f32), h_b[1:][::-1]], axis=0)
    return jnp.fft.rfft(k, axis=0)


def _fftconv(u, hf, skip):
    L = u.shape[1]
    uf32 = u.astype(jnp.float32)
    uf = jnp.fft.rfft(uf32, n=2 * L, axis=1)
    y = jnp.fft.irfft(uf * hf[None], n=2 * L, axis=1)[:, :L]
    return (y + uf32 * skip.astype(jnp.float32)).astype(u.dtype)


def _mixer(h, w_in, conv_dw_w, conv_dw_b, conv_ln_g, conv_ln_b, conv_pw_w,
           hy_short_w, hy_short_b, filt, hy_skip, hy_out_w, w_out):
    proj = jnp.einsum('bld,de->ble', h, w_in)
    s0 = D_CONV
    s1 = 2 * D_CONV
    s2 = s1 + (HYENA_ORDER + 1) * D_HYENA
    s3 = s2 + D_MODEL
    a_val, a_gate, hy_in, g_a, g_b = jnp.split(proj, [s0, s1, s2, s3], axis=-1)
    a = a_val * jax.nn.sigmoid(a_gate)
    a = _dwconv(a, conv_dw_w, conv_dw_b)
    a = jax.nn.silu(_ln(a, conv_ln_g, conv_ln_b))
    y_a = jnp.einsum('blc,cd->bld', a, conv_pw_w)
    u = _dwconv(hy_in, hy_short_w, hy_short_b)
    v, x1, x2 = jnp.split(u, HYENA_ORDER + 1, axis=-1)
    zb = x1 * _fftconv(v, filt[:, 0], hy_skip[0])
    zb = x2 * _fftconv(zb, filt[:, 1], hy_skip[1])
    y_b = jnp.einsum('blc,cd->bld', zb, hy_out_w)
    m = jax.nn.sigmoid(g_a) * y_a + jax.nn.sigmoid(g_b) * y_b
    return jnp.einsum('bld,de->ble', m, w_out)


def _encoder(x, c, params):
    (w_ada, b_ada, w_in, conv_dw_w, conv_dw_b, conv_ln_g, conv_ln_b, conv_pw_w,
     hy_short_w, hy_short_b, hy_ffn_w1, hy_ffn_b1, hy_sin_freq, hy_ffn_w2, hy_ffn_b2,
     hy_ffn_w3, hy_skip, hy_out_w, w_out, ln1_g, ln1_b,
     mlp_w1, mlp_b1, mlp_w2, mlp_b2, ln2_g, ln2_b) = params
    L = x.shape[1]
    for l in range(DEPTH):
        mod = jnp.einsum('bd,de->be', jax.nn.silu(c), w_ada[l]) + b_ada[l]
        shift1, scale1, gate1, shift2, scale2, gate2 = [m[:, None, :] for m in jnp.split(mod, N_MOD, axis=-1)]
        filt = _implicit_filters(L, hy_ffn_w1[l], hy_ffn_b1[l], hy_sin_freq[l],
                                 hy_ffn_w2[l], hy_ffn_b2[l], hy_ffn_w3[l])
        h = _ln(x) * (1.0 + scale1) + shift1
        mix = _mixer(h, w_in[l], conv_dw_w[l], conv_dw_b[l], conv_ln_g[l], conv_ln_b[l], conv_pw_w[l],
                     hy_short_w[l], hy_short_b[l], filt, hy_skip[l], hy_out_w[l], w_out[l])
        x = _ln(DEEPNORM_ALPHA * x + gate1 * mix, ln1_g[l], ln1_b[l])
        h = _ln(x) * (1.0 + scale2) + shift2
        f = jnp.square(jax.nn.relu(jnp.einsum('bld,df->blf', h, mlp_w1[l]) + mlp_b1[l]))
        f = jnp.einsum('blf,fd->bld', f, mlp_w2[l]) + mlp_b2[l]
        x = _ln(DEEPNORM_ALPHA * x + gate2 * f, ln2_g[l], ln2_b[l])
    return x


def setup_inputs(seed: int = 0) -> dict:
    key = jax.random.key(seed)
    ks = iter(jax.random.split(key, 40))
    f32 = jnp.float32

    def nrm(shape, scale):
        return jax.random.normal(next(ks), shape, f32) * scale

    def gain(shape):
        return 1.0 + nrm(shape, 0.01)

    D = DEPTH
    return {
        "x_prompt": nrm((BATCH, SEQ, D_MODEL), 1.0),
        "x_sample": nrm((DEC_BATCH, DEC_SEQ, D_MODEL), 1.0),
        "c_prompt": nrm((BATCH, D_MODEL), 1.0),
        "c_sample": nrm((DEC_BATCH, D_MODEL), 1.0),
        "w_ada": nrm((D, D_MODEL, N_MOD * D_MODEL), 0.5 * D_MODEL ** -0.5),
        "b_ada": nrm((D, N_MOD * D_MODEL), 0.01),
        "w_in": nrm((D, D_MODEL, IN_WIDTH), D_MODEL ** -0.5),
        "conv_dw_w": nrm((D, CONV_KERNEL, D_CONV), CONV_KERNEL ** -0.5),
        "conv_dw_b": nrm((D, D_CONV), 0.01),
        "conv_ln_g": gain((D, D_CONV)),
        "conv_ln_b": nrm((D, D_CONV), 0.01),
        "conv_pw_w": nrm((D, D_CONV, D_MODEL), D_CONV ** -0.5),
        "hy_short_w": nrm((D, SHORT_KERNEL, (HYENA_ORDER + 1) * D_HYENA), SHORT_KERNEL ** -0.5),
        "hy_short_b": nrm((D, (HYENA_ORDER + 1) * D_HYENA), 0.01),
        "hy_ffn_w1": nrm((D, FILTER_EMB, FILTER_HIDDEN), FILTER_EMB ** -0.5),
        "hy_ffn_b1": nrm((D, FILTER_HIDDEN), 0.01),
        "hy_sin_freq": gain((D, FILTER_HIDDEN)),
        "hy_ffn_w2": nrm((D, FILTER_HIDDEN, FILTER_HIDDEN), FILTER_HIDDEN ** -0.5),
        "hy_ffn_b2": nrm((D, FILTER_HIDDEN), 0.01),
        "hy_ffn_w3": nrm((D, FILTER_HIDDEN, HYENA_ORDER * N_DIRS * D_HYENA), 0.05 * FILTER_HIDDEN ** -0.5),
        "hy_skip": nrm((D, HYENA_ORDER, D_HYENA), 0.5),
        "hy_out_w": nrm((D, D_HYENA, D_MODEL), D_HYENA ** -0.5),
        "w_out": nrm((D, D_MODEL, D_MODEL), DEEPNORM_BETA * D_MODEL ** -0.5),
        "ln1_g": gain((D, D_MODEL)),
        "ln1_b": nrm((D, D_MODEL), 0.01),
        "mlp_w1": nrm((D, D_MODEL, D_FF), D_MODEL ** -0.5),
        "mlp_b1": nrm((D, D_FF), 0.01),
        "mlp_w2": nrm((D, D_FF, D_MODEL), DEEPNORM_BETA * D_FF ** -0.5),
        "mlp_b2": nrm((D, D_MODEL), 0.01),
        "ln2_g": gain((D, D_MODEL)),
        "ln2_b": nrm((D, D_MODEL), 0.01),
    }


def reference(x_prompt, x_sample, c_prompt, c_sample, w_ada, b_ada, w_in,
              conv_dw_w, conv_dw_b, conv_ln_g, conv_ln_b, conv_pw_w,
              hy_short_w, hy_short_b, hy_ffn_w1, hy_ffn_b1, hy_sin_freq, hy_ffn_w2, hy_ffn_b2,
              hy_ffn_w3, hy_skip, hy_out_w, w_out, ln1_g, ln1_b,
              mlp_w1, mlp_b1, mlp_w2, mlp_b2, ln2_g, ln2_b):
    params = (w_ada, b_ada, w_in, conv_dw_w, conv_dw_b, conv_ln_g, conv_ln_b, conv_pw_w,
              hy_short_w, hy_short_b, hy_ffn_w1, hy_ffn_b1, hy_sin_freq, hy_ffn_w2, hy_ffn_b2,
              hy_ffn_w3, hy_skip, hy_out_w, w_out, ln1_g, ln1_b,
              mlp_w1, mlp_b1, mlp_w2, mlp_b2, ln2_g, ln2_b)
    y_prompt = _encoder(x_prompt, c_prompt, params)
    y_sample = _encoder(x_sample, c_sample, params)
    return (y_prompt, y_sample)
```

```python
import math
from contextlib import ExitStack

import numpy as np
import ml_dtypes

import concourse.bass as bass
import concourse.mybir as mybir
from concourse.bass_utils import run_bass_kernel_spmd

F32 = mybir.dt.float32
BF16 = mybir.dt.bfloat16
ACTF = mybir.ActivationFunctionType
ALU = mybir.AluOpType

D = 1024
DFF = 4096
CH = 512
NCORES = 8
LN_EPS = 1e-5
ALPHA = 2.0 ** 0.25
MAGIC = 12582912.0
TWO_PI = 2.0 * math.pi

ENGS = ("pe", "act", "dve", "pool", "sp")
import os as _os0
NODISJOINT = bool(_os0.environ.get("NODISJOINT"))
CHECK = bool(_os0.environ.get("SCHED_CHECK"))


class Buf:
    __slots__ = ("name", "w", "r", "pr", "lsem")

    def __init__(self, name):
        self.name = name
        self.w = []
        self.r = []
        self.pr = []
        self.lsem = None


class Sched:
    def __init__(self, nc):
        self.nc = nc
        self.ops = {e: [] for e in ENGS}
        self.cnt = {e: 0 for e in ENGS}
        self.seen = {e: {} for e in ENGS}
        self.dcnt = {}
        self.free_dma_sems = []
        self.sems = {}
        self.pending_nosig = {e: False for e in ENGS}
        self.nblk = 0
        self.log = []
        self.chk_sem = {}
        self.chk_semvc = {}
        self.chk_vc = {e: {} for e in ENGS}
        self.chk_acc = {}

    def register(self, eng_sems, dma_sems):
        for e, s in zip(ENGS, eng_sems):
            self.sems[e] = s
        for i, s in enumerate(dma_sems):
            k = "d%d" % i
            self.sems[k] = s
            self.dcnt[k] = 0
            self.free_dma_sems.append(k)

    def release(self, bufs):
        for b in bufs:
            if b.lsem is not None:
                self.free_dma_sems.append(b.lsem)
                b.lsem = None

    def _dma_sem(self, buf):
        if buf.lsem is None:
            if not self.free_dma_sems:
                raise RuntimeError("out of DMA semaphores")
            buf.lsem = self.free_dma_sems.pop(0)
        return buf.lsem

    def _waits(self, eng, reads, writes, disjoint=False):
        need = {}
        for b in reads:
            for (s, v) in b.w:
                need[s] = max(need.get(s, 0), v)
        for b in writes:
            if not disjoint:
                for (s, v) in b.w:
                    need[s] = max(need.get(s, 0), v)
            for (s, v) in b.r:
                need[s] = max(need.get(s, 0), v)
            for (s, v) in b.pr:
                need[s] = max(need.get(s, 0), v)
        out = []
        for s, v in need.items():
            if s == eng and eng == "pe":
                continue
            if self.seen[eng].get(s, 0) >= v:
                continue
            self.seen[eng][s] = v
            out.append((s, v))
        return out

    def _note(self, tok, reads, writes, disjoint):
        for b in reads:
            b.r.append(tok)
            if len(b.r) > 48:
                b.r = _compact(b.r)
        for b in writes:
            if disjoint:
                if b.r:
                    b.pr = b.r
                    b.r = []
                    b.w = [tok]
                else:
                    b.w.append(tok)
                    if len(b.w) > 48:
                        b.w = _compact(b.w)
            else:
                b.pr = _compact(b.r + b.w)
                b.w = [tok]
                b.r = []

    def op(self, eng, fn, reads=(), writes=(), signal=True, disjoint=False):
        if NODISJOINT:
            disjoint = False
        waits = self._waits(eng, reads, writes, disjoint)
        if signal:
            self.cnt[eng] += 1
            tok = (eng, self.cnt[eng])
            self.pending_nosig[eng] = False
            inc = (eng, 1)
        else:
            tok = (eng, self.cnt[eng] + 1)
            self.pending_nosig[eng] = True
            inc = None
        self.ops[eng].append((waits, fn, inc))
        if CHECK:
            self.log.append(("op", eng, list(waits), inc, [b for b in reads], [b for b in writes], disjoint, len(self.ops[eng]) - 1))
        self._note(tok, reads, writes, disjoint)
        return tok

    def dma(self, queue, out_ap, in_ap, reads=(), writes=(), sembuf=None, disjoint=False, **kw):
        waits = self._waits(queue, reads, writes, disjoint)
        if sembuf is None:
            sembuf = writes[0] if writes else reads[0]
        k = self._dma_sem(sembuf)
        self.dcnt[k] += 16
        tok = (k, self.dcnt[k])
        fn = lambda e, o=out_ap, i=in_ap, kw=kw: e.dma_start(out=o, in_=i, **kw)
        self.ops[queue].append((waits, fn, (k, 16)))
        if CHECK:
            self.log.append(("dma", queue, list(waits), (k, 16), [b for b in reads], [b for b in writes], disjoint, len(self.ops[queue]) - 1))
        self._note(tok, reads, writes, disjoint)
        return tok

    def check(self):
        queues = {e: [] for e in ENGS}
        for ent in self.log:
            queues[ent[1]].append(ent)
        full = {e: [] for e in ENGS}
        for e in ENGS:
            li = {ent[7]: ent for ent in queues[e]}
            for i, (waits, fn, inc) in enumerate(self.ops[e]):
                if i in li:
                    full[e].append(li[i])
                else:
                    full[e].append(("bar", e, list(waits), inc, [], [], False, i))
        ptr = {e: 0 for e in ENGS}
        sem = self.chk_sem
        semvc = self.chk_semvc
        vc = self.chk_vc
        pos = getattr(self, "chk_pos", {e: 0 for e in ENGS})
        nerr = 0

        def join(a, b):
            for k, v in b.items():
                if a.get(k, 0) < v:
                    a[k] = v

        progress = True
        while progress:
            progress = False
            for e in ENGS:
                while ptr[e] < len(full[e]):
                    kind, _, waits, inc, reads, writes, disjoint, _i = full[e][ptr[e]]
                    ok = all(sem.get(s, 0) >= v for s, v in waits)
                    if not ok:
                        break
                    cur = dict(vc[e])
                    for s_, v in waits:
                        key = (s_, v)
                        if key in semvc:
                            join(cur, semvc[key])
                        else:
                            cands = [kk for kk in semvc if kk[0] == s_ and kk[1] <= v]
                            for kk in cands:
                                join(cur, semvc[kk])
                    pos[e] += 1
                    cur[e] = pos[e]
                    vc[e] = cur
                    if kind != "bar":
                        if kind == "dma":
                            thread = inc[0]
                            done = dict(cur)
                            sem[thread] = sem.get(thread, 0) + 16
                            done[thread] = sem[thread]
                            semvc[(thread, sem[thread])] = done
                            acc_vc_start = cur
                            acc_key = (thread, sem[thread])
                        else:
                            if inc is not None:
                                sem[inc[0]] = sem.get(inc[0], 0) + 1
                                semvc[(inc[0], sem[inc[0]])] = cur
                            acc_key = (e, pos[e])
                            acc_vc_start = cur
                        for b in reads:
                            a = self.chk_acc.setdefault(id(b), dict(w=[], r=[], name=b.name))
                            for (k, p) in a["w"]:
                                if acc_vc_start.get(k, 0) < p and not (k == e == "pe"):
                                    nerr += 1
                                    if nerr < 20:
                                        print("RACE RAW buf=%s reader=%s@%d writer=%s@%d" % (b.name, e, pos[e], k, p))
                            a["r"].append(acc_key)
                            if len(a["r"]) > 200:
                                a["r"] = a["r"][-100:]
                        for b in writes:
                            a = self.chk_acc.setdefault(id(b), dict(w=[], r=[], name=b.name))
                            for (k, p) in a["r"]:
                                if acc_vc_start.get(k, 0) < p and not (k == e == "pe"):
                                    nerr += 1
                                    if nerr < 20:
                                        print("RACE WAR buf=%s writer=%s@%d reader=%s@%d" % (b.name, e, pos[e], k, p))
                            if not disjoint:
                                for (k, p) in a["w"]:
                                    if acc_vc_start.get(k, 0) < p and not (k == e == "pe"):
                                        nerr += 1
                                        if nerr < 20:
                                            print("RACE WAW buf=%s writer=%s@%d prev=%s@%d" % (b.name, e, pos[e], k, p))
                                a["w"] = [acc_key]
                                a["r"] = []
                            else:
                                if a["r"]:
                                    a["w"] = [acc_key]
                                    a["r"] = []
                                else:
                                    a["w"].append(acc_key)
                    ptr[e] += 1
                    progress = True
        for e in ENGS:
            if ptr[e] < len(full[e]):
                print("DEADLOCK: engine %s stuck at op %d/%d waits=%s semvals=%s" % (
                    e, ptr[e], len(full[e]), full[e][ptr[e]][2], {s_: sem.get(s_, 0) for s_, _ in full[e][ptr[e]][2]}))
                nerr += 1
        self.chk_pos = pos
        self.log = []
        print("[check] block ok" if nerr == 0 else "[check] %d problems" % nerr)

    def barrier(self):
        allw = [(e, self.cnt[e]) for e in ENGS if self.cnt[e] > 0]
        allw += [(k, v) for k, v in self.dcnt.items() if v > 0]
        for e in ENGS:
            assert not self.pending_nosig[e], "engine %s has trailing non-signalling op" % e
            waits = []
            for s, v in allw:
                if s == e:
                    continue
                if self.seen[e].get(s, 0) >= v:
                    continue
                self.seen[e][s] = v
                waits.append((s, v))
            if waits:
                self.ops[e].append((waits, None, None))

    def emit(self):
        nc = self.nc
        S = self
        if CHECK:
            self.check()

        def replay(e, name):
            for waits, fn, inc in S.ops[name]:
                for s, v in waits:
                    e.wait_ge(S.sems[s], v)
                if fn is not None:
                    ins = fn(e)
                    if inc is not None:
                        ins.then_inc(S.sems[inc[0]], inc[1])

        with nc.Block() as block:
            @block.tensor
            def _(e):
                replay(e, "pe")

            @block.scalar
            def _(e):
                replay(e, "act")

            @block.vector
            def _(e):
                replay(e, "dve")

            @block.gpsimd
            def _(e):
                replay(e, "pool")

            @block.sync
            def _(e):
                replay(e, "sp")
        self.ops = {e: [] for e in ENGS}


def _compact(toks):
    best = {}
    for s, v in toks:
        best[s] = max(best.get(s, 0), v)
    return list(best.items())


class Ring:
    def __init__(self, es, nc, name, shape, dt, n, psum=False):
        mk = nc.psum_tensor if psum else nc.sbuf_tensor
        self.t = [es.enter_context(mk("%s_%d" % (name, i), shape, dt)) for i in range(n)]
        self.b = [Buf("%s_%d" % (name, i)) for i in range(n)]
        self.n = n
        self.i = 0

    def next(self):
        k = self.i % self.n
        self.i += 1
        return self.t[k], self.b[k]


def build(cfg, debug=False):
    NP, LP, NS, LS = cfg["NP"], cfg["LP"], cfg["NS"], cfg["LS"]
    NSEQ = NP + NS
    NTOK = NP * LP + NS * LS
    Ls = sorted(set(([LP] if NP else []) + ([LS] if NS else [])))
    seqs = []
    t0 = 0
    for i in range(NP):
        seqs.append(dict(g="p", i=i, L=LP, tok0=t0, row0=i * LP))
        t0 += LP
    for i in range(NS):
        seqs.append(dict(g="s", i=i, L=LS, tok0=t0, row0=i * LS))
        t0 += LS

    nc = bass.Bass("TRN2", target_bir_lowering=False)

    def din(name, shape, dt=F32):
        return nc.dram_tensor(name, list(shape), dt, kind="ExternalInput").ap()

    def dout(name, shape, dt=F32):
        return nc.dram_tensor(name, list(shape), dt, kind="ExternalOutput").ap()

    def dscr(name, shape, dt):
        return nc.dram_tensor(name, list(shape), dt, kind=("ExternalOutput" if debug else "Internal")).ap()

    X = {}
    Y = {}
    if NP:
        X["p"] = din("x_p", [NP * LP, D])
        Y["p"] = dout("y_p", [NP * LP, D])
    if NS:
        X["s"] = din("x_s", [NS * LS, D])
        Y["s"] = dout("y_s", [NS * LS, D])
    cT = din("cT", [128, 8, NSEQ])
    w_ada = din("w_ada", [D, 6 * D])
    b_ada = din("b_ada", [6 * D])
    b_adaT = din("b_adaT", [128, 48])
    w_in = din("w_in", [D, 4608])
    dw_wT = din("dw_wT", [128, 4, 31])
    dw_pk = din("dw_pk", [128, 16, 8])
    dw_bT = din("dw_bT", [128, 4])
    cln_gT = din("cln_gT", [128, 4])
    cln_bT = din("cln_bT", [128, 4])
    pw_w = din("pw_w", [CH, D])
    hs_w = din("hs_w", [3, 1536])
    hs_b = din("hs_b", [1536])
    hs_bT = din("hs_bT", [128, 12])
    hs_wT = din("hs_wT", [128, 12, 3])
    f_w1 = din("f_w1", [33, 64])
    f_b1 = din("f_b1", [64, 1])
    f_fr = din("f_fr", [64, 1])
    f_w2 = din("f_w2", [64, 64])
    f_b2 = din("f_b2", [64, 1])
    f_w3 = din("f_w3", [64, 2048])
    hy_skip = din("hy_skip", [2, 512])
    hyo_w = din("hyo_w", [CH, D])
    w_out = din("w_out", [D, D])
    ln1_g = din("ln1_g", [D])
    ln1_b = din("ln1_b", [D])
    mlp_w1 = din("mlp_w1", [D, DFF])
    mlp_b1T = din("mlp_b1T", [128, 32])
    mlp_w2 = din("mlp_w2", [DFF, D])
    mlp_b2 = din("mlp_b2", [D])
    ln2_g = din("ln2_g", [D])
    ln2_b = din("ln2_b", [D])
    altc = din("altc", [128, 2])
    TAB = {}
    for L in Ls:
        KB = L // 128
        TAB[L] = dict(
            zT=din("zT_%d" % L, [33, L]),
            decF=din("decF_%d" % L, [L, CH]),
            decB=din("decB_%d" % L, [L, CH]),
            TC=din("TC_%d" % L, [KB, 128, KB, 128], BF16),
            TSF=din("TSF_%d" % L, [KB, 128, KB, 128], BF16),
            TSI=din("TSI_%d" % L, [KB, 128, KB, 128], BF16),
            HS=dscr("HS_%d" % L, [2, 2 * KB, 128, CH], F32),
        )
    V_s = dscr("V_s", [NTOK, CH], BF16)
    X1_s = dscr("X1_s", [NTOK, CH], BF16)
    X2_s = dscr("X2_s", [NTOK, CH], BF16)
    A_s = dscr("A_s", [CH, NTOK], BF16)
    ZB_s = dscr("ZB_s", [CH, NTOK], BF16)
    MODROW = dscr("MODROW", [NSEQ, 2 * D], F32)
    G_s = dscr("G_s", [2 * D, NTOK], BF16)
    X1DBG = dscr("X1DBG", [NTOK, D], F32) if debug else None

    def xrows(sq, t, n):
        return X[sq["g"]][sq["row0"] + t: sq["row0"] + t + n, :]

    def yrows(sq, t, n):
        return Y[sq["g"]][sq["row0"] + t: sq["row0"] + t + n, :]

    with ExitStack() as top:
        esems = [top.enter_context(nc.semaphore("eng%d" % i)) for i in range(5)]
        dsems = []
        for i in range(98):
            try:
                dsems.append(top.enter_context(nc.semaphore("dma%d" % i)))
            except KeyError:
                break
        S = Sched(nc)
        S.register(esems, dsems)

        PS = Ring(top, nc, "ps", [128, 512], F32, 8, psum=True)

        uniq = [0]

        def Tt(es, name, shape, dt=F32):
            uniq[0] += 1
            name = "%s_u%d" % (name, uniq[0])
            return es.enter_context(nc.sbuf_tensor(name, list(shape), dt)), Buf(name)

        ident_f, b_identf = Tt(top, "ident_f", [128, 128], F32)
        ident, b_ident = Tt(top, "ident", [128, 128], BF16)
        ones_f, b_ones = Tt(top, "ones_f", [128, 128], F32)
        modT, b_modT = Tt(top, "modT", [128, 48, NSEQ], F32)
        S.op("pool", lambda e: e.memset(ident_f[:], 0.0), writes=[b_identf])
        S.op("pool", lambda e: e.affine_select(out=ident_f[:], in_=ident_f[:], compare_op=ALU.not_equal, fill=1.0,
                                               base=0, pattern=[[-1, 128]], channel_multiplier=1),
             reads=[b_identf], writes=[b_identf])
        S.op("dve", lambda e: e.tensor_copy(ident[:], ident_f[:]), reads=[b_identf], writes=[b_ident])
        S.op("pool", lambda e: e.memset(ones_f[:], 1.0), writes=[b_ones])

        stage_bufs = []

        def end_stage():
            S.barrier()
            S.emit()
            S.release(stage_bufs)
            del stage_bufs[:]

        def T(es, name, shape, dt=F32):
            t, b = Tt(es, name, shape, dt)
            stage_bufs.append(b)
            return t, b

        def R(es, name, shape, dt, n):
            uniq[0] += 1
            r = Ring(es, nc, "%s_u%d" % (name, uniq[0]), shape, dt, n)
            stage_bufs.extend(r.b)
            return r

        with ExitStack() as es:
            sc, b_sc = T(es, "sc", [128, 8, NSEQ])
            badaT, b_badaT = T(es, "badaT", [128, 48])
            wr = R(es, "wada", [128, 8, 512], F32, 2)
            brow = R(es, "brow", [NSEQ, 512], F32, 2)
            grow = R(es, "grow", [NSEQ, 512], F32, 2)
            S.dma("sp", sc[:], cT, writes=[b_sc])
            S.dma("sp", badaT[:], b_adaT, writes=[b_badaT])
            S.op("act", lambda e: e.activation(out=sc[:], in_=sc[:], func=ACTF.Silu), reads=[b_sc], writes=[b_sc])
            for ck in range(12):
                wt, wb = wr.next()
                S.dma("sp", wt[:], w_ada[:, ck * 512:(ck + 1) * 512].rearrange("(kb p) n -> p kb n", p=128), writes=[wb])
                if ck in (4, 5, 10, 11):
                    pt, pb = PS.next()
                    for kb in range(8):
                        S.op("pe", lambda e, pt=pt, wt=wt, kb=kb: e.matmul(pt[0:NSEQ, :], sc[:, kb, :], wt[:, kb, :], start=(kb == 0), stop=(kb == 7)),
                             reads=[b_sc, wb], writes=[pb], signal=(kb == 7))
                    bt, bb = brow.next()
                    S.dma("sp", bt[:], b_ada[ck * 512:(ck + 1) * 512].partition_broadcast(NSEQ), writes=[bb])
                    gt, gb = grow.next()
                    S.op("dve", lambda e, gt=gt, pt=pt, bt=bt: e.tensor_tensor(gt[:], pt[0:NSEQ, :], bt[:], ALU.add),
                         reads=[pb, bb], writes=[gb])
                    col = (0 if ck < 6 else D) + (ck % 2) * 512
                    S.dma("pool", MODROW[:, col:col + 512], gt[:], reads=[gb])
                else:
                    for q in range(4):
                        blk = ck * 4 + q
                        pt, pb = PS.next()
                        for kb in range(8):
                            S.op("pe", lambda e, pt=pt, wt=wt, kb=kb, q=q: e.matmul(pt[:, 0:NSEQ], wt[:, kb, q * 128:(q + 1) * 128], sc[:, kb, :], start=(kb == 0), stop=(kb == 7)),
                                 reads=[b_sc, wb], writes=[pb], signal=(kb == 7))
                        one = 1.0 if ck in (2, 3, 8, 9) else 0.0
                        S.op("dve", lambda e, pt=pt, blk=blk, one=one: e.tensor_scalar(modT[:, blk, :], pt[:, 0:NSEQ], badaT[:, blk:blk + 1], one, ALU.add, ALU.add),
                             reads=[pb, b_badaT], writes=[b_modT])
            end_stage()

        def ln_phaseA(lnb, tiles):
            NT = len(tiles)
            st, stb = lnb["st"].next()
            mv, mvb = lnb["mv"].next()
            for t, (xt, xb) in enumerate(tiles):
                for i in range(2):
                    S.op("dve", lambda e, i=i, t=t, st=st, xt=xt: e.bn_stats(st[:, t, i, :], xt[:, i * 512:(i + 1) * 512]), reads=[xb], writes=[stb], disjoint=True)
                S.op("dve", lambda e, t=t, st=st, mv=mv: e.bn_aggr(mv[:, t, 0:2], st[:, t].rearrange("p a b -> p (a b)")), reads=[stb], writes=[mvb], disjoint=True)
            S.op("act", lambda e, mv=mv: e.activation(out=mv[:, 0:NT, 2:3], in_=mv[:, 0:NT, 1:2], func=ACTF.Sqrt, bias=LN_EPS), reads=[mvb], writes=[mvb])
            S.op("dve", lambda e, mv=mv: e.reciprocal(mv[:, 0:NT, 2:3], mv[:, 0:NT, 2:3]), reads=[mvb], writes=[mvb])
            S.op("dve", lambda e, mv=mv: e.scalar_tensor_tensor(mv[:, 0:NT, 3:4], mv[:, 0:NT, 0:1], -1.0, mv[:, 0:NT, 2:3], ALU.mult, ALU.mult), reads=[mvb], writes=[mvb])
            xns = []
            for t, (xt, xb) in enumerate(tiles):
                xn, xnb = lnb["xn"].next()
                S.op("act", lambda e, mv=mv, xn=xn, xt=xt, t=t: e.activation(out=xn[:], in_=xt, func=ACTF.Identity, bias=mv[:, t, 3:4], scale=mv[:, t, 2:3]),
                     reads=[xb, mvb], writes=[xnb])
                xns.append((xn, xnb))
            return xns

        def ln_phaseB(xns, dst_fn, dst_bufs, sidx, shift_blk, scale_blk):
            for t, (xn, xnb) in enumerate(xns):
                pt, pb = PS.next()
                pv = pt.bitcast(BF16)
                for kb in range(8):
                    S.op("pe", lambda e, kb=kb, pv=pv, xn=xn: e.transpose(pv[:, kb * 128:(kb + 1) * 128], xn[:, kb * 128:(kb + 1) * 128], ident[:]),
                         reads=[xnb, b_ident], writes=[pb], signal=(kb == 7))
                eng = "dve" if t % 2 == 0 else "act"
                for kb in range(8):
                    dv = dst_fn(t, kb)
                    sc_ap = modT[:, scale_blk + kb, sidx:sidx + 1]
                    sh_ap = modT[:, shift_blk + kb, sidx:sidx + 1]
                    if eng == "dve":
                        S.op("dve", lambda e, dv=dv, pv=pv, kb=kb, sc_ap=sc_ap, sh_ap=sh_ap: e.tensor_scalar(dv, pv[:, kb * 128:(kb + 1) * 128], sc_ap, sh_ap, ALU.mult, ALU.add),
                             reads=[pb, b_modT], writes=[dst_bufs[t]], disjoint=True)
                    else:
                        S.op("act", lambda e, dv=dv, pv=pv, kb=kb, sc_ap=sc_ap, sh_ap=sh_ap: e.activation(out=dv, in_=pv[:, kb * 128:(kb + 1) * 128], func=ACTF.Identity, bias=sh_ap, scale=sc_ap),
                             reads=[pb, b_modT], writes=[dst_bufs[t]], disjoint=True)

        def ln_rings(es, tag, nt, nxn):
            return dict(st=R(es, "st" + tag, [128, nt, 2, 6], F32, 2), mv=R(es, "mv" + tag, [128, nt, 4], F32, 2),
                        xn=R(es, "xn" + tag, [128, D], BF16, nxn))

        def load_w_bf16(dst, dst_b, src, K, N, stg, step, col0=0, scale_bc=None, scale_b=None):
            KBn = K // 128
            c = 0
            i = 0
            while c < N:
                n = min(step, N - c)
                stt, stb_ = stg.next()
                S.dma("sp", stt[:, 0:KBn, 0:n], src[:, c:c + n].rearrange("(kb p) n -> p kb n", p=128), writes=[stb_])
                if scale_bc is None:
                    eng = "act" if i % 2 == 0 else "dve"
                    if eng == "act":
                        S.op("act", lambda e, stt=stt, c=c, n=n: e.activation(out=dst[:, :, col0 + c:col0 + c + n], in_=stt[:, 0:KBn, 0:n], func=ACTF.Copy),
                             reads=[stb_], writes=[dst_b])
                    else:
                        S.op("dve", lambda e, stt=stt, c=c, n=n: e.tensor_copy(dst[:, :, col0 + c:col0 + c + n], stt[:, 0:KBn, 0:n]),
                             reads=[stb_], writes=[dst_b])
                else:
                    for kb in range(KBn):
                        eng = "dve" if kb % 2 == 0 else "pool"
                        S.op(eng, lambda e, stt=stt, c=c, n=n, kb=kb: e.tensor_tensor(dst[:, kb, col0 + c:col0 + c + n], stt[:, kb, 0:n], scale_bc[:, c:c + n], ALU.mult),
                             reads=[stb_, scale_b], writes=[dst_b])
                c += n
                i += 1

        def sub_stage(bufs_before):
            S.barrier()
            S.emit()
            S.release(stage_bufs[bufs_before:])
            del stage_bufs[bufs_before:]

        for L in Ls:
            KB = L // 128
            tab = TAB[L]
            with ExitStack() as es:
                HSUM, b_HSUM = T(es, "HSUM", [128, KB, 2, 512], BF16)
                HDIF, b_HDIF = T(es, "HDIF", [128, KB, 2, 512], BF16)
                w3s, b_w3s = T(es, "w3s", [64, 2048])
                hd2, b_hd2 = T(es, "hd2", [64, L])
                skr, b_skr = T(es, "skr", [1, 2, 512])
                alt_f, b_altf = T(es, "alt_f", [128, 2])
                alt, b_alt = T(es, "alt", [128, 2], BF16)
                S.dma("sp", w3s[:], f_w3, writes=[b_w3s])
                S.dma("sp", skr[:], hy_skip.rearrange("(a o) c -> a o c", a=1), writes=[b_skr])
                S.dma("sp", alt_f[:], altc, writes=[b_altf])
                S.op("dve", lambda e: e.tensor_copy(alt[:], alt_f[:]), reads=[b_altf], writes=[b_alt])
                nb0 = len(stage_bufs)
                with ExitStack() as es2:
                    zT, b_zT = T(es2, "zT", [33, L])
                    w1s, b_w1s = T(es2, "w1s", [33, 64])
                    w2s, b_w2s = T(es2, "w2s", [64, 64])
                    fpar, b_fpar = T(es2, "fpar", [64, 8])
                    hd1, b_hd1 = T(es2, "hd1", [64, L])
                    argr = R(es2, "argr", [64, 512], F32, 2)
                    kr = R(es2, "kr", [64, 512], F32, 2)
                    S.dma("sp", zT[:], tab["zT"], writes=[b_zT])
                    S.dma("sp", w1s[:], f_w1, writes=[b_w1s])
                    S.dma("sp", w2s[:], f_w2, writes=[b_w2s])
                    S.dma("sp", fpar[:, 0:1], f_b1, writes=[b_fpar])
                    S.dma("sp", fpar[:, 1:2], f_fr, writes=[b_fpar])
                    S.dma("sp", fpar[:, 2:3], f_b2, writes=[b_fpar])
                    S.op("dve", lambda e: e.tensor_tensor(fpar[:, 3:4], fpar[:, 0:1], fpar[:, 1:2], ALU.mult), reads=[b_fpar], writes=[b_fpar])
                    S.op("dve", lambda e: e.tensor_tensor(fpar[:, 4:5], fpar[:, 2:3], fpar[:, 1:2], ALU.mult), reads=[b_fpar], writes=[b_fpar])
                    for layer in range(2):
                        src, srcb = (zT, b_zT) if layer == 0 else (hd1, b_hd1)
                        wl, wlb = (w1s, b_w1s) if layer == 0 else (w2s, b_w2s)
                        dst, dstb = (hd1, b_hd1) if layer == 0 else (hd2, b_hd2)
                        kin = 33 if layer == 0 else 64
                        fb_col = 3 if layer == 0 else 4
                        for ck in range(L // 512):
                            pt, pb = PS.next()
                            S.op("pe", lambda e, pt=pt, wl=wl, src=src, ck=ck, kin=kin: e.matmul(pt[0:64, :], wl[0:kin, :], src[0:kin, ck * 512:(ck + 1) * 512], start=True, stop=True),
                                 reads=[wlb, srcb], writes=[pb])
                            at, ab = argr.next()
                            kt, kb_ = kr.next()
                            S.op("dve", lambda e, at=at, pt=pt, fb_col=fb_col: e.tensor_scalar(at[:], pt[0:64, :], fpar[:, 1:2], fpar[:, fb_col:fb_col + 1], ALU.mult, ALU.add),
                                 reads=[pb, b_fpar], writes=[ab])
                            S.op("dve", lambda e, at=at, kt=kt: e.tensor_scalar(kt[:], at[:], 1.0 / TWO_PI, MAGIC, ALU.mult, ALU.add), reads=[ab], writes=[kb_])
                            S.op("dve", lambda e, kt=kt: e.tensor_scalar(kt[:], kt[:], -MAGIC, -TWO_PI, ALU.add, ALU.mult), reads=[kb_], writes=[kb_])
                            S.op("dve", lambda e, at=at, kt=kt: e.tensor_tensor(at[:], at[:], kt[:], ALU.add), reads=[ab, kb_], writes=[ab])
                            S.op("act", lambda e, at=at, dst=dst, ck=ck: e.activation(out=dst[:, ck * 512:(ck + 1) * 512], in_=at[:], func=ACTF.Sin),
                                 reads=[ab], writes=[dstb])
                    sub_stage(nb0)
                with ExitStack() as es2:
                    decr = R(es2, "decr", [128, 2, 512], F32, 2)
                    hfr = R(es2, "hfr", [128, 4, 512], F32, 2)
                    for tb in range(KB):
                        dt_, db_ = decr.next()
                        S.dma("sp", dt_[:, 0, :], tab["decF"][tb * 128:(tb + 1) * 128, :], writes=[db_])
                        S.dma("sp", dt_[:, 1, :], tab["decB"][tb * 128:(tb + 1) * 128, :], writes=[db_], disjoint=True)
                        ht, hb = hfr.next()
                        for q in range(4):
                            pt, pb = PS.next()
                            S.op("pe", lambda e, pt=pt, tb=tb, q=q: e.matmul(pt[:], hd2[:, tb * 128:(tb + 1) * 128], w3s[:, q * 512:(q + 1) * 512], start=True, stop=True),
                                 reads=[b_hd2, b_w3s], writes=[pb])
                            S.op("dve", lambda e, ht=ht, pt=pt, dt_=dt_, q=q: e.tensor_tensor(ht[:, q, :], pt[:], dt_[:, q % 2, :], ALU.mult),
                                 reads=[pb, db_], writes=[hb])
                        if tb == 0:
                            for o in range(2):
                                S.op("dve", lambda e, ht=ht, o=o: e.memset(ht[0:1, 2 * o + 1, :], 0.0), writes=[hb])
                                S.op("dve", lambda e, ht=ht, o=o: e.tensor_tensor(ht[0:1, 2 * o, :], ht[0:1, 2 * o, :], skr[0:1, o, :], ALU.add),
                                     reads=[hb, b_skr], writes=[hb])
                        for o in range(2):
                            S.op("dve", lambda e, ht=ht, o=o, tb=tb: e.tensor_tensor(HSUM[:, tb, o, :], ht[:, 2 * o, :], ht[:, 2 * o + 1, :], ALU.add),
                                 reads=[hb], writes=[b_HSUM])
                            S.op("pool", lambda e, ht=ht, o=o, tb=tb: e.tensor_tensor(HDIF[:, tb, o, :], ht[:, 2 * o, :], ht[:, 2 * o + 1, :], ALU.subtract),
                                 reads=[hb], writes=[b_HDIF])
                    sub_stage(nb0)
                slabC = R(es, "slabC", [128, KB, 128], BF16, 2)
                slabS = R(es, "slabS", [128, KB, 128], BF16, 2)
                outr = R(es, "outr", [128, 512], F32, 4)
                sc_all = 1.0 / L
                for j in range(KB):
                    ct, cb_ = slabC.next()
                    stt, sb_ = slabS.next()
                    S.dma("sp", ct[:], tab["TC"][j], writes=[cb_])
                    S.dma("sp", stt[:], tab["TSF"][j], writes=[sb_])
                    for o in range(2):
                        pP, pPb = PS.next()
                        for kb in range(KB):
                            S.op("pe", lambda e, pP=pP, ct=ct, kb=kb, o=o: e.matmul(pP[:], ct[:, kb, :], HSUM[:, kb, o, :], start=(kb == 0), stop=(kb == KB - 1)),
                                 reads=[cb_, b_HSUM], writes=[pPb], signal=(kb == KB - 1))
                        pQ, pQb = PS.next()
                        for kb in range(KB):
                            S.op("pe", lambda e, pQ=pQ, stt=stt, kb=kb, o=o: e.matmul(pQ[:], stt[:, kb, :], HDIF[:, kb, o, :], start=(kb == 0), stop=(kb == KB - 1)),
                                 reads=[sb_, b_HDIF], writes=[pQb], signal=(kb == KB - 1))
                        oP, oPb = outr.next()
                        oQ, oQb = outr.next()
                        S.op("act", lambda e, oP=oP, pP=pP: e.activation(out=oP[:], in_=pP[:], func=ACTF.Copy, scale=sc_all), reads=[pPb], writes=[oPb])
                        S.op("dve", lambda e, oQ=oQ, pQ=pQ: e.tensor_scalar_mul(oQ[:], pQ[:], sc_all), reads=[pQb], writes=[oQb])
                        if j == 0:
                            pN, pNb = PS.next()
                            for kb in range(KB):
                                S.op("pe", lambda e, pN=pN, kb=kb, o=o: e.matmul(pN[0:1, :], alt[:, 0:1], HSUM[:, kb, o, :], start=(kb == 0), stop=(kb == KB - 1)),
                                     reads=[b_alt, b_HSUM], writes=[pNb], signal=(kb == KB - 1))
                            S.op("dve", lambda e, oP=oP: e.tensor_scalar_mul(oP[0:1, :], oP[0:1, :], 0.5), reads=[oPb], writes=[oPb])
                            S.op("dve", lambda e, oQ=oQ, pN=pN: e.tensor_scalar_mul(oQ[0:1, :], pN[0:1, :], 0.5 * sc_all), reads=[pNb, oQb], writes=[oQb])
                        S.dma("pool", tab["HS"][o, j], oP[:], reads=[oPb])
                        S.dma("pool", tab["HS"][o, KB + j], oQ[:], reads=[oQb])
                end_stage()

        all_chunks = [(si, sq, ck) for si, sq in enumerate(seqs) for ck in range(sq["L"] // 512)]
        with ExitStack() as es:
            Wa, b_Wa = T(es, "Wa", [128, 8, 1024], BF16)
            Wg, b_Wg = T(es, "Wg", [128, 8, 2048], BF16)
            nb0 = len(stage_bufs)
            with ExitStack() as es2:
                stg = R(es2, "stg", [128, 8, 256], F32, 3)
                load_w_bf16(Wa, b_Wa, w_in[:, 0:1024], D, 1024, stg, 256)
                load_w_bf16(Wg, b_Wg, w_in[:, 2560:4608], D, 2048, stg, 256)
                sub_stage(nb0)
            xr = R(es, "xr", [128, D], F32, 6)
            lnb = ln_rings(es, "a1", 4, 8)
            hTt = [T(es, "hTc%d" % i, [128, 8, 512], BF16)[0] for i in range(2)]
            hTb = [[Buf("hTc%d_%d" % (i, t)) for t in range(4)] for i in range(2)]
            sgr = R(es, "sgr", [128, 512], F32, 2)
            abuf = R(es, "abuf", [128, 4, 512], BF16, 2)
            gbuf = R(es, "gbuf", [128, 16, 512], BF16, 2)

            def a1_phaseA(n):
                si, sq, ck = all_chunks[n]
                tiles = []
                for tt in range(4):
                    xt, xb = xr.next()
                    S.dma("sp", xt[:], xrows(sq, ck * 512 + tt * 128, 128), writes=[xb])
                    tiles.append((xt[:], xb))
                return ln_phaseA(lnb, tiles)

            def a1_phaseB(n, xns):
                si, sq, ck = all_chunks[n]
                hTc = hTt[n % 2]
                ln_phaseB(xns, lambda t, kb, hTc=hTc: hTc[:, kb, t * 128:(t + 1) * 128], hTb[n % 2], si, 0, 8)

            a1_phaseB(0, a1_phaseA(0))
            for n, (si, sq, ck) in enumerate(all_chunks):
                c0 = ck * 512
                g0 = sq["tok0"] + c0
                hTc = hTt[n % 2]
                hb_ = hTb[n % 2]
                at, ab = abuf.next()
                for cb in range(4):
                    pv_, pvb = PS.next()
                    pg_, pgb = PS.next()
                    for kb in range(8):
                        S.op("pe", lambda e, pv_=pv_, kb=kb, cb=cb, hTc=hTc: e.matmul(pv_[:], Wa[:, kb, cb * 128:(cb + 1) * 128], hTc[:, kb, :], start=(kb == 0), stop=(kb == 7)),
                             reads=[b_Wa] + hb_, writes=[pvb], signal=(kb == 7))
                    for kb in range(8):
                        S.op("pe", lambda e, pg_=pg_, kb=kb, cb=cb, hTc=hTc: e.matmul(pg_[:], Wa[:, kb, 512 + cb * 128:512 + (cb + 1) * 128], hTc[:, kb, :], start=(kb == 0), stop=(kb == 7)),
                             reads=[b_Wa] + hb_, writes=[pgb], signal=(kb == 7))
                    sg, sgb = sgr.next()
                    S.op("act", lambda e, sg=sg, pg_=pg_: e.activation(out=sg[:], in_=pg_[:], func=ACTF.Sigmoid), reads=[pgb], writes=[sgb])
                    S.op("dve", lambda e, at=at, cb=cb, pv_=pv_, sg=sg: e.tensor_tensor(at[:, cb, :], pv_[:], sg[:], ALU.mult), reads=[pvb, sgb], writes=[ab], disjoint=True)
                S.dma("pool", A_s[:, g0:g0 + 512].rearrange("(cb p) t -> p cb t", p=128), at[:], reads=[ab])
                xns = a1_phaseA(n + 1) if n + 1 < len(all_chunks) else None
                gt, gb = gbuf.next()
                for gi in range(16):
                    if gi == 8 and xns is not None:
                        a1_phaseB(n + 1, xns)
                    pt, pb = PS.next()
                    for kb in range(8):
                        S.op("pe", lambda e, pt=pt, kb=kb, gi=gi, hTc=hTc: e.matmul(pt[:], Wg[:, kb, gi * 128:(gi + 1) * 128], hTc[:, kb, :], start=(kb == 0), stop=(kb == 7)),
                             reads=[b_Wg] + hb_, writes=[pb], signal=(kb == 7))
                    S.op("act", lambda e, pt=pt, gi=gi, gt=gt: e.activation(out=gt[:, gi, :], in_=pt[:], func=ACTF.Sigmoid), reads=[pb], writes=[gb], disjoint=True)
                S.dma("pool", G_s[:, g0:g0 + 512].rearrange("(gi p) t -> p gi t", p=128), gt[:], reads=[gb])
            end_stage()

        with ExitStack() as es:
            LMAX = max(sq["L"] for sq in seqs)
            hT, _ = T(es, "hT", [128, 8, LMAX + 2], BF16)
            hTtb = [Buf("hTt%d" % t) for t in range(LMAX // 128)]
            b_halo = Buf("halo")
            Why, b_Why = T(es, "Why", [128, 8, 1536], BF16)
            nb0 = len(stage_bufs)
            with ExitStack() as es2:
                stg = R(es2, "stg2", [128, 8, 256], F32, 3)
                load_w_bf16(Why, b_Why, w_in[:, 1024:2560], D, 1536, stg, 256)
                sub_stage(nb0)
            hswT, b_hswT = T(es, "hswT", [128, 12, 3])
            hsbT, b_hsbT = T(es, "hsbT", [128, 12])
            xr = R(es, "xr2", [128, D], F32, 4)
            lnb = ln_rings(es, "a2", 4, 8)
            halr = R(es, "halr", [128, 12, 2], F32, 2)
            ber = R(es, "ber", [128, 2, 12], F32, 2)
            u32 = R(es, "u32", [128, 512], F32, 3)
            ubf = R(es, "ubf", [128, 512], BF16, 4)
            obuf = R(es, "obufA", [128, 4, 512], BF16, 3)
            S.dma("sp", hswT[:], hs_wT, writes=[b_hswT])
            S.dma("sp", hsbT[:], hs_bT, writes=[b_hsbT])
            for si, sq in enumerate(seqs):
                L = sq["L"]
                NTL = L // 128

                def a2_phaseA(tl, sq=sq):
                    tiles = []
                    for t in tl:
                        xt, xb = xr.next()
                        S.dma("sp", xt[:], xrows(sq, t * 128, 128), writes=[xb])
                        tiles.append((xt[:], xb))
                    return ln_phaseA(lnb, tiles)

                def a2_phaseB(tl, xns, si=si):
                    ln_phaseB(xns, lambda t, kb, tl=tl: hT[:, kb, 1 + tl[t] * 128: 1 + (tl[t] + 1) * 128], [hTtb[t] for t in tl], si, 0, 8)

                S.op("pool", lambda e: e.memset(hT[:, :, 0:1], 0.0), writes=[b_halo])
                S.op("pool", lambda e, L=L: e.memset(hT[:, :, L + 1:L + 2], 0.0), writes=[b_halo], disjoint=True)
                groups = [[0]] + [[t for t in range(4 * g + 1, 4 * g + 5) if t < NTL] for g in range(L // 512)]
                groups = [g for g in groups if g]
                a2_phaseB(groups[0], a2_phaseA(groups[0]))
                a2_phaseB(groups[1], a2_phaseA(groups[1]))
                for ck in range(L // 512):
                    c0 = ck * 512
                    g0 = sq["tok0"] + c0
                    nxt = groups[ck + 2] if ck + 2 < len(groups) else None
                    xns = a2_phaseA(nxt) if nxt else None
                    tb0 = c0 // 128
                    rd_main = [hTtb[t] for t in range(tb0, tb0 + 4)] + [b_Why]
                    rd_halo = [hTtb[t] for t in (tb0 - 1, tb0 + 4) if 0 <= t < NTL] + [b_halo, b_Why]
                    ph, phb = PS.next()
                    for blk in range(12):
                        for kb in range(8):
                            S.op("pe", lambda e, ph=ph, blk=blk, kb=kb, c0=c0: e.matmul(ph[:, blk * 2:blk * 2 + 2], Why[:, kb, blk * 128:(blk + 1) * 128], hT[:, kb, c0:c0 + 514:513],
                                                                                    start=(kb == 0), stop=(kb == 7)),
                                 reads=rd_halo, writes=[phb], signal=(blk == 11 and kb == 7))
                    hl, hlb = halr.next()
                    S.op("dve", lambda e, hl=hl, ph=ph: e.tensor_copy(hl[:].rearrange("p a b -> p (a b)"), ph[:, 0:24]), reads=[phb], writes=[hlb])
                    be, beb = ber.next()
                    for side, tap in ((0, 0), (1, 2)):
                        S.op("dve", lambda e, be=be, hl=hl, side=side, tap=tap: e.tensor_tensor(be[:, side, :], hl[:, :, side], hswT[:, :, tap], ALU.mult),
                             reads=[hlb, b_hswT], writes=[beb], disjoint=(side > 0))
                        S.op("dve", lambda e, be=be, side=side: e.tensor_tensor(be[:, side, :], be[:, side, :], hsbT[:], ALU.add),
                             reads=[beb, b_hsbT], writes=[beb])
                    pending = [None]

                    def flush():
                        if pending[0] is None:
                            return
                        which_, cb_, dt_, db_, pT_, ot_, ob_ = pending[0]
                        pending[0] = None
                        for tt in range(4):
                            pt_, ptb_ = pT_[tt // 2]
                            pv = pt_.bitcast(BF16)
                            S.op("pe", lambda e, pv=pv, tt=tt, cb_=cb_, dt_=dt_: e.transpose(pv[:, (tt % 2) * 512 + cb_ * 128:(tt % 2) * 512 + (cb_ + 1) * 128], dt_[:, tt * 128:(tt + 1) * 128], ident[:]),
                                 reads=[db_, b_ident], writes=[ptb_])
                        if cb_ == 3:
                            for h2 in range(2):
                                pt_, ptb_ = pT_[h2]
                                pv = pt_.bitcast(BF16)
                                S.op("act", lambda e, ot_=ot_, pv=pv, h2=h2: e.activation(out=ot_[:, 2 * h2:2 * h2 + 2, :].rearrange("p a b -> p (a b)"), in_=pv[:, 0:1024], func=ACTF.Copy),
                                     reads=[ptb_], writes=[ob_], disjoint=True)
                            dst = (V_s, X1_s, X2_s)[which_]
                            S.dma("pool", dst[g0:g0 + 512, :].rearrange("(tt p) c -> p tt c", p=128), ot_[:], reads=[ob_])
                            if which_ == 1 and xns is not None:
                                a2_phaseB(nxt, xns)

                    for which in range(3):
                        ot, ob = obuf.next()
                        pT = None
                        for cb in range(4):
                            blk = which * 4 + cb
                            pm, pmb = PS.next()
                            for kb in range(8):
                                S.op("pe", lambda e, pm=pm, blk=blk, kb=kb, c0=c0: e.matmul(pm[:], Why[:, kb, blk * 128:(blk + 1) * 128], hT[:, kb, 1 + c0:1 + c0 + 512], start=(kb == 0), stop=(kb == 7)),
                                     reads=rd_main, writes=[pmb], signal=(kb == 7))
                            flush()
                            if cb == 0:
                                pT = [PS.next() for _ in range(2)]
                            u, ub = u32.next()
                            w0 = hswT[:, blk, 0:1]
                            w1 = hswT[:, blk, 1:2]
                            w2 = hswT[:, blk, 2:3]
                            S.op("act", lambda e, u=u, pm=pm, w1=w1, blk=blk: e.activation(out=u[:, 1:511], in_=pm[:, 1:511], func=ACTF.Identity, bias=hsbT[:, blk:blk + 1], scale=w1),
                                 reads=[pmb, b_hswT, b_hsbT], writes=[ub])
                            S.op("act", lambda e, u=u, pm=pm, w1=w1, blk=blk, be=be: e.activation(out=u[:, 0:1], in_=pm[:, 0:1], func=ACTF.Identity, bias=be[:, 0, blk:blk + 1], scale=w1),
                                 reads=[pmb, b_hswT, beb], writes=[ub], disjoint=True)
                            S.op("act", lambda e, u=u, pm=pm, w1=w1, blk=blk, be=be: e.activation(out=u[:, 511:512], in_=pm[:, 511:512], func=ACTF.Identity, bias=be[:, 1, blk:blk + 1], scale=w1),
                                 reads=[pmb, b_hswT, beb], writes=[ub], disjoint=True)
                            S.op("dve", lambda e, u=u, pm=pm, w0=w0: e.scalar_tensor_tensor(u[:, 1:512], pm[:, 0:511], w0, u[:, 1:512], ALU.mult, ALU.add),
                                 reads=[pmb, ub, b_hswT], writes=[ub])
                            dt_, db_ = ubf.next()
                            dv = dt_
                            S.op("dve", lambda e, u=u, pm=pm, w2=w2, dv=dv: e.scalar_tensor_tensor(dv[:, 0:511], pm[:, 1:512], w2, u[:, 0:511], ALU.mult, ALU.add),
                                 reads=[pmb, ub, b_hswT], writes=[db_])
                            S.op("dve", lambda e, u=u, dv=dv: e.tensor_copy(dv[:, 511:512], u[:, 511:512]), reads=[ub], writes=[db_], disjoint=True)
                            pending[0] = (which, cb, dt_, db_, pT, ot, ob)
                    flush()
            end_stage()

        with ExitStack() as es:
            LMAX = max(sq["L"] for sq in seqs)
            KBM = LMAX // 128
            vb, b_vb = T(es, "vb", [128, KBM, 512], BF16)
            Yb, b_Yb = T(es, "Yb", [128, 2 * KBM, 512], BF16)
            slabC = R(es, "slabCb", [128, KBM, 128], BF16, 2)
            slabS = R(es, "slabSb", [128, KBM, 128], BF16, 2)
            hsr = R(es, "hsr", [128, 2, 512], F32, 2)
            abr = R(es, "abr", [128, 2, 512], F32, 2)
            tmr = R(es, "tmr", [128, 4, 512], F32, 2)
            x1r = R(es, "x1r", [128, 512], BF16, 3)
            ztk = R(es, "ztk", [128, 512], BF16, 3)
            zbr = R(es, "zbr", [128, 4, 512], BF16, 2)
            for si, sq in enumerate(seqs):
                L = sq["L"]
                KB = L // 128
                tab = TAB[L]
                tok0 = sq["tok0"]
                S.dma("sp", vb[:, 0:KB, :], V_s[tok0:tok0 + L, :].rearrange("(kb p) c -> p kb c", p=128), writes=[b_vb])
                for order in range(2):
                    for j in range(KB):
                        ct, cb_ = slabC.next()
                        stt, sb_ = slabS.next()
                        S.dma("sp", ct[:, 0:KB, :], tab["TC"][j], writes=[cb_])
                        S.dma("sp", stt[:, 0:KB, :], tab["TSF"][j], writes=[sb_])
                        ht, hb = hsr.next()
                        S.dma("sp", ht[:, 0, :], tab["HS"][order, j], writes=[hb])
                        S.dma("sp", ht[:, 1, :], tab["HS"][order, KB + j], writes=[hb], disjoint=True)
                        pA, pAb = PS.next()
                        for kb in range(KB):
                            S.op("pe", lambda e, pA=pA, ct=ct, kb=kb: e.matmul(pA[:], ct[:, kb, :], vb[:, kb, :], start=(kb == 0), stop=(kb == KB - 1)),
                                 reads=[cb_, b_vb], writes=[pAb], signal=(kb == KB - 1))
                        pB, pBb = PS.next()
                        for kb in range(KB):
                            S.op("pe", lambda e, pB=pB, stt=stt, kb=kb: e.matmul(pB[:], stt[:, kb, :], vb[:, kb, :], start=(kb == 0), stop=(kb == KB - 1)),
                                 reads=[sb_, b_vb], writes=[pBb], signal=(kb == KB - 1))
                        ab_t, ab_b = abr.next()
                        S.op("act", lambda e, ab_t=ab_t, pA=pA: e.activation(out=ab_t[:, 0, :], in_=pA[:], func=ACTF.Copy), reads=[pAb], writes=[ab_b])
                        S.op("act", lambda e, ab_t=ab_t, pB=pB: e.activation(out=ab_t[:, 1, :], in_=pB[:], func=ACTF.Copy), reads=[pBb], writes=[ab_b])
                        tm, tmb = tmr.next()
                        S.op("dve", lambda e, tm=tm, ab_t=ab_t, ht=ht: e.tensor_tensor(tm[:, 0, :], ab_t[:, 0, :], ht[:, 0, :], ALU.mult), reads=[ab_b, hb], writes=[tmb])
                        S.op("pool", lambda e, tm=tm, ab_t=ab_t, ht=ht: e.tensor_tensor(tm[:, 1, :], ab_t[:, 1, :], ht[:, 1, :], ALU.mult), reads=[ab_b, hb], writes=[tmb])
                        S.op("pool", lambda e, tm=tm, ab_t=ab_t, ht=ht: e.tensor_tensor(tm[:, 2, :], ab_t[:, 0, :], ht[:, 1, :], ALU.mult), reads=[ab_b, hb], writes=[tmb])
                        S.op("dve", lambda e, tm=tm, ab_t=ab_t, ht=ht: e.tensor_tensor(tm[:, 3, :], ab_t[:, 1, :], ht[:, 0, :], ALU.mult), reads=[ab_b, hb], writes=[tmb])
                        S.op("dve", lambda e, tm=tm, j=j: e.tensor_tensor(Yb[:, j, :], tm[:, 0, :], tm[:, 1, :], ALU.subtract), reads=[tmb], writes=[b_Yb])
                        S.op("pool", lambda e, tm=tm, j=j, KB=KB: e.tensor_tensor(Yb[:, KB + j, :], tm[:, 2, :], tm[:, 3, :], ALU.add), reads=[tmb], writes=[b_Yb])
                        if j == 0:
                            S.op("dve", lambda e, tm=tm: e.tensor_copy(Yb[0:1, 0, :], tm[0:1, 0, :]), reads=[tmb], writes=[b_Yb])
                            S.op("dve", lambda e, tm=tm, KB=KB: e.tensor_copy(Yb[0:1, KB, :], tm[0:1, 1, :]), reads=[tmb], writes=[b_Yb])
                    if order == 0:
                        for tb in range(KB):
                            ct, cb_ = slabC.next()
                            stt, sb_ = slabS.next()
                            S.dma("sp", ct[:, 0:KB, :], tab["TC"][tb], writes=[cb_])
                            S.dma("sp", stt[:, 0:KB, :], tab["TSI"][tb], writes=[sb_])
                            x1t, x1b = x1r.next()
                            S.dma("sp", x1t[:], X1_s[tok0 + tb * 128: tok0 + (tb + 1) * 128, :], writes=[x1b])
                            py, pyb = PS.next()
                            for fb in range(KB):
                                S.op("pe", lambda e, py=py, ct=ct, fb=fb: e.matmul(py[:], ct[:, fb, :], Yb[:, fb, :], start=(fb == 0), stop=False),
                                     reads=[cb_, b_Yb], writes=[pyb], signal=False)
                            for fb in range(KB):
                                S.op("pe", lambda e, py=py, stt=stt, fb=fb, KB=KB: e.matmul(py[:], stt[:, fb, :], Yb[:, KB + fb, :], start=False, stop=(fb == KB - 1)),
                                     reads=[sb_, b_Yb], writes=[pyb], signal=(fb == KB - 1))
                            S.op("dve", lambda e, py=py, x1t=x1t, tb=tb: e.tensor_tensor(vb[:, tb, :], py[:], x1t[:], ALU.mult), reads=[pyb, x1b], writes=[b_vb])
                    else:
                        pend = [None]

                        def flush_t(tok0=tok0):
                            if pend[0] is None:
                                return
                            tb_, zk, zkb, zt_, ztb_ = pend[0]
                            pend[0] = None
                            pz, pzb = PS.next()
                            pzv = pz.bitcast(BF16)
                            for cb in range(4):
                                S.op("pe", lambda e, pzv=pzv, cb=cb, zk=zk: e.transpose(pzv[:, cb * 128:(cb + 1) * 128], zk[:, cb * 128:(cb + 1) * 128], ident[:]),
                                     reads=[zkb, b_ident], writes=[pzb], signal=(cb == 3))
                            q = tb_ % 4
                            S.op("act", lambda e, pzv=pzv, zt_=zt_, q=q: e.activation(out=zt_[:, :, q * 128:(q + 1) * 128], in_=pzv[:, 0:512].rearrange("p (a b) -> p a b", a=4), func=ACTF.Copy),
                                 reads=[pzb], writes=[ztb_], disjoint=True)
                            if q == 3:
                                g0 = tok0 + (tb_ // 4) * 512
                                S.dma("pool", ZB_s[:, g0:g0 + 512].rearrange("(cb p) t -> p cb t", p=128), zt_[:], reads=[ztb_])

                        zt, zb_ = None, None
                        for tb in range(KB):
                            if tb % 4 == 0:
                                zt, zb_ = zbr.next()
                            ct, cb_ = slabC.next()
                            stt, sb_ = slabS.next()
                            S.dma("sp", ct[:, 0:KB, :], tab["TC"][tb], writes=[cb_])
                            S.dma("sp", stt[:, 0:KB, :], tab["TSI"][tb], writes=[sb_])
                            x2t, x2b = x1r.next()
                            S.dma("sp", x2t[:], X2_s[tok0 + tb * 128: tok0 + (tb + 1) * 128, :], writes=[x2b])
                            py, pyb = PS.next()
                            for fb in range(KB):
                                S.op("pe", lambda e, py=py, ct=ct, fb=fb: e.matmul(py[:], ct[:, fb, :], Yb[:, fb, :], start=(fb == 0), stop=False),
                                     reads=[cb_, b_Yb], writes=[pyb], signal=False)
                            for fb in range(KB):
                                S.op("pe", lambda e, py=py, stt=stt, fb=fb, KB=KB: e.matmul(py[:], stt[:, fb, :], Yb[:, KB + fb, :], start=False, stop=(fb == KB - 1)),
                                     reads=[sb_, b_Yb], writes=[pyb], signal=(fb == KB - 1))
                            flush_t()
                            zk, zkb = ztk.next()
                            S.op("dve", lambda e, py=py, x2t=x2t, zk=zk: e.tensor_tensor(zk[:], py[:], x2t[:], ALU.mult), reads=[pyb, x2b], writes=[zkb])
                            pend[0] = (tb, zk, zkb, zt, zb_)
                        flush_t()
            end_stage()

        TC_ = 256
        NTT = TC_ // 128
        c_chunks = [(si, sq, ck) for si, sq in enumerate(seqs) for ck in range(sq["L"] // TC_)]
        with ExitStack() as es:
            Wpw, b_Wpw = T(es, "Wpw", [128, 4, D], BF16)
            Whyo, b_Whyo = T(es, "Whyo", [128, 4, D], BF16)
            Wout, b_Wout = T(es, "Wout", [128, 8, D], BF16)
            Lw, b_Lw = T(es, "Lw", [128, 16, 8, 32], BF16)
            nb0 = len(stage_bufs)
            with ExitStack() as es2:
                stg = R(es2, "stgc", [128, 8, 256], F32, 3)
                wsel, b_wsel = T(es2, "wsel", [128, 16, 8])
                E4, b_E4 = T(es2, "E4", [128, 32])
                S.dma("sp", wsel[:], dw_pk, writes=[b_wsel])
                S.op("dve", lambda e: e.tensor_tensor(E4[:], ident_f[:, 0:32], ident_f[:, 32:64], ALU.add), reads=[b_identf], writes=[b_E4])
                S.op("dve", lambda e: e.tensor_tensor(E4[:], E4[:], ident_f[:, 64:96], ALU.add), reads=[b_identf, b_E4], writes=[b_E4])
                S.op("dve", lambda e: e.tensor_tensor(E4[:], E4[:], ident_f[:, 96:128], ALU.add), reads=[b_identf, b_E4], writes=[b_E4])
                load_w_bf16(Wpw, b_Wpw, pw_w, CH, D, stg, 256)
                load_w_bf16(Whyo, b_Whyo, hyo_w, CH, D, stg, 256)
                load_w_bf16(Wout, b_Wout, w_out, D, D, stg, 256)
                for cg in range(16):
                    for g in range(8):
                        eng = "dve" if (cg * 8 + g) % 2 == 0 else "pool"
                        S.op(eng, lambda e, cg=cg, g=g: e.tensor_scalar_mul(Lw[:, cg, g, :], E4[:], wsel[:, cg, g:g + 1]),
                             reads=[b_E4, b_wsel], writes=[b_Lw], disjoint=True)
                sub_stage(nb0)
            cpar, b_cpar = T(es, "cpar", [128, 3, 4])
            g1bc, b_g1bc = T(es, "g1bc", [128, D])
            l1g, b_l1g = T(es, "l1g", [128, D])
            l1b, b_l1b = T(es, "l1b", [128, D])
            xr = R(es, "xrc", [128, D], F32, 3)
            lnb = dict(st=R(es, "stc", [128, 2, 6], F32, 2), mv=R(es, "mvc", [128, 4], F32, 2))
            sgl = R(es, "sgl", [128, 16, TC_], BF16, 2)
            ah = R(es, "ah", [128, 16, TC_ + 30], BF16, 3)
            zbl = R(es, "zbl", [128, 4, TC_], BF16, 2)
            acvr = R(es, "acv", [128, 4, TC_], F32, 2)
            asqr = R(es, "asq", [128, 4, TC_], F32, 2)
            acv_bufs = [[Buf("acv%d_%d" % (i, c)) for c in range(4)] for i in range(2)]
            asq_bufs = [[Buf("asq%d_%d" % (i, c)) for c in range(4)] for i in range(2)]
            b_an4 = [Buf("an%d" % c) for c in range(4)]
            b_mt8 = [Buf("mt%d" % c) for c in range(8)]
            stt_, b_stt = T(es, "stats", [128, 4, TC_])
            an, b_an = T(es, "an", [128, 4, TC_], BF16)
            mt, b_mt = T(es, "mt", [128, 8, TC_], BF16)
            tmp1 = R(es, "tmp1", [128, TC_], F32, 2)
            tmp2 = R(es, "tmp2", [128, TC_], F32, 2)
            rr = R(es, "rr", [128, D], F32, 4)
            x1o = R(es, "x1o", [128, D], F32, 2)
            S.dma("sp", cpar[:, 0, :], dw_bT, writes=[b_cpar])
            S.dma("sp", cpar[:, 1, :], cln_gT, writes=[b_cpar], disjoint=True)
            S.dma("sp", cpar[:, 2, :], cln_bT, writes=[b_cpar], disjoint=True)
            S.dma("sp", l1g[:], ln1_g.partition_broadcast(128), writes=[b_l1g])
            S.dma("sp", l1b[:], ln1_b.partition_broadcast(128), writes=[b_l1b])

            loaded = {}

            def c1_load(n):
                si, sq, ck = c_chunks[n]
                L = sq["L"]
                tok0 = sq["tok0"]
                c0 = ck * TC_
                at, ab = ah.next()
                W_ = TC_ + 28
                edge = (c0 - 15 < 0) or (c0 - 15 + 3 + W_ > L)
                if edge:
                    S.op("pool", lambda e, at=at: e.memset(at[:], 0.0), writes=[ab])
                for j in range(4):
                    s0 = c0 - 15 + j
                    lo = max(s0, 0)
                    hi = min(s0 + W_, L)
                    S.dma("sp", at[32 * j:32 * (j + 1), :, lo - s0: hi - s0], A_s[:, tok0 + lo: tok0 + hi].rearrange("(cg c) t -> c cg t", c=32),
                          writes=[ab], disjoint=(not edge))
                loaded[n] = (at, ab)

            def c1_conv(n):
                at, ab = loaded.pop(n)
                acv, _ = acvr.next()
                asq, _ = asqr.next()
                k_ = (acvr.i - 1) % 2
                b_acv = acv_bufs[k_]
                b_asq = asq_bufs[k_]
                for cb in range(4):
                    pt, pb = PS.next()
                    for g in range(8):
                        for i in range(4):
                            cg = cb * 4 + i
                            S.op("pe", lambda e, pt=pt, cg=cg, g=g, i=i, at=at: e.matmul(pt[32 * i:32 * (i + 1), 0:TC_], Lw[:, cg, g, :], at[:, cg, 4 * g:4 * g + TC_],
                                                                                    start=(g == 0), stop=(g == 7), skip_group_check=True, tile_position=(0, 32 * i)),
                                 reads=[b_Lw, ab], writes=[pb], signal=(g == 7 and i == 3))
                    S.op("act", lambda e, pt=pt, cb=cb, acv=acv: e.activation(out=acv[:, cb, :], in_=pt[:, 0:TC_], func=ACTF.Identity, bias=cpar[:, 0, cb:cb + 1]),
                         reads=[pb, b_cpar], writes=[b_acv[cb]])
                    S.op("pool", lambda e, cb=cb, acv=acv, asq=asq: e.tensor_tensor(asq[:, cb, :], acv[:, cb, :], acv[:, cb, :], ALU.mult), reads=[b_acv[cb]], writes=[b_asq[cb]])
                return acv, b_acv, asq, b_asq

            def c1_epilogue(ep):
                for (rt, rb, dst_rows, dbg_rows) in ep:
                    ot, ob = x1o.next()
                    _ln_aff(S, lnb, rt, rb, ot, ob, l1g, b_l1g, l1b, b_l1b)
                    S.dma("pool", dst_rows, ot[:], reads=[ob])
                    if debug:
                        S.dma("pool", dbg_rows, ot[:], reads=[ob])

            an2 = [T(es, "an2_%d" % i, [128, 4, TC_], BF16)[0] for i in range(2)]
            an2b = [[Buf("an2_%d_%d" % (i, c)) for c in range(4)] for i in range(2)]
            stt2 = [T(es, "stt2_%d" % i, [128, 4, TC_])[0] for i in range(2)]
            stt2b = [Buf("stt2_%d" % i) for i in range(2)]

            def c1_X(n):
                acv, b_acv, asq, b_asq = c1_conv(n)
                st_ = stt2[n % 2]
                b_st = stt2b[n % 2]
                an_ = an2[n % 2]
                b_an_ = an2b[n % 2]
                p1, p1b = PS.next()
                for cb in range(4):
                    S.op("pe", lambda e, p1=p1, cb=cb, acv=acv: e.matmul(p1[:, 0:TC_], ones_f[:], acv[:, cb, :], start=(cb == 0), stop=(cb == 3)),
                         reads=[b_ones, b_acv[cb]], writes=[p1b], signal=(cb == 3))
                p2, p2b = PS.next()
                for cb in range(4):
                    S.op("pe", lambda e, p2=p2, cb=cb, asq=asq: e.matmul(p2[:, 0:TC_], ones_f[:], asq[:, cb, :], start=(cb == 0), stop=(cb == 3)),
                         reads=[b_ones, b_asq[cb]], writes=[p2b], signal=(cb == 3))
                S.op("dve", lambda e, p1=p1: e.tensor_scalar_mul(st_[:, 0, :], p1[:, 0:TC_], 1.0 / CH), reads=[p1b], writes=[b_st])
                S.op("dve", lambda e: e.tensor_tensor(st_[:, 3, :], st_[:, 0, :], st_[:, 0, :], ALU.mult), reads=[b_st], writes=[b_st])
                S.op("dve", lambda e, p2=p2: e.scalar_tensor_tensor(st_[:, 1, :], p2[:, 0:TC_], 1.0 / CH, st_[:, 3, :], ALU.mult, ALU.subtract), reads=[p2b, b_st], writes=[b_st])
                S.op("act", lambda e: e.activation(out=st_[:, 2, :], in_=st_[:, 1, :], func=ACTF.Sqrt, bias=LN_EPS), reads=[b_st], writes=[b_st])
                S.op("dve", lambda e: e.reciprocal(st_[:, 2, :], st_[:, 2, :]), reads=[b_st], writes=[b_st])
                for cb in range(4):
                    S.op("dve", lambda e, cb=cb: e.tensor_tensor(acv[:, cb, :], acv[:, cb, :], st_[:, 0, :], ALU.subtract), reads=[b_acv[cb], b_st], writes=[b_acv[cb]])
                    S.op("pool", lambda e, cb=cb: e.tensor_tensor(acv[:, cb, :], acv[:, cb, :], st_[:, 2, :], ALU.mult), reads=[b_acv[cb], b_st], writes=[b_acv[cb]])
                    S.op("act", lambda e, cb=cb: e.activation(out=an_[:, cb, :], in_=acv[:, cb, :], func=ACTF.Silu, bias=cpar[:, 2, cb:cb + 1], scale=cpar[:, 1, cb:cb + 1]),
                         reads=[b_acv[cb], b_cpar], writes=[b_an_[cb]])

            c1_load(0)
            if len(c_chunks) > 1:
                c1_load(1)
            c1_X(0)
            pend_ep = None
            last_si = -1
            for n, (si, sq, ck) in enumerate(c_chunks):
                L = sq["L"]
                tok0 = sq["tok0"]
                c0 = ck * TC_
                g0 = tok0 + c0
                an_ = an2[n % 2]
                b_an_ = an2b[n % 2]
                sg, b_sg = sgl.next()
                S.dma("sp", sg[:], G_s[:, g0:g0 + TC_].rearrange("(gi p) t -> p gi t", p=128), writes=[b_sg])
                zt, zb_ = zbl.next()
                S.dma("sp", zt[:], ZB_s[:, g0:g0 + TC_].rearrange("(cb p) t -> p cb t", p=128), writes=[zb_])
                if n + 2 < len(c_chunks):
                    c1_load(n + 2)
                if n + 1 < len(c_chunks):
                    c1_X(n + 1)
                if pend_ep is not None:
                    c1_epilogue(pend_ep)
                    pend_ep = None
                if si != last_si:
                    S.dma("sp", g1bc[:], MODROW[si, 0:D].partition_broadcast(128), writes=[b_g1bc])
                    last_si = si
                for db in range(8):
                    pa, pab = PS.next()
                    for cb in range(4):
                        S.op("pe", lambda e, pa=pa, cb=cb, db=db, an_=an_: e.matmul(pa[:, 0:TC_], Wpw[:, cb, db * 128:(db + 1) * 128], an_[:, cb, :], start=(cb == 0), stop=(cb == 3)),
                             reads=[b_Wpw, b_an_[cb]], writes=[pab], signal=(cb == 3))
                    pbb, pbbb = PS.next()
                    for cb in range(4):
                        S.op("pe", lambda e, pbb=pbb, cb=cb, db=db, zt=zt: e.matmul(pbb[:, 0:TC_], Whyo[:, cb, db * 128:(db + 1) * 128], zt[:, cb, :], start=(cb == 0), stop=(cb == 3)),
                             reads=[b_Whyo, zb_], writes=[pbbb], signal=(cb == 3))
                    t1, t1b = tmp1.next()
                    t2, t2b = tmp2.next()
                    S.op("dve", lambda e, t1=t1, pa=pa, db=db, sg=sg: e.tensor_tensor(t1[:], pa[:, 0:TC_], sg[:, db, :], ALU.mult), reads=[pab, b_sg], writes=[t1b])
                    S.op("dve", lambda e, t2=t2, pbb=pbb, db=db, sg=sg: e.tensor_tensor(t2[:], pbb[:, 0:TC_], sg[:, 8 + db, :], ALU.mult), reads=[pbbb, b_sg], writes=[t2b])
                    S.op("pool", lambda e, t1=t1, t2=t2, db=db: e.tensor_tensor(mt[:, db, :], t1[:], t2[:], ALU.add), reads=[t1b, t2b], writes=[b_mt8[db]])
                ep = []
                for tt in range(NTT):
                    xt, xb = xr.next()
                    S.dma("sp", xt[:], xrows(sq, c0 + tt * 128, 128), writes=[xb])
                    rt, rb = rr.next()
                    for half in range(2):
                        pm, pmb = PS.next()
                        for kb in range(8):
                            S.op("pe", lambda e, pm=pm, kb=kb, tt=tt, half=half: e.matmul(pm[:], mt[:, kb, tt * 128:(tt + 1) * 128], Wout[:, kb, half * 512:(half + 1) * 512], start=(kb == 0), stop=(kb == 7)),
                                 reads=[b_mt8[kb], b_Wout], writes=[pmb], signal=(kb == 7))
                        S.op("dve", lambda e, rt=rt, pm=pm, half=half: e.tensor_tensor(rt[:, half * 512:(half + 1) * 512], pm[:], g1bc[:, half * 512:(half + 1) * 512], ALU.mult),
                             reads=[pmb, b_g1bc], writes=[rb], disjoint=(half > 0))
                    S.op("dve", lambda e, rt=rt, xt=xt: e.scalar_tensor_tensor(rt[:], xt[:], ALPHA, rt[:], ALU.mult, ALU.add), reads=[xb, rb], writes=[rb])
                    ep.append((rt, rb, yrows(sq, c0 + tt * 128, 128), X1DBG[g0 + tt * 128:g0 + (tt + 1) * 128, :] if debug else None))
                pend_ep = ep
            if pend_ep is not None:
                c1_epilogue(pend_ep)
            end_stage()

        with ExitStack() as es:
            W1, b_W1 = T(es, "W1", [128, 8, DFF], BF16)
            W2, b_W2 = T(es, "W2", [128, 32, D], BF16)
            nb0 = len(stage_bufs)
            with ExitStack() as es2:
                stg = R(es2, "stgd", [128, 8, 256], F32, 2)
                stg2 = R(es2, "stgd2", [128, 32, 64], F32, 2)
                load_w_bf16(W1, b_W1, mlp_w1, D, DFF, stg, 256)
                load_w_bf16(W2, b_W2, mlp_w2, DFF, D, stg2, 64)
                sub_stage(nb0)
            b1T, b_b1T = T(es, "b1T", [128, 32])
            g2bc, b_g2bc = T(es, "g2bc", [128, D])
            l2g, b_l2g = T(es, "l2g", [128, D])
            l2b, b_l2b = T(es, "l2b", [128, D])
            b2f, b_b2f = T(es, "b2f", [1, D])
            b2r, b_b2r = T(es, "b2r", [1, D], BF16)
            onesr, b_onesr = T(es, "onesr", [1, 128], BF16)
            xr = R(es, "xrd", [128, D], F32, 2 * NTT)
            lnb = ln_rings(es, "c2", NTT, NTT)
            lnb2 = dict(st=R(es, "std", [128, 2, 6], F32, 2), mv=R(es, "mvd", [128, 4], F32, 2))
            hTt = [T(es, "hTd%d" % i, [128, 8, TC_], BF16)[0] for i in range(2)]
            hTb = [[Buf("hTd%d_%d" % (i, t)) for t in range(NTT)] for i in range(2)]
            fT, _ = T(es, "fT", [128, 32, TC_], BF16)
            fTb = [Buf("fT%d" % i) for i in range(32)]
            rl = R(es, "rl", [128, TC_], F32, 2)
            rr = R(es, "rrd", [128, D], F32, 1)
            yo = R(es, "yo", [128, D], F32, 2)
            S.dma("sp", b1T[:], mlp_b1T, writes=[b_b1T])
            S.dma("sp", l2g[:], ln2_g.partition_broadcast(128), writes=[b_l2g])
            S.dma("sp", l2b[:], ln2_b.partition_broadcast(128), writes=[b_l2b])
            S.dma("sp", b2f[:], mlp_b2.rearrange("(a d) -> a d", a=1), writes=[b_b2f])
            S.op("dve", lambda e: e.tensor_copy(b2r[:], b2f[:]), reads=[b_b2f], writes=[b_b2r])
            S.op("pool", lambda e: e.memset(onesr[:], 1.0), writes=[b_onesr])

            def c2_phaseA(n):
                si, sq, ck = c_chunks[n]
                tiles = []
                for tt in range(NTT):
                    xt, xb = xr.next()
                    S.dma("sp", xt[:], yrows(sq, ck * TC_ + tt * 128, 128), writes=[xb])
                    tiles.append((xt[:], xb))
                return tiles, ln_phaseA(lnb, tiles)

            def c2_phaseB(n, xns):
                si, sq, ck = c_chunks[n]
                hTc = hTt[n % 2]
                ln_phaseB(xns, lambda t, kb, hTc=hTc: hTc[:, kb, t * 128:(t + 1) * 128], hTb[n % 2], si, 24, 32)

            tiles_cur, xns0 = c2_phaseA(0)
            c2_phaseB(0, xns0)
            last_si = -1
            for n, (si, sq, ck) in enumerate(c_chunks):
                c0 = ck * TC_
                if si != last_si:
                    S.dma("sp", g2bc[:], MODROW[si, D:2 * D].partition_broadcast(128), writes=[b_g2bc])
                    last_si = si
                hTc = hTt[n % 2]
                hb_ = hTb[n % 2]
                nxt = None
                for fb in range(32):
                    if fb == 8 and n + 1 < len(c_chunks):
                        nxt = c2_phaseA(n + 1)
                    pt, pb = PS.next()
                    for kb in range(8):
                        S.op("pe", lambda e, pt=pt, kb=kb, fb=fb, hTc=hTc: e.matmul(pt[:, 0:TC_], W1[:, kb, fb * 128:(fb + 1) * 128], hTc[:, kb, :], start=(kb == 0), stop=(kb == 7)),
                             reads=[b_W1] + hb_, writes=[pb], signal=(kb == 7))
                    rt_, rtb = rl.next()
                    S.op("act", lambda e, rt_=rt_, pt=pt, fb=fb: e.activation(out=rt_[:], in_=pt[:, 0:TC_], func=ACTF.Relu, bias=b1T[:, fb:fb + 1]), reads=[pb, b_b1T], writes=[rtb])
                    eng = "dve" if fb % 2 == 0 else "pool"
                    S.op(eng, lambda e, rt_=rt_, fb=fb: e.tensor_tensor(fT[:, fb, :], rt_[:], rt_[:], ALU.mult), reads=[rtb], writes=[fTb[fb]])
                if nxt is not None:
                    c2_phaseB(n + 1, nxt[1])
                for tt in range(NTT):
                    xt, xb = tiles_cur[tt]
                    rt, rb = rr.next()
                    for half in range(2):
                        pm, pmb = PS.next()
                        for fb in range(32):
                            S.op("pe", lambda e, pm=pm, fb=fb, tt=tt, half=half: e.matmul(pm[:], fT[:, fb, tt * 128:(tt + 1) * 128], W2[:, fb, half * 512:(half + 1) * 512], start=(fb == 0), stop=False),
                                 reads=[fTb[fb], b_W2], writes=[pmb], signal=False)
                        S.op("pe", lambda e, pm=pm, half=half: e.matmul(pm[:], onesr[:], b2r[:, half * 512:(half + 1) * 512], start=False, stop=True),
                             reads=[b_onesr, b_b2r], writes=[pmb], signal=True)
                        S.op("dve", lambda e, rt=rt, pm=pm, half=half: e.tensor_tensor(rt[:, half * 512:(half + 1) * 512], pm[:], g2bc[:, half * 512:(half + 1) * 512], ALU.mult),
                             reads=[pmb, b_g2bc], writes=[rb], disjoint=(half > 0))
                    S.op("dve", lambda e, rt=rt, xt=xt: e.scalar_tensor_tensor(rt[:], xt, ALPHA, rt[:], ALU.mult, ALU.add), reads=[xb, rb], writes=[rb])
                    ot, ob = yo.next()
                    _ln_aff(S, lnb2, rt, rb, ot, ob, l2g, b_l2g, l2b, b_l2b)
                    S.dma("pool", yrows(sq, c0 + tt * 128, 128), ot[:], reads=[ob])
                if nxt is not None:
                    tiles_cur = nxt[0]
            end_stage()
    return nc


def _ln_aff(S, lnb, rt, rb, ot, ob, g, gb, b, bb):
    st, stb = lnb["st"].next()
    mv, mvb = lnb["mv"].next()
    for i in range(2):
        S.op("dve", lambda e, i=i: e.bn_stats(st[:, i, :], rt[:, i * 512:(i + 1) * 512]), reads=[rb], writes=[stb])
    S.op("dve", lambda e: e.bn_aggr(mv[:, 0:2], st[:].rearrange("p a b -> p (a b)")), reads=[stb], writes=[mvb])
    S.op("act", lambda e: e.activation(out=mv[:, 2:3], in_=mv[:, 1:2], func=ACTF.Sqrt, bias=LN_EPS), reads=[mvb], writes=[mvb])
    S.op("dve", lambda e: e.reciprocal(mv[:, 2:3], mv[:, 2:3]), reads=[mvb], writes=[mvb])
    S.op("dve", lambda e: e.scalar_tensor_tensor(mv[:, 3:4], mv[:, 0:1], -1.0, mv[:, 2:3], ALU.mult, ALU.mult), reads=[mvb], writes=[mvb])
    S.op("act", lambda e: e.activation(out=ot[:], in_=rt[:], func=ACTF.Identity, bias=mv[:, 3:4], scale=mv[:, 2:3]), reads=[rb, mvb], writes=[ob])
    S.op("pool", lambda e: e.tensor_tensor(ot[:], ot[:], g[:], ALU.mult), reads=[ob, gb], writes=[ob])
    S.op("dve", lambda e: e.tensor_tensor(ot[:], ot[:], b[:], ALU.add), reads=[ob, bb], writes=[ob])


_TABLE_CACHE = {}


def _tables(L):
    if L in _TABLE_CACHE:
        return _TABLE_CACHE[L]
    KB = L // 128
    n = np.arange(L, dtype=np.int64)
    m = (n[:, None] * n[None, :]) % (2 * L)
    ang = m.astype(np.float64) * (math.pi / L)
    C = np.cos(ang)
    Sn = np.sin(ang)
    alt = np.where(n % 2 == 0, 1.0, -1.0)
    SF = Sn.copy()
    SF[:, 0] = alt
    SI = Sn.copy()
    SI[0, :] = alt

    def lay(M):
        return np.ascontiguousarray(M.reshape(KB, 128, KB, 128).transpose(2, 1, 0, 3)).astype(ml_dtypes.bfloat16)

    t = np.linspace(0.0, 1.0, L, dtype=np.float32)[:, None]
    w = ((2.0 * math.pi / L) * np.arange(L, dtype=np.float32))[:, None].astype(np.float32)
    bands = np.linspace(1e-4, 15, 16, dtype=np.float32)[None, :]
    z = np.concatenate([t, np.cos(w * bands), -np.sin(w * bands)], axis=-1).astype(np.float32)
    max_decay = math.log(1e-2) / 0.3
    min_decay = math.log(1e-2) / 1.5
    deltas = np.abs(np.linspace(min_decay, max_decay, CH, dtype=np.float32))
    decF = np.exp(-t * deltas[None, :]).astype(np.float32)
    decB = np.exp(-t * deltas[::-1][None, :]).astype(np.float32)
    out = dict(zT=np.ascontiguousarray(z.T), decF=decF, decB=decB, TC=lay(C), TSF=lay(SF), TSI=lay(SI))
    _TABLE_CACHE[L] = out
    return out


def _dw_pack(w):
    wp = np.zeros((32, 512), np.float32)
    wp[:31] = w
    return np.ascontiguousarray(wp.reshape(8, 4, 16, 32).transpose(1, 3, 2, 0).reshape(128, 16, 8))


def _colT(v, nblk):
    return np.ascontiguousarray(np.asarray(v, np.float32).reshape(nblk, 128).T)


def run(cfg, inputs, ncores, debug=False):
    NP, LP, NS, LS = cfg["NP"], cfg["LP"], cfg["NS"], cfg["LS"]
    f = lambda k: np.ascontiguousarray(np.asarray(inputs[k], dtype=np.float32))
    shared = dict(
        altc=np.ascontiguousarray(np.stack([np.where(np.arange(128) % 2 == 0, 1.0, -1.0)] * 2, axis=1).astype(np.float32)),
        w_ada=f("w_ada")[0], b_ada=f("b_ada")[0], b_adaT=_colT(f("b_ada")[0], 48),
        w_in=f("w_in")[0],
        dw_wT=np.ascontiguousarray(f("conv_dw_w")[0].reshape(31, 4, 128).transpose(2, 1, 0)),
        dw_pk=_dw_pack(f("conv_dw_w")[0]),
        dw_bT=_colT(f("conv_dw_b")[0], 4), cln_gT=_colT(f("conv_ln_g")[0], 4), cln_bT=_colT(f("conv_ln_b")[0], 4),
        pw_w=f("conv_pw_w")[0], hs_w=f("hy_short_w")[0], hs_b=f("hy_short_b")[0], hs_bT=_colT(f("hy_short_b")[0], 12),
        hs_wT=np.ascontiguousarray(f("hy_short_w")[0].reshape(3, 12, 128).transpose(2, 1, 0)),
        f_w1=f("hy_ffn_w1")[0], f_b1=f("hy_ffn_b1")[0].reshape(64, 1), f_fr=f("hy_sin_freq")[0].reshape(64, 1),
        f_w2=f("hy_ffn_w2")[0], f_b2=f("hy_ffn_b2")[0].reshape(64, 1), f_w3=f("hy_ffn_w3")[0],
        hy_skip=f("hy_skip")[0], hyo_w=f("hy_out_w")[0], w_out=f("w_out")[0],
        ln1_g=f("ln1_g")[0], ln1_b=f("ln1_b")[0], mlp_w1=f("mlp_w1")[0], mlp_b1T=_colT(f("mlp_b1")[0], 32),
        mlp_w2=f("mlp_w2")[0], mlp_b2=f("mlp_b2")[0], ln2_g=f("ln2_g")[0], ln2_b=f("ln2_b")[0],
    )
    Ls = sorted(set(([LP] if NP else []) + ([LS] if NS else [])))
    for L in Ls:
        for k, v in _tables(L).items():
            shared["%s_%d" % (k, L)] = v
    xp = f("x_prompt") if NP else None
    xs = f("x_sample") if NS else None
    cp = f("c_prompt") if NP else None
    cs = f("c_sample") if NS else None
    in_maps = []
    for c in range(ncores):
        m = dict(shared)
        cl = []
        if NP:
            m["x_p"] = xp[c * NP:(c + 1) * NP].reshape(NP * LP, D)
            cl.append(cp[c * NP:(c + 1) * NP])
        if NS:
            m["x_s"] = xs[c * NS:(c + 1) * NS].reshape(NS * LS, D)
            cl.append(cs[c * NS:(c + 1) * NS])
        call = np.concatenate(cl, axis=0)
        m["cT"] = np.ascontiguousarray(call.reshape(-1, 8, 128).transpose(2, 1, 0))
        in_maps.append(m)
    nc = build(cfg, debug=debug)
    res = run_bass_kernel_spmd(nc, in_maps, core_ids=list(range(ncores)))
    outs = []
    if NP:
        outs.append(np.concatenate([np.asarray(r["y_p"], np.float32).reshape(NP, LP, D) for r in res.results], axis=0))
    if NS:
        outs.append(np.concatenate([np.asarray(r["y_s"], np.float32).reshape(NS, LS, D) for r in res.results], axis=0))
    return tuple(outs), res


def kernel(**inputs):
    cfg = dict(NP=4, LP=2048, NS=2, LS=4096)
    outs, _ = run(cfg, inputs, NCORES)
    return outs
```

```python
import math
from contextlib import ExitStack

import numpy as np
import ml_dtypes

import concourse.bass as bass
import concourse.mybir as mybir
from concourse.bass_utils import run_bass_kernel_spmd

F32 = mybir.dt.float32
BF16 = mybir.dt.bfloat16
ACTF = mybir.ActivationFunctionType
ALU = mybir.AluOpType

D = 1024
DFF = 4096
CH = 512
NCORES = 8
LN_EPS = 1e-5
ALPHA = 2.0 ** 0.25
MAGIC = 12582912.0
TWO_PI = 2.0 * math.pi

ENGS = ("pe", "act", "dve", "pool", "sp")
import os as _os0
NODISJOINT = bool(_os0.environ.get("NODISJOINT"))
CHECK = bool(_os0.environ.get("SCHED_CHECK"))


class Buf:
    __slots__ = ("name", "w", "r", "pr", "lsem")

    def __init__(self, name):
        self.name = name
        self.w = []
        self.r = []
        self.pr = []
        self.lsem = None


class Sched:
    def __init__(self, nc):
        self.nc = nc
        self.ops = {e: [] for e in ENGS}
        self.cnt = {e: 0 for e in ENGS}
        self.seen = {e: {} for e in ENGS}
        self.dcnt = {}
        self.free_dma_sems = []
        self.sems = {}
        self.pending_nosig = {e: False for e in ENGS}
        self.nblk = 0
        self.log = []
        self.chk_sem = {}
        self.chk_semvc = {}
        self.chk_vc = {e: {} for e in ENGS}
        self.chk_acc = {}

    def register(self, eng_sems, dma_sems):
        for e, s in zip(ENGS, eng_sems):
            self.sems[e] = s
        for i, s in enumerate(dma_sems):
            k = "d%d" % i
            self.sems[k] = s
            self.dcnt[k] = 0
            self.free_dma_sems.append(k)

    def release(self, bufs):
        for b in bufs:
            if b.lsem is not None:
                self.free_dma_sems.append(b.lsem)
                b.lsem = None

    def _dma_sem(self, buf):
        if buf.lsem is None:
            if not self.free_dma_sems:
                raise RuntimeError("out of DMA semaphores")
            buf.lsem = self.free_dma_sems.pop(0)
        return buf.lsem

    def _waits(self, eng, reads, writes, disjoint=False):
        need = {}
        for b in reads:
            for (s, v) in b.w:
                need[s] = max(need.get(s, 0), v)
        for b in writes:
            if not disjoint:
                for (s, v) in b.w:
                    need[s] = max(need.get(s, 0), v)
            for (s, v) in b.r:
                need[s] = max(need.get(s, 0), v)
            for (s, v) in b.pr:
                need[s] = max(need.get(s, 0), v)
        out = []
        for s, v in need.items():
            if s == eng and eng == "pe":
                continue
            if self.seen[eng].get(s, 0) >= v:
                continue
            self.seen[eng][s] = v
            out.append((s, v))
        return out

    def _note(self, tok, reads, writes, disjoint):
        for b in reads:
            b.r.append(tok)
            if len(b.r) > 48:
                b.r = _compact(b.r)
        for b in writes:
            if disjoint:
                if b.r:
                    b.pr = b.r
                    b.r = []
                    b.w = [tok]
                else:
                    b.w.append(tok)
                    if len(b.w) > 48:
                        b.w = _compact(b.w)
            else:
                b.pr = _compact(b.r + b.w)
                b.w = [tok]
                b.r = []

    def op(self, eng, fn, reads=(), writes=(), signal=True, disjoint=False):
        if NODISJOINT:
            disjoint = False
        waits = self._waits(eng, reads, writes, disjoint)
        if signal:
            self.cnt[eng] += 1
            tok = (eng, self.cnt[eng])
            self.pending_nosig[eng] = False
            inc = (eng, 1)
        else:
            tok = (eng, self.cnt[eng] + 1)
            self.pending_nosig[eng] = True
            inc = None
        self.ops[eng].append((waits, fn, inc))
        if CHECK:
            self.log.append(("op", eng, list(waits), inc, [b for b in reads], [b for b in writes], disjoint, len(self.ops[eng]) - 1))
        self._note(tok, reads, writes, disjoint)
        return tok

    def dma(self, queue, out_ap, in_ap, reads=(), writes=(), sembuf=None, disjoint=False, **kw):
        waits = self._waits(queue, reads, writes, disjoint)
        if sembuf is None:
            sembuf = writes[0] if writes else reads[0]
        k = self._dma_sem(sembuf)
        self.dcnt[k] += 16
        tok = (k, self.dcnt[k])
        fn = lambda e, o=out_ap, i=in_ap, kw=kw: e.dma_start(out=o, in_=i, **kw)
        self.ops[queue].append((waits, fn, (k, 16)))
        if CHECK:
            self.log.append(("dma", queue, list(waits), (k, 16), [b for b in reads], [b for b in writes], disjoint, len(self.ops[queue]) - 1))
        self._note(tok, reads, writes, disjoint)
        return tok

    def check(self):
        queues = {e: [] for e in ENGS}
        for ent in self.log:
            queues[ent[1]].append(ent)
        full = {e: [] for e in ENGS}
        for e in ENGS:
            li = {ent[7]: ent for ent in queues[e]}
            for i, (waits, fn, inc) in enumerate(self.ops[e]):
                if i in li:
                    full[e].append(li[i])
                else:
                    full[e].append(("bar", e, list(waits), inc, [], [], False, i))
        ptr = {e: 0 for e in ENGS}
        sem = self.chk_sem
        semvc = self.chk_semvc
        vc = self.chk_vc
        pos = getattr(self, "chk_pos", {e: 0 for e in ENGS})
        nerr = 0

        def join(a, b):
            for k, v in b.items():
                if a.get(k, 0) < v:
                    a[k] = v

        progress = True
        while progress:
            progress = False
            for e in ENGS:
                while ptr[e] < len(full[e]):
                    kind, _, waits, inc, reads, writes, disjoint, _i = full[e][ptr[e]]
                    ok = all(sem.get(s, 0) >= v for s, v in waits)
                    if not ok:
                        break
                    cur = dict(vc[e])
                    for s_, v in waits:
                        key = (s_, v)
                        if key in semvc:
                            join(cur, semvc[key])
                        else:
                            cands = [kk for kk in semvc if kk[0] == s_ and kk[1] <= v]
                            for kk in cands:
                                join(cur, semvc[kk])
                    pos[e] += 1
                    cur[e] = pos[e]
                    vc[e] = cur
                    if kind != "bar":
                        if kind == "dma":
                            thread = inc[0]
                            done = dict(cur)
                            sem[thread] = sem.get(thread, 0) + 16
                            done[thread] = sem[thread]
                            semvc[(thread, sem[thread])] = done
                            acc_vc_start = cur
                            acc_key = (thread, sem[thread])
                        else:
                            if inc is not None:
                                sem[inc[0]] = sem.get(inc[0], 0) + 1
                                semvc[(inc[0], sem[inc[0]])] = cur
                            acc_key = (e, pos[e])
                            acc_vc_start = cur
                        for b in reads:
                            a = self.chk_acc.setdefault(id(b), dict(w=[], r=[], name=b.name))
                            for (k, p) in a["w"]:
                                if acc_vc_start.get(k, 0) < p and not (k == e == "pe"):
                                    nerr += 1
                                    if nerr < 20:
                                        print("RACE RAW buf=%s reader=%s@%d writer=%s@%d" % (b.name, e, pos[e], k, p))
                            a["r"].append(acc_key)
                            if len(a["r"]) > 200:
                                a["r"] = a["r"][-100:]
                        for b in writes:
                            a = self.chk_acc.setdefault(id(b), dict(w=[], r=[], name=b.name))
                            for (k, p) in a["r"]:
                                if acc_vc_start.get(k, 0) < p and not (k == e == "pe"):
                                    nerr += 1
                                    if nerr < 20:
                                        print("RACE WAR buf=%s writer=%s@%d reader=%s@%d" % (b.name, e, pos[e], k, p))
                            if not disjoint:
                                for (k, p) in a["w"]:
                                    if acc_vc_start.get(k, 0) < p and not (k == e == "pe"):
                                        nerr += 1
                                        if nerr < 20:
                                            print("RACE WAW buf=%s writer=%s@%d prev=%s@%d" % (b.name, e, pos[e], k, p))
                                a["w"] = [acc_key]
                                a["r"] = []
                            else:
                                if a["r"]:
                                    a["w"] = [acc_key]
                                    a["r"] = []
                                else:
                                    a["w"].append(acc_key)
                    ptr[e] += 1
                    progress = True
        for e in ENGS:
            if ptr[e] < len(full[e]):
                print("DEADLOCK: engine %s stuck at op %d/%d waits=%s semvals=%s" % (
                    e, ptr[e], len(full[e]), full[e][ptr[e]][2], {s_: sem.get(s_, 0) for s_, _ in full[e][ptr[e]][2]}))
                nerr += 1
        self.chk_pos = pos
        self.log = []
        print("[check] block ok" if nerr == 0 else "[check] %d problems" % nerr)

    def barrier(self):
        allw = [(e, self.cnt[e]) for e in ENGS if self.cnt[e] > 0]
        allw += [(k, v) for k, v in self.dcnt.items() if v > 0]
        for e in ENGS:
            assert not self.pending_nosig[e], "engine %s has trailing non-signalling op" % e
            waits = []
            for s, v in allw:
                if s == e:
                    continue
                if self.seen[e].get(s, 0) >= v:
                    continue
                self.seen[e][s] = v
                waits.append((s, v))
            if waits:
                self.ops[e].append((waits, None, None))

    def emit(self):
        nc = self.nc
        S = self
        if CHECK:
            self.check()

        def replay(e, name):
            for waits, fn, inc in S.ops[name]:
                for s, v in waits:
                    e.wait_ge(S.sems[s], v)
                if fn is not None:
                    ins = fn(e)
                    if inc is not None:
                        ins.then_inc(S.sems[inc[0]], inc[1])

        with nc.Block() as block:
            @block.tensor
            def _(e):
                replay(e, "pe")

            @block.scalar
            def _(e):
                replay(e, "act")

            @block.vector
            def _(e):
                replay(e, "dve")

            @block.gpsimd
            def _(e):
                replay(e, "pool")

            @block.sync
            def _(e):
                replay(e, "sp")
        self.ops = {e: [] for e in ENGS}


def _compact(toks):
    best = {}
    for s, v in toks:
        best[s] = max(best.get(s, 0), v)
    return list(best.items())


class Ring:
    def __init__(self, es, nc, name, shape, dt, n, psum=False):
        mk = nc.psum_tensor if psum else nc.sbuf_tensor
        self.t = [es.enter_context(mk("%s_%d" % (name, i), shape, dt)) for i in range(n)]
        self.b = [Buf("%s_%d" % (name, i)) for i in range(n)]
        self.n = n
        self.i = 0

    def next(self):
        k = self.i % self.n
        self.i += 1
        return self.t[k], self.b[k]


def build(cfg, debug=False):
    NP, LP, NS, LS = cfg["NP"], cfg["LP"], cfg["NS"], cfg["LS"]
    NSEQ = NP + NS
    NTOK = NP * LP + NS * LS
    Ls = sorted(set(([LP] if NP else []) + ([LS] if NS else [])))
    seqs = []
    t0 = 0
    for i in range(NP):
        seqs.append(dict(g="p", i=i, L=LP, tok0=t0, row0=i * LP))
        t0 += LP
    for i in range(NS):
        seqs.append(dict(g="s", i=i, L=LS, tok0=t0, row0=i * LS))
        t0 += LS

    nc = bass.Bass("TRN2", target_bir_lowering=False)

    def din(name, shape, dt=F32):
        return nc.dram_tensor(name, list(shape), dt, kind="ExternalInput").ap()

    def dout(name, shape, dt=F32):
        return nc.dram_tensor(name, list(shape), dt, kind="ExternalOutput").ap()

    def dscr(name, shape, dt):
        return nc.dram_tensor(name, list(shape), dt, kind=("ExternalOutput" if debug else "Internal")).ap()

    X = {}
    Y = {}
    if NP:
        X["p"] = din("x_p", [NP * LP, D])
        Y["p"] = dout("y_p", [NP * LP, D])
    if NS:
        X["s"] = din("x_s", [NS * LS, D])
        Y["s"] = dout("y_s", [NS * LS, D])
    cT = din("cT", [128, 8, NSEQ])
    w_ada = din("w_ada", [D, 6 * D])
    b_ada = din("b_ada", [6 * D])
    b_adaT = din("b_adaT", [128, 48])
    w_in = din("w_in", [D, 4608])
    dw_wT = din("dw_wT", [128, 4, 31])
    dw_pk = din("dw_pk", [128, 16, 8])
    dw_bT = din("dw_bT", [128, 4])
    cln_gT = din("cln_gT", [128, 4])
    cln_bT = din("cln_bT", [128, 4])
    pw_w = din("pw_w", [CH, D])
    hs_w = din("hs_w", [3, 1536])
    hs_b = din("hs_b", [1536])
    hs_bT = din("hs_bT", [128, 12])
    hs_wT = din("hs_wT", [128, 12, 3])
    f_w1 = din("f_w1", [33, 64])
    f_b1 = din("f_b1", [64, 1])
    f_fr = din("f_fr", [64, 1])
    f_w2 = din("f_w2", [64, 64])
    f_b2 = din("f_b2", [64, 1])
    f_w3 = din("f_w3", [64, 2048])
    hy_skip = din("hy_skip", [2, 512])
    hyo_w = din("hyo_w", [CH, D])
    w_out = din("w_out", [D, D])
    ln1_g = din("ln1_g", [D])
    ln1_b = din("ln1_b", [D])
    mlp_w1 = din("mlp_w1", [D, DFF])
    mlp_b1T = din("mlp_b1T", [128, 32])
    mlp_w2 = din("mlp_w2", [DFF, D])
    mlp_b2 = din("mlp_b2", [D])
    ln2_g = din("ln2_g", [D])
    ln2_b = din("ln2_b", [D])
    altc = din("altc", [128, 2])
    TAB = {}
    for L in Ls:
        KB = L // 128
        TAB[L] = dict(
            zT=din("zT_%d" % L, [33, L]),
            decF=din("decF_%d" % L, [L, CH]),
            decB=din("decB_%d" % L, [L, CH]),
            TC=din("TC_%d" % L, [KB, 128, KB, 128], BF16),
            TSF=din("TSF_%d" % L, [KB, 128, KB, 128], BF16),
            TSI=din("TSI_%d" % L, [KB, 128, KB, 128], BF16),
            HS=dscr("HS_%d" % L, [2, 2 * KB, 128, CH], F32),
        )
    V_s = dscr("V_s", [NTOK, CH], BF16)
    X1_s = dscr("X1_s", [NTOK, CH], BF16)
    X2_s = dscr("X2_s", [NTOK, CH], BF16)
    A_s = dscr("A_s", [CH, NTOK], BF16)
    ZB_s = dscr("ZB_s", [CH, NTOK], BF16)
    MODROW = dscr("MODROW", [NSEQ, 2 * D], F32)
    G_s = dscr("G_s", [2 * D, NTOK], BF16)
    X1DBG = dscr("X1DBG", [NTOK, D], F32) if debug else None

    def xrows(sq, t, n):
        return X[sq["g"]][sq["row0"] + t: sq["row0"] + t + n, :]

    def yrows(sq, t, n):
        return Y[sq["g"]][sq["row0"] + t: sq["row0"] + t + n, :]

    with ExitStack() as top:
        esems = [top.enter_context(nc.semaphore("eng%d" % i)) for i in range(5)]
        dsems = []
        for i in range(98):
            try:
                dsems.append(top.enter_context(nc.semaphore("dma%d" % i)))
            except KeyError:
                break
        S = Sched(nc)
        S.register(esems, dsems)

        PS = Ring(top, nc, "ps", [128, 512], F32, 8, psum=True)

        uniq = [0]

        def Tt(es, name, shape, dt=F32):
            uniq[0] += 1
            name = "%s_u%d" % (name, uniq[0])
            return es.enter_context(nc.sbuf_tensor(name, list(shape), dt)), Buf(name)

        ident_f, b_identf = Tt(top, "ident_f", [128, 128], F32)
        ident, b_ident = Tt(top, "ident", [128, 128], BF16)
        ones_f, b_ones = Tt(top, "ones_f", [128, 128], F32)
        modT, b_modT = Tt(top, "modT", [128, 48, NSEQ], F32)
        S.op("pool", lambda e: e.memset(ident_f[:], 0.0), writes=[b_identf])
        S.op("pool", lambda e: e.affine_select(out=ident_f[:], in_=ident_f[:], compare_op=ALU.not_equal, fill=1.0,
                                               base=0, pattern=[[-1, 128]], channel_multiplier=1),
             reads=[b_identf], writes=[b_identf])
        S.op("dve", lambda e: e.tensor_copy(ident[:], ident_f[:]), reads=[b_identf], writes=[b_ident])
        S.op("pool", lambda e: e.memset(ones_f[:], 1.0), writes=[b_ones])

        stage_bufs = []

        def end_stage():
            S.barrier()
            S.emit()
            S.release(stage_bufs)
            del stage_bufs[:]

        def T(es, name, shape, dt=F32):
            t, b = Tt(es, name, shape, dt)
            stage_bufs.append(b)
            return t, b

        def R(es, name, shape, dt, n):
            uniq[0] += 1
            r = Ring(es, nc, "%s_u%d" % (name, uniq[0]), shape, dt, n)
            stage_bufs.extend(r.b)
            return r

        with ExitStack() as es:
            sc, b_sc = T(es, "sc", [128, 8, NSEQ])
            badaT, b_badaT = T(es, "badaT", [128, 48])
            wr = R(es, "wada", [128, 8, 512], F32, 2)
            brow = R(es, "brow", [NSEQ, 512], F32, 2)
            grow = R(es, "grow", [NSEQ, 512], F32, 2)
            S.dma("sp", sc[:], cT, writes=[b_sc])
            S.dma("sp", badaT[:], b_adaT, writes=[b_badaT])
            S.op("act", lambda e: e.activation(out=sc[:], in_=sc[:], func=ACTF.Silu), reads=[b_sc], writes=[b_sc])
            for ck in range(12):
                wt, wb = wr.next()
                S.dma("sp", wt[:], w_ada[:, ck * 512:(ck + 1) * 512].rearrange("(kb p) n -> p kb n", p=128), writes=[wb])
                if ck in (4, 5, 10, 11):
                    pt, pb = PS.next()
                    for kb in range(8):
                        S.op("pe", lambda e, pt=pt, wt=wt, kb=kb: e.matmul(pt[0:NSEQ, :], sc[:, kb, :], wt[:, kb, :], start=(kb == 0), stop=(kb == 7)),
                             reads=[b_sc, wb], writes=[pb], signal=(kb == 7))
                    bt, bb = brow.next()
                    S.dma("sp", bt[:], b_ada[ck * 512:(ck + 1) * 512].partition_broadcast(NSEQ), writes=[bb])
                    gt, gb = grow.next()
                    S.op("dve", lambda e, gt=gt, pt=pt, bt=bt: e.tensor_tensor(gt[:], pt[0:NSEQ, :], bt[:], ALU.add),
                         reads=[pb, bb], writes=[gb])
                    col = (0 if ck < 6 else D) + (ck % 2) * 512
                    S.dma("pool", MODROW[:, col:col + 512], gt[:], reads=[gb])
                else:
                    for q in range(4):
                        blk = ck * 4 + q
                        pt, pb = PS.next()
                        for kb in range(8):
                            S.op("pe", lambda e, pt=pt, wt=wt, kb=kb, q=q: e.matmul(pt[:, 0:NSEQ], wt[:, kb, q * 128:(q + 1) * 128], sc[:, kb, :], start=(kb == 0), stop=(kb == 7)),
                                 reads=[b_sc, wb], writes=[pb], signal=(kb == 7))
                        one = 1.0 if ck in (2, 3, 8, 9) else 0.0
                        S.op("dve", lambda e, pt=pt, blk=blk, one=one: e.tensor_scalar(modT[:, blk, :], pt[:, 0:NSEQ], badaT[:, blk:blk + 1], one, ALU.add, ALU.add),
                             reads=[pb, b_badaT], writes=[b_modT])
            end_stage()

        def ln_phaseA(lnb, tiles):
            NT = len(tiles)
            st, stb = lnb["st"].next()
            mv, mvb = lnb["mv"].next()
            for t, (xt, xb) in enumerate(tiles):
                for i in range(2):
                    S.op("dve", lambda e, i=i, t=t, st=st, xt=xt: e.bn_stats(st[:, t, i, :], xt[:, i * 512:(i + 1) * 512]), reads=[xb], writes=[stb], disjoint=True)
                S.op("dve", lambda e, t=t, st=st, mv=mv: e.bn_aggr(mv[:, t, 0:2], st[:, t].rearrange("p a b -> p (a b)")), reads=[stb], writes=[mvb], disjoint=True)
            S.op("act", lambda e, mv=mv: e.activation(out=mv[:, 0:NT, 2:3], in_=mv[:, 0:NT, 1:2], func=ACTF.Sqrt, bias=LN_EPS), reads=[mvb], writes=[mvb])
            S.op("dve", lambda e, mv=mv: e.reciprocal(mv[:, 0:NT, 2:3], mv[:, 0:NT, 2:3]), reads=[mvb], writes=[mvb])
            S.op("dve", lambda e, mv=mv: e.scalar_tensor_tensor(mv[:, 0:NT, 3:4], mv[:, 0:NT, 0:1], -1.0, mv[:, 0:NT, 2:3], ALU.mult, ALU.mult), reads=[mvb], writes=[mvb])
            xns = []
            for t, (xt, xb) in enumerate(tiles):
                xn, xnb = lnb["xn"].next()
                S.op("act", lambda e, mv=mv, xn=xn, xt=xt, t=t: e.activation(out=xn[:], in_=xt, func=ACTF.Identity, bias=mv[:, t, 3:4], scale=mv[:, t, 2:3]),
                     reads=[xb, mvb], writes=[xnb])
                xns.append((xn, xnb))
            return xns

        def ln_phaseB(xns, dst_fn, dst_bufs, sidx, shift_blk, scale_blk):
            for t, (xn, xnb) in enumerate(xns):
                pt, pb = PS.next()
                pv = pt.bitcast(BF16)
                for kb in range(8):
                    S.op("pe", lambda e, kb=kb, pv=pv, xn=xn: e.transpose(pv[:, kb * 128:(kb + 1) * 128], xn[:, kb * 128:(kb + 1) * 128], ident[:]),
                         reads=[xnb, b_ident], writes=[pb], signal=(kb == 7))
                eng = "dve" if t % 2 == 0 else "act"
                for kb in range(8):
                    dv = dst_fn(t, kb)
                    sc_ap = modT[:, scale_blk + kb, sidx:sidx + 1]
                    sh_ap = modT[:, shift_blk + kb, sidx:sidx + 1]
                    if eng == "dve":
                        S.op("dve", lambda e, dv=dv, pv=pv, kb=kb, sc_ap=sc_ap, sh_ap=sh_ap: e.tensor_scalar(dv, pv[:, kb * 128:(kb + 1) * 128], sc_ap, sh_ap, ALU.mult, ALU.add),
                             reads=[pb, b_modT], writes=[dst_bufs[t]], disjoint=True)
                    else:
                        S.op("act", lambda e, dv=dv, pv=pv, kb=kb, sc_ap=sc_ap, sh_ap=sh_ap: e.activation(out=dv, in_=pv[:, kb * 128:(kb + 1) * 128], func=ACTF.Identity, bias=sh_ap, scale=sc_ap),
                             reads=[pb, b_modT], writes=[dst_bufs[t]], disjoint=True)

        def ln_rings(es, tag, nt, nxn):
            return dict(st=R(es, "st" + tag, [128, nt, 2, 6], F32, 2), mv=R(es, "mv" + tag, [128, nt, 4], F32, 2),
                        xn=R(es, "xn" + tag, [128, D], BF16, nxn))

        def load_w_bf16(dst, dst_b, src, K, N, stg, step, col0=0, scale_bc=None, scale_b=None):
            KBn = K // 128
            c = 0
            i = 0
            while c < N:
                n = min(step, N - c)
                stt, stb_ = stg.next()
                S.dma("sp", stt[:, 0:KBn, 0:n], src[:, c:c + n].rearrange("(kb p) n -> p kb n", p=128), writes=[stb_])
                if scale_bc is None:
                    eng = "act" if i % 2 == 0 else "dve"
                    if eng == "act":
                        S.op("act", lambda e, stt=stt, c=c, n=n: e.activation(out=dst[:, :, col0 + c:col0 + c + n], in_=stt[:, 0:KBn, 0:n], func=ACTF.Copy),
                             reads=[stb_], writes=[dst_b])
                    else:
                        S.op("dve", lambda e, stt=stt, c=c, n=n: e.tensor_copy(dst[:, :, col0 + c:col0 + c + n], stt[:, 0:KBn, 0:n]),
                             reads=[stb_], writes=[dst_b])
                else:
                    for kb in range(KBn):
                        eng = "dve" if kb % 2 == 0 else "pool"
                        S.op(eng, lambda e, stt=stt, c=c, n=n, kb=kb: e.tensor_tensor(dst[:, kb, col0 + c:col0 + c + n], stt[:, kb, 0:n], scale_bc[:, c:c + n], ALU.mult),
                             reads=[stb_, scale_b], writes=[dst_b])
                c += n
                i += 1

        def sub_stage(bufs_before):
            S.barrier()
            S.emit()
            S.release(stage_bufs[bufs_before:])
            del stage_bufs[bufs_before:]

        for L in Ls:
            KB = L // 128
            tab = TAB[L]
            with ExitStack() as es:
                HSUM, b_HSUM = T(es, "HSUM", [128, KB, 2, 512], BF16)
                HDIF, b_HDIF = T(es, "HDIF", [128, KB, 2, 512], BF16)
                w3s, b_w3s = T(es, "w3s", [64, 2048])
                hd2, b_hd2 = T(es, "hd2", [64, L])
                skr, b_skr = T(es, "skr", [1, 2, 512])
                alt_f, b_altf = T(es, "alt_f", [128, 2])
                alt, b_alt = T(es, "alt", [128, 2], BF16)
                S.dma("sp", w3s[:], f_w3, writes=[b_w3s])
                S.dma("sp", skr[:], hy_skip.rearrange("(a o) c -> a o c", a=1), writes=[b_skr])
                S.dma("sp", alt_f[:], altc, writes=[b_altf])
                S.op("dve", lambda e: e.tensor_copy(alt[:], alt_f[:]), reads=[b_altf], writes=[b_alt])
                nb0 = len(stage_bufs)
                with ExitStack() as es2:
                    zT, b_zT = T(es2, "zT", [33, L])
                    w1s, b_w1s = T(es2, "w1s", [33, 64])
                    w2s, b_w2s = T(es2, "w2s", [64, 64])
                    fpar, b_fpar = T(es2, "fpar", [64, 8])
                    hd1, b_hd1 = T(es2, "hd1", [64, L])
                    argr = R(es2, "argr", [64, 512], F32, 2)
                    kr = R(es2, "kr", [64, 512], F32, 2)
                    S.dma("sp", zT[:], tab["zT"], writes=[b_zT])
                    S.dma("sp", w1s[:], f_w1, writes=[b_w1s])
                    S.dma("sp", w2s[:], f_w2, writes=[b_w2s])
                    S.dma("sp", fpar[:, 0:1], f_b1, writes=[b_fpar])
                    S.dma("sp", fpar[:, 1:2], f_fr, writes=[b_fpar])
                    S.dma("sp", fpar[:, 2:3], f_b2, writes=[b_fpar])
                    S.op("dve", lambda e: e.tensor_tensor(fpar[:, 3:4], fpar[:, 0:1], fpar[:, 1:2], ALU.mult), reads=[b_fpar], writes=[b_fpar])
                    S.op("dve", lambda e: e.tensor_tensor(fpar[:, 4:5], fpar[:, 2:3], fpar[:, 1:2], ALU.mult), reads=[b_fpar], writes=[b_fpar])
                    for layer in range(2):
                        src, srcb = (zT, b_zT) if layer == 0 else (hd1, b_hd1)
                        wl, wlb = (w1s, b_w1s) if layer == 0 else (w2s, b_w2s)
                        dst, dstb = (hd1, b_hd1) if layer == 0 else (hd2, b_hd2)
                        kin = 33 if layer == 0 else 64
                        fb_col = 3 if layer == 0 else 4
                        for ck in range(L // 512):
                            pt, pb = PS.next()
                            S.op("pe", lambda e, pt=pt, wl=wl, src=src, ck=ck, kin=kin: e.matmul(pt[0:64, :], wl[0:kin, :], src[0:kin, ck * 512:(ck + 1) * 512], start=True, stop=True),
                                 reads=[wlb, srcb], writes=[pb])
                            at, ab = argr.next()
                            kt, kb_ = kr.next()
                            S.op("dve", lambda e, at=at, pt=pt, fb_col=fb_col: e.tensor_scalar(at[:], pt[0:64, :], fpar[:, 1:2], fpar[:, fb_col:fb_col + 1], ALU.mult, ALU.add),
                                 reads=[pb, b_fpar], writes=[ab])
                            S.op("dve", lambda e, at=at, kt=kt: e.tensor_scalar(kt[:], at[:], 1.0 / TWO_PI, MAGIC, ALU.mult, ALU.add), reads=[ab], writes=[kb_])
                            S.op("dve", lambda e, kt=kt: e.tensor_scalar(kt[:], kt[:], -MAGIC, -TWO_PI, ALU.add, ALU.mult), reads=[kb_], writes=[kb_])
                            S.op("dve", lambda e, at=at, kt=kt: e.tensor_tensor(at[:], at[:], kt[:], ALU.add), reads=[ab, kb_], writes=[ab])
                            S.op("act", lambda e, at=at, dst=dst, ck=ck: e.activation(out=dst[:, ck * 512:(ck + 1) * 512], in_=at[:], func=ACTF.Sin),
                                 reads=[ab], writes=[dstb])
                    sub_stage(nb0)
                with ExitStack() as es2:
                    decr = R(es2, "decr", [128, 2, 512], F32, 2)
                    hfr = R(es2, "hfr", [128, 4, 512], F32, 2)
                    for tb in range(KB):
                        dt_, db_ = decr.next()
                        S.dma("sp", dt_[:, 0, :], tab["decF"][tb * 128:(tb + 1) * 128, :], writes=[db_])
                        S.dma("sp", dt_[:, 1, :], tab["decB"][tb * 128:(tb + 1) * 128, :], writes=[db_], disjoint=True)
                        ht, hb = hfr.next()
                        for q in range(4):
                            pt, pb = PS.next()
                            S.op("pe", lambda e, pt=pt, tb=tb, q=q: e.matmul(pt[:], hd2[:, tb * 128:(tb + 1) * 128], w3s[:, q * 512:(q + 1) * 512], start=True, stop=True),
                                 reads=[b_hd2, b_w3s], writes=[pb])
                            S.op("dve", lambda e, ht=ht, pt=pt, dt_=dt_, q=q: e.tensor_tensor(ht[:, q, :], pt[:], dt_[:, q % 2, :], ALU.mult),
                                 reads=[pb, db_], writes=[hb])
                        if tb == 0:
                            for o in range(2):
                                S.op("dve", lambda e, ht=ht, o=o: e.memset(ht[0:1, 2 * o + 1, :], 0.0), writes=[hb])
                                S.op("dve", lambda e, ht=ht, o=o: e.tensor_tensor(ht[0:1, 2 * o, :], ht[0:1, 2 * o, :], skr[0:1, o, :], ALU.add),
                                     reads=[hb, b_skr], writes=[hb])
                        for o in range(2):
                            S.op("dve", lambda e, ht=ht, o=o, tb=tb: e.tensor_tensor(HSUM[:, tb, o, :], ht[:, 2 * o, :], ht[:, 2 * o + 1, :], ALU.add),
                                 reads=[hb], writes=[b_HSUM])
                            S.op("pool", lambda e, ht=ht, o=o, tb=tb: e.tensor_tensor(HDIF[:, tb, o, :], ht[:, 2 * o, :], ht[:, 2 * o + 1, :], ALU.subtract),
                                 reads=[hb], writes=[b_HDIF])
                    sub_stage(nb0)
                slabC = R(es, "slabC", [128, KB, 128], BF16, 2)
                slabS = R(es, "slabS", [128, KB, 128], BF16, 2)
                outr = R(es, "outr", [128, 512], F32, 4)
                sc_all = 1.0 / L
                for j in range(KB):
                    ct, cb_ = slabC.next()
                    stt, sb_ = slabS.next()
                    S.dma("sp", ct[:], tab["TC"][j], writes=[cb_])
                    S.dma("sp", stt[:], tab["TSF"][j], writes=[sb_])
                    for o in range(2):
                        pP, pPb = PS.next()
                        for kb in range(KB):
                            S.op("pe", lambda e, pP=pP, ct=ct, kb=kb, o=o: e.matmul(pP[:], ct[:, kb, :], HSUM[:, kb, o, :], start=(kb == 0), stop=(kb == KB - 1)),
                                 reads=[cb_, b_HSUM], writes=[pPb], signal=(kb == KB - 1))
                        pQ, pQb = PS.next()
                        for kb in range(KB):
                            S.op("pe", lambda e, pQ=pQ, stt=stt, kb=kb, o=o: e.matmul(pQ[:], stt[:, kb, :], HDIF[:, kb, o, :], start=(kb == 0), stop=(kb == KB - 1)),
                                 reads=[sb_, b_HDIF], writes=[pQb], signal=(kb == KB - 1))
                        oP, oPb = outr.next()
                        oQ, oQb = outr.next()
                        S.op("act", lambda e, oP=oP, pP=pP: e.activation(out=oP[:], in_=pP[:], func=ACTF.Copy, scale=sc_all), reads=[pPb], writes=[oPb])
                        S.op("dve", lambda e, oQ=oQ, pQ=pQ: e.tensor_scalar_mul(oQ[:], pQ[:], sc_all), reads=[pQb], writes=[oQb])
                        if j == 0:
                            pN, pNb = PS.next()
                            for kb in range(KB):
                                S.op("pe", lambda e, pN=pN, kb=kb, o=o: e.matmul(pN[0:1, :], alt[:, 0:1], HSUM[:, kb, o, :], start=(kb == 0), stop=(kb == KB - 1)),
                                     reads=[b_alt, b_HSUM], writes=[pNb], signal=(kb == KB - 1))
                            S.op("dve", lambda e, oP=oP: e.tensor_scalar_mul(oP[0:1, :], oP[0:1, :], 0.5), reads=[oPb], writes=[oPb])
                            S.op("dve", lambda e, oQ=oQ, pN=pN: e.tensor_scalar_mul(oQ[0:1, :], pN[0:1, :], 0.5 * sc_all), reads=[pNb, oQb], writes=[oQb])
                        S.dma("pool", tab["HS"][o, j], oP[:], reads=[oPb])
                        S.dma("pool", tab["HS"][o, KB + j], oQ[:], reads=[oQb])
                end_stage()

        all_chunks = [(si, sq, ck) for si, sq in enumerate(seqs) for ck in range(sq["L"] // 512)]
        with ExitStack() as es:
            Wa, b_Wa = T(es, "Wa", [128, 8, 1024], BF16)
            Wg, b_Wg = T(es, "Wg", [128, 8, 2048], BF16)
            nb0 = len(stage_bufs)
            with ExitStack() as es2:
                stg = R(es2, "stg", [128, 8, 256], F32, 3)
                load_w_bf16(Wa, b_Wa, w_in[:, 0:1024], D, 1024, stg, 256)
                load_w_bf16(Wg, b_Wg, w_in[:, 2560:4608], D, 2048, stg, 256)
                sub_stage(nb0)
            xr = R(es, "xr", [128, D], F32, 6)
            lnb = ln_rings(es, "a1", 4, 8)
            hTt = [T(es, "hTc%d" % i, [128, 8, 512], BF16)[0] for i in range(2)]
            hTb = [[Buf("hTc%d_%d" % (i, t)) for t in range(4)] for i in range(2)]
            sgr = R(es, "sgr", [128, 512], F32, 2)
            abuf = R(es, "abuf", [128, 4, 512], BF16, 2)
            gbuf = R(es, "gbuf", [128, 16, 512], BF16, 2)

            def a1_phaseA(n):
                si, sq, ck = all_chunks[n]
                tiles = []
                for tt in range(4):
                    xt, xb = xr.next()
                    S.dma("sp", xt[:], xrows(sq, ck * 512 + tt * 128, 128), writes=[xb])
                    tiles.append((xt[:], xb))
                return ln_phaseA(lnb, tiles)

            def a1_phaseB(n, xns):
                si, sq, ck = all_chunks[n]
                hTc = hTt[n % 2]
                ln_phaseB(xns, lambda t, kb, hTc=hTc: hTc[:, kb, t * 128:(t + 1) * 128], hTb[n % 2], si, 0, 8)

            a1_phaseB(0, a1_phaseA(0))
            for n, (si, sq, ck) in enumerate(all_chunks):
                c0 = ck * 512
                g0 = sq["tok0"] + c0
                hTc = hTt[n % 2]
                hb_ = hTb[n % 2]
                at, ab = abuf.next()
                for cb in range(4):
                    pv_, pvb = PS.next()
                    pg_, pgb = PS.next()
                    for kb in range(8):
                        S.op("pe", lambda e, pv_=pv_, kb=kb, cb=cb, hTc=hTc: e.matmul(pv_[:], Wa[:, kb, cb * 128:(cb + 1) * 128], hTc[:, kb, :], start=(kb == 0), stop=(kb == 7)),
                             reads=[b_Wa] + hb_, writes=[pvb], signal=(kb == 7))
                    for kb in range(8):
                        S.op("pe", lambda e, pg_=pg_, kb=kb, cb=cb, hTc=hTc: e.matmul(pg_[:], Wa[:, kb, 512 + cb * 128:512 + (cb + 1) * 128], hTc[:, kb, :], start=(kb == 0), stop=(kb == 7)),
                             reads=[b_Wa] + hb_, writes=[pgb], signal=(kb == 7))
                    sg, sgb = sgr.next()
                    S.op("act", lambda e, sg=sg, pg_=pg_: e.activation(out=sg[:], in_=pg_[:], func=ACTF.Sigmoid), reads=[pgb], writes=[sgb])
                    S.op("dve", lambda e, at=at, cb=cb, pv_=pv_, sg=sg: e.tensor_tensor(at[:, cb, :], pv_[:], sg[:], ALU.mult), reads=[pvb, sgb], writes=[ab], disjoint=True)
                S.dma("pool", A_s[:, g0:g0 + 512].rearrange("(cb p) t -> p cb t", p=128), at[:], reads=[ab])
                xns = a1_phaseA(n + 1) if n + 1 < len(all_chunks) else None
                gt, gb = gbuf.next()
                for gi in range(16):
                    if gi == 8 and xns is not None:
                        a1_phaseB(n + 1, xns)
                    pt, pb = PS.next()
                    for kb in range(8):
                        S.op("pe", lambda e, pt=pt, kb=kb, gi=gi, hTc=hTc: e.matmul(pt[:], Wg[:, kb, gi * 128:(gi + 1) * 128], hTc[:, kb, :], start=(kb == 0), stop=(kb == 7)),
                             reads=[b_Wg] + hb_, writes=[pb], signal=(kb == 7))
                    S.op("act", lambda e, pt=pt, gi=gi, gt=gt: e.activation(out=gt[:, gi, :], in_=pt[:], func=ACTF.Sigmoid), reads=[pb], writes=[gb], disjoint=True)
                S.dma("pool", G_s[:, g0:g0 + 512].rearrange("(gi p) t -> p gi t", p=128), gt[:], reads=[gb])
            end_stage()

        with ExitStack() as es:
            LMAX = max(sq["L"] for sq in seqs)
            hT, _ = T(es, "hT", [128, 8, LMAX + 2], BF16)
            hTtb = [Buf("hTt%d" % t) for t in range(LMAX // 128)]
            b_halo = Buf("halo")
            Why, b_Why = T(es, "Why", [128, 8, 1536], BF16)
            nb0 = len(stage_bufs)
            with ExitStack() as es2:
                stg = R(es2, "stg2", [128, 8, 256], F32, 3)
                load_w_bf16(Why, b_Why, w_in[:, 1024:2560], D, 1536, stg, 256)
                sub_stage(nb0)
            hswT, b_hswT = T(es, "hswT", [128, 12, 3])
            hsbT, b_hsbT = T(es, "hsbT", [128, 12])
            xr = R(es, "xr2", [128, D], F32, 4)
            lnb = ln_rings(es, "a2", 4, 8)
            halr = R(es, "halr", [128, 12, 2], F32, 2)
            ber = R(es, "ber", [128, 2, 12], F32, 2)
            u32 = R(es, "u32", [128, 512], F32, 3)
            ubf = R(es, "ubf", [128, 512], BF16, 4)
            obuf = R(es, "obufA", [128, 4, 512], BF16, 3)
            S.dma("sp", hswT[:], hs_wT, writes=[b_hswT])
            S.dma("sp", hsbT[:], hs_bT, writes=[b_hsbT])
            for si, sq in enumerate(seqs):
                L = sq["L"]
                NTL = L // 128

                def a2_phaseA(tl, sq=sq):
                    tiles = []
                    for t in tl:
                        xt, xb = xr.next()
                        S.dma("sp", xt[:], xrows(sq, t * 128, 128), writes=[xb])
                        tiles.append((xt[:], xb))
                    return ln_phaseA(lnb, tiles)

                def a2_phaseB(tl, xns, si=si):
                    ln_phaseB(xns, lambda t, kb, tl=tl: hT[:, kb, 1 + tl[t] * 128: 1 + (tl[t] + 1) * 128], [hTtb[t] for t in tl], si, 0, 8)

                S.op("pool", lambda e: e.memset(hT[:, :, 0:1], 0.0), writes=[b_halo])
                S.op("pool", lambda e, L=L: e.memset(hT[:, :, L + 1:L + 2], 0.0), writes=[b_halo], disjoint=True)
                groups = [[0]] + [[t for t in range(4 * g + 1, 4 * g + 5) if t < NTL] for g in range(L // 512)]
                groups = [g for g in groups if g]
                a2_phaseB(groups[0], a2_phaseA(groups[0]))
                a2_phaseB(groups[1], a2_phaseA(groups[1]))
                for ck in range(L // 512):
                    c0 = ck * 512
                    g0 = sq["tok0"] + c0
                    nxt = groups[ck + 2] if ck + 2 < len(groups) else None
                    xns = a2_phaseA(nxt) if nxt else None
                    tb0 = c0 // 128
                    rd_main = [hTtb[t] for t in range(tb0, tb0 + 4)] + [b_Why]
                    rd_halo = [hTtb[t] for t in (tb0 - 1, tb0 + 4) if 0 <= t < NTL] + [b_halo, b_Why]
                    ph, phb = PS.next()
                    for blk in range(12):
                        for kb in range(8):
                            S.op("pe", lambda e, ph=ph, blk=blk, kb=kb, c0=c0: e.matmul(ph[:, blk * 2:blk * 2 + 2], Why[:, kb, blk * 128:(blk + 1) * 128], hT[:, kb, c0:c0 + 514:513],
                                                                                    start=(kb == 0), stop=(kb == 7)),
                                 reads=rd_halo, writes=[phb], signal=(blk == 11 and kb == 7))
                    hl, hlb = halr.next()
                    S.op("dve", lambda e, hl=hl, ph=ph: e.tensor_copy(hl[:].rearrange("p a b -> p (a b)"), ph[:, 0:24]), reads=[phb], writes=[hlb])
                    be, beb = ber.next()
                    for side, tap in ((0, 0), (1, 2)):
                        S.op("dve", lambda e, be=be, hl=hl, side=side, tap=tap: e.tensor_tensor(be[:, side, :], hl[:, :, side], hswT[:, :, tap], ALU.mult),
                             reads=[hlb, b_hswT], writes=[beb], disjoint=(side > 0))
                        S.op("dve", lambda e, be=be, side=side: e.tensor_tensor(be[:, side, :], be[:, side, :], hsbT[:], ALU.add),
                             reads=[beb, b_hsbT], writes=[beb])
                    pending = [None]

                    def flush():
                        if pending[0] is None:
                            return
                        which_, cb_, dt_, db_, pT_, ot_, ob_ = pending[0]
                        pending[0] = None
                        for tt in range(4):
                            pt_, ptb_ = pT_[tt // 2]
                            pv = pt_.bitcast(BF16)
                            S.op("pe", lambda e, pv=pv, tt=tt, cb_=cb_, dt_=dt_: e.transpose(pv[:, (tt % 2) * 512 + cb_ * 128:(tt % 2) * 512 + (cb_ + 1) * 128], dt_[:, tt * 128:(tt + 1) * 128], ident[:]),
                                 reads=[db_, b_ident], writes=[ptb_])
                        if cb_ == 3:
                            for h2 in range(2):
                                pt_, ptb_ = pT_[h2]
                                pv = pt_.bitcast(BF16)
                                S.op("act", lambda e, ot_=ot_, pv=pv, h2=h2: e.activation(out=ot_[:, 2 * h2:2 * h2 + 2, :].rearrange("p a b -> p (a b)"), in_=pv[:, 0:1024], func=ACTF.Copy),
                                     reads=[ptb_], writes=[ob_], disjoint=True)
                            dst = (V_s, X1_s, X2_s)[which_]
                            S.dma("pool", dst[g0:g0 + 512, :].rearrange("(tt p) c -> p tt c", p=128), ot_[:], reads=[ob_])
                            if which_ == 1 and xns is not None:
                                a2_phaseB(nxt, xns)

                    for which in range(3):
                        ot, ob = obuf.next()
                        pT = None
                        for cb in range(4):
                            blk = which * 4 + cb
                            pm, pmb = PS.next()
                            for kb in range(8):
                                S.op("pe", lambda e, pm=pm, blk=blk, kb=kb, c0=c0: e.matmul(pm[:], Why[:, kb, blk * 128:(blk + 1) * 128], hT[:, kb, 1 + c0:1 + c0 + 512], start=(kb == 0), stop=(kb == 7)),
                                     reads=rd_main, writes=[pmb], signal=(kb == 7))
                            flush()
                            if cb == 0:
                                pT = [PS.next() for _ in range(2)]
                            u, ub = u32.next()
                            w0 = hswT[:, blk, 0:1]
                            w1 = hswT[:, blk, 1:2]
                            w2 = hswT[:, blk, 2:3]
                            S.op("act", lambda e, u=u, pm=pm, w1=w1, blk=blk: e.activation(out=u[:, 1:511], in_=pm[:, 1:511], func=ACTF.Identity, bias=hsbT[:, blk:blk + 1], scale=w1),
                                 reads=[pmb, b_hswT, b_hsbT], writes=[ub])
                            S.op("act", lambda e, u=u, pm=pm, w1=w1, blk=blk, be=be: e.activation(out=u[:, 0:1], in_=pm[:, 0:1], func=ACTF.Identity, bias=be[:, 0, blk:blk + 1], scale=w1),
                                 reads=[pmb, b_hswT, beb], writes=[ub], disjoint=True)
                            S.op("act", lambda e, u=u, pm=pm, w1=w1, blk=blk, be=be: e.activation(out=u[:, 511:512], in_=pm[:, 511:512], func=ACTF.Identity, bias=be[:, 1, blk:blk + 1], scale=w1),
                                 reads=[pmb, b_hswT, beb], writes=[ub], disjoint=True)
                            S.op("dve", lambda e, u=u, pm=pm, w0=w0: e.scalar_tensor_tensor(u[:, 1:512], pm[:, 0:511], w0, u[:, 1:512], ALU.mult, ALU.add),
                                 reads=[pmb, ub, b_hswT], writes=[ub])
                            dt_, db_ = ubf.next()
                            dv = dt_
                            S.op("dve", lambda e, u=u, pm=pm, w2=w2, dv=dv: e.scalar_tensor_tensor(dv[:, 0:511], pm[:, 1:512], w2, u[:, 0:511], ALU.mult, ALU.add),
                                 reads=[pmb, ub, b_hswT], writes=[db_])
                            S.op("dve", lambda e, u=u, dv=dv: e.tensor_copy(dv[:, 511:512], u[:, 511:512]), reads=[ub], writes=[db_], disjoint=True)
                            pending[0] = (which, cb, dt_, db_, pT, ot, ob)
                    flush()
            end_stage()

        with ExitStack() as es:
            LMAX = max(sq["L"] for sq in seqs)
            KBM = LMAX // 128
            vb, b_vb = T(es, "vb", [128, KBM, 512], BF16)
            Yb, b_Yb = T(es, "Yb", [128, 2 * KBM, 512], BF16)
            slabC = R(es, "slabCb", [128, KBM, 128], BF16, 2)
            slabS = R(es, "slabSb", [128, KBM, 128], BF16, 2)
            hsr = R(es, "hsr", [128, 2, 512], F32, 2)
            abr = R(es, "abr", [128, 2, 512], F32, 2)
            tmr = R(es, "tmr", [128, 4, 512], F32, 2)
            x1r = R(es, "x1r", [128, 512], BF16, 3)
            ztk = R(es, "ztk", [128, 512], BF16, 3)
            zbr = R(es, "zbr", [128, 4, 512], BF16, 2)
            for si, sq in enumerate(seqs):
                L = sq["L"]
                KB = L // 128
                tab = TAB[L]
                tok0 = sq["tok0"]
                S.dma("sp", vb[:, 0:KB, :], V_s[tok0:tok0 + L, :].rearrange("(kb p) c -> p kb c", p=128), writes=[b_vb])
                for order in range(2):
                    for j in range(KB):
                        ct, cb_ = slabC.next()
                        stt, sb_ = slabS.next()
                        S.dma("sp", ct[:, 0:KB, :], tab["TC"][j], writes=[cb_])
                        S.dma("sp", stt[:, 0:KB, :], tab["TSF"][j], writes=[sb_])
                        ht, hb = hsr.next()
                        S.dma("sp", ht[:, 0, :], tab["HS"][order, j], writes=[hb])
                        S.dma("sp", ht[:, 1, :], tab["HS"][order, KB + j], writes=[hb], disjoint=True)
                        pA, pAb = PS.next()
                        for kb in range(KB):
                            S.op("pe", lambda e, pA=pA, ct=ct, kb=kb: e.matmul(pA[:], ct[:, kb, :], vb[:, kb, :], start=(kb == 0), stop=(kb == KB - 1)),
                                 reads=[cb_, b_vb], writes=[pAb], signal=(kb == KB - 1))
                        pB, pBb = PS.next()
                        for kb in range(KB):
                            S.op("pe", lambda e, pB=pB, stt=stt, kb=kb: e.matmul(pB[:], stt[:, kb, :], vb[:, kb, :], start=(kb == 0), stop=(kb == KB - 1)),
                                 reads=[sb_, b_vb], writes=[pBb], signal=(kb == KB - 1))
                        ab_t, ab_b = abr.next()
                        S.op("act", lambda e, ab_t=ab_t, pA=pA: e.activation(out=ab_t[:, 0, :], in_=pA[:], func=ACTF.Copy), reads=[pAb], writes=[ab_b])
                        S.op("act", lambda e, ab_t=ab_t, pB=pB: e.activation(out=ab_t[:, 1, :], in_=pB[:], func=ACTF.Copy), reads=[pBb], writes=[ab_b])
                        tm, tmb = tmr.next()
                        S.op("dve", lambda e, tm=tm, ab_t=ab_t, ht=ht: e.tensor_tensor(tm[:, 0, :], ab_t[:, 0, :], ht[:, 0, :], ALU.mult), reads=[ab_b, hb], writes=[tmb])
                        S.op("pool", lambda e, tm=tm, ab_t=ab_t, ht=ht: e.tensor_tensor(tm[:, 1, :], ab_t[:, 1, :], ht[:, 1, :], ALU.mult), reads=[ab_b, hb], writes=[tmb])
                        S.op("pool", lambda e, tm=tm, ab_t=ab_t, ht=ht: e.tensor_tensor(tm[:, 2, :], ab_t[:, 0, :], ht[:, 1, :], ALU.mult), reads=[ab_b, hb], writes=[tmb])
                        S.op("dve", lambda e, tm=tm, ab_t=ab_t, ht=ht: e.tensor_tensor(tm[:, 3, :], ab_t[:, 1, :], ht[:, 0, :], ALU.mult), reads=[ab_b, hb], writes=[tmb])
                        S.op("dve", lambda e, tm=tm, j=j: e.tensor_tensor(Yb[:, j, :], tm[:, 0, :], tm[:, 1, :], ALU.subtract), reads=[tmb], writes=[b_Yb])
                        S.op("pool", lambda e, tm=tm, j=j, KB=KB: e.tensor_tensor(Yb[:, KB + j, :], tm[:, 2, :], tm[:, 3, :], ALU.add), reads=[tmb], writes=[b_Yb])
                        if j == 0:
                            S.op("dve", lambda e, tm=tm: e.tensor_copy(Yb[0:1, 0, :], tm[0:1, 0, :]), reads=[tmb], writes=[b_Yb])
                            S.op("dve", lambda e, tm=tm, KB=KB: e.tensor_copy(Yb[0:1, KB, :], tm[0:1, 1, :]), reads=[tmb], writes=[b_Yb])
                    if order == 0:
                        for tb in range(KB):
                            ct, cb_ = slabC.next()
                            stt, sb_ = slabS.next()
                            S.dma("sp", ct[:, 0:KB, :], tab["TC"][tb], writes=[cb_])
                            S.dma("sp", stt[:, 0:KB, :], tab["TSI"][tb], writes=[sb_])
                            x1t, x1b = x1r.next()
                            S.dma("sp", x1t[:], X1_s[tok0 + tb * 128: tok0 + (tb + 1) * 128, :], writes=[x1b])
                            py, pyb = PS.next()
                            for fb in range(KB):
                                S.op("pe", lambda e, py=py, ct=ct, fb=fb: e.matmul(py[:], ct[:, fb, :], Yb[:, fb, :], start=(fb == 0), stop=False),
                                     reads=[cb_, b_Yb], writes=[pyb], signal=False)
                            for fb in range(KB):
                                S.op("pe", lambda e, py=py, stt=stt, fb=fb, KB=KB: e.matmul(py[:], stt[:, fb, :], Yb[:, KB + fb, :], start=False, stop=(fb == KB - 1)),
                                     reads=[sb_, b_Yb], writes=[pyb], signal=(fb == KB - 1))
                            S.op("dve", lambda e, py=py, x1t=x1t, tb=tb: e.tensor_tensor(vb[:, tb, :], py[:], x1t[:], ALU.mult), reads=[pyb, x1b], writes=[b_vb])
                    else:
                        pend = [None]

                        def flush_t(tok0=tok0):
                            if pend[0] is None:
                                return
                            tb_, zk, zkb, zt_, ztb_ = pend[0]
                            pend[0] = None
                            pz, pzb = PS.next()
                            pzv = pz.bitcast(BF16)
                            for cb in range(4):
                                S.op("pe", lambda e, pzv=pzv, cb=cb, zk=zk: e.transpose(pzv[:, cb * 128:(cb + 1) * 128], zk[:, cb * 128:(cb + 1) * 128], ident[:]),
                                     reads=[zkb, b_ident], writes=[pzb], signal=(cb == 3))
                            q = tb_ % 4
                            S.op("act", lambda e, pzv=pzv, zt_=zt_, q=q: e.activation(out=zt_[:, :, q * 128:(q + 1) * 128], in_=pzv[:, 0:512].rearrange("p (a b) -> p a b", a=4), func=ACTF.Copy),
                                 reads=[pzb], writes=[ztb_], disjoint=True)
                            if q == 3:
                                g0 = tok0 + (tb_ // 4) * 512
                                S.dma("pool", ZB_s[:, g0:g0 + 512].rearrange("(cb p) t -> p cb t", p=128), zt_[:], reads=[ztb_])

                        zt, zb_ = None, None
                        for tb in range(KB):
                            if tb % 4 == 0:
                                zt, zb_ = zbr.next()
                            ct, cb_ = slabC.next()
                            stt, sb_ = slabS.next()
                            S.dma("sp", ct[:, 0:KB, :], tab["TC"][tb], writes=[cb_])
                            S.dma("sp", stt[:, 0:KB, :], tab["TSI"][tb], writes=[sb_])
                            x2t, x2b = x1r.next()
                            S.dma("sp", x2t[:], X2_s[tok0 + tb * 128: tok0 + (tb + 1) * 128, :], writes=[x2b])
                            py, pyb = PS.next()
                            for fb in range(KB):
                                S.op("pe", lambda e, py=py, ct=ct, fb=fb: e.matmul(py[:], ct[:, fb, :], Yb[:, fb, :], start=(fb == 0), stop=False),
                                     reads=[cb_, b_Yb], writes=[pyb], signal=False)
                            for fb in range(KB):
                                S.op("pe", lambda e, py=py, stt=stt, fb=fb, KB=KB: e.matmul(py[:], stt[:, fb, :], Yb[:, KB + fb, :], start=False, stop=(fb == KB - 1)),
                                     reads=[sb_, b_Yb], writes=[pyb], signal=(fb == KB - 1))
                            flush_t()
                            zk, zkb = ztk.next()
                            S.op("dve", lambda e, py=py, x2t=x2t, zk=zk: e.tensor_tensor(zk[:], py[:], x2t[:], ALU.mult), reads=[pyb, x2b], writes=[zkb])
                            pend[0] = (tb, zk, zkb, zt, zb_)
                        flush_t()
            end_stage()

        TC_ = 256
        NTT = TC_ // 128
        c_chunks = [(si, sq, ck) for si, sq in enumerate(seqs) for ck in range(sq["L"] // TC_)]
        with ExitStack() as es:
            Wpw, b_Wpw = T(es, "Wpw", [128, 4, D], BF16)
            Whyo, b_Whyo = T(es, "Whyo", [128, 4, D], BF16)
            Wout, b_Wout = T(es, "Wout", [128, 8, D], BF16)
            Lw, b_Lw = T(es, "Lw", [128, 16, 8, 32], BF16)
            nb0 = len(stage_bufs)
            with ExitStack() as es2:
                stg = R(es2, "stgc", [128, 8, 256], F32, 3)
                wsel, b_wsel = T(es2, "wsel", [128, 16, 8])
                E4, b_E4 = T(es2, "E4", [128, 32])
                S.dma("sp", wsel[:], dw_pk, writes=[b_wsel])
                S.op("dve", lambda e: e.tensor_tensor(E4[:], ident_f[:, 0:32], ident_f[:, 32:64], ALU.add), reads=[b_identf], writes=[b_E4])
                S.op("dve", lambda e: e.tensor_tensor(E4[:], E4[:], ident_f[:, 64:96], ALU.add), reads=[b_identf, b_E4], writes=[b_E4])
                S.op("dve", lambda e: e.tensor_tensor(E4[:], E4[:], ident_f[:, 96:128], ALU.add), reads=[b_identf, b_E4], writes=[b_E4])
                load_w_bf16(Wpw, b_Wpw, pw_w, CH, D, stg, 256)
                load_w_bf16(Whyo, b_Whyo, hyo_w, CH, D, stg, 256)
                load_w_bf16(Wout, b_Wout, w_out, D, D, stg, 256)
                for cg in range(16):
                    for g in range(8):
                        eng = "dve" if (cg * 8 + g) % 2 == 0 else "pool"
                        S.op(eng, lambda e, cg=cg, g=g: e.tensor_scalar_mul(Lw[:, cg, g, :], E4[:], wsel[:, cg, g:g + 1]),
                             reads=[b_E4, b_wsel], writes=[b_Lw], disjoint=True)
                sub_stage(nb0)
            cpar, b_cpar = T(es, "cpar", [128, 3, 4])
            g1bc, b_g1bc = T(es, "g1bc", [128, D])
            l1g, b_l1g = T(es, "l1g", [128, D])
            l1b, b_l1b = T(es, "l1b", [128, D])
            xr = R(es, "xrc", [128, D], F32, 3)
            lnb = dict(st=R(es, "stc", [128, 2, 6], F32, 2), mv=R(es, "mvc", [128, 4], F32, 2))
            sgl = R(es, "sgl", [128, 16, TC_], BF16, 2)
            ah = R(es, "ah", [128, 16, TC_ + 30], BF16, 3)
            zbl = R(es, "zbl", [128, 4, TC_], BF16, 2)
            acvr = R(es, "acv", [128, 4, TC_], F32, 2)
            asqr = R(es, "asq", [128, 4, TC_], F32, 2)
            acv_bufs = [[Buf("acv%d_%d" % (i, c)) for c in range(4)] for i in range(2)]
            asq_bufs = [[Buf("asq%d_%d" % (i, c)) for c in range(4)] for i in range(2)]
            b_an4 = [Buf("an%d" % c) for c in range(4)]
            b_mt8 = [Buf("mt%d" % c) for c in range(8)]
            stt_, b_stt = T(es, "stats", [128, 4, TC_])
            an, b_an = T(es, "an", [128, 4, TC_], BF16)
            mt, b_mt = T(es, "mt", [128, 8, TC_], BF16)
            tmp1 = R(es, "tmp1", [128, TC_], F32, 2)
            tmp2 = R(es, "tmp2", [128, TC_], F32, 2)
            rr = R(es, "rr", [128, D], F32, 4)
            x1o = R(es, "x1o", [128, D], F32, 2)
            S.dma("sp", cpar[:, 0, :], dw_bT, writes=[b_cpar])
            S.dma("sp", cpar[:, 1, :], cln_gT, writes=[b_cpar], disjoint=True)
            S.dma("sp", cpar[:, 2, :], cln_bT, writes=[b_cpar], disjoint=True)
            S.dma("sp", l1g[:], ln1_g.partition_broadcast(128), writes=[b_l1g])
            S.dma("sp", l1b[:], ln1_b.partition_broadcast(128), writes=[b_l1b])

            loaded = {}

            def c1_load(n):
                si, sq, ck = c_chunks[n]
                L = sq["L"]
                tok0 = sq["tok0"]
                c0 = ck * TC_
                at, ab = ah.next()
                W_ = TC_ + 28
                edge = (c0 - 15 < 0) or (c0 - 15 + 3 + W_ > L)
                if edge:
                    S.op("pool", lambda e, at=at: e.memset(at[:], 0.0), writes=[ab])
                for j in range(4):
                    s0 = c0 - 15 + j
                    lo = max(s0, 0)
                    hi = min(s0 + W_, L)
                    S.dma("sp", at[32 * j:32 * (j + 1), :, lo - s0: hi - s0], A_s[:, tok0 + lo: tok0 + hi].rearrange("(cg c) t -> c cg t", c=32),
                          writes=[ab], disjoint=(not edge))
                loaded[n] = (at, ab)

            def c1_conv(n):
                at, ab = loaded.pop(n)
                acv, _ = acvr.next()
                asq, _ = asqr.next()
                k_ = (acvr.i - 1) % 2
                b_acv = acv_bufs[k_]
                b_asq = asq_bufs[k_]
                for cb in range(4):
                    pt, pb = PS.next()
                    for g in range(8):
                        for i in range(4):
                            cg = cb * 4 + i
                            S.op("pe", lambda e, pt=pt, cg=cg, g=g, i=i, at=at: e.matmul(pt[32 * i:32 * (i + 1), 0:TC_], Lw[:, cg, g, :], at[:, cg, 4 * g:4 * g + TC_],
                                                                                    start=(g == 0), stop=(g == 7), skip_group_check=True, tile_position=(0, 32 * i)),
                                 reads=[b_Lw, ab], writes=[pb], signal=(g == 7 and i == 3))
                    S.op("act", lambda e, pt=pt, cb=cb, acv=acv: e.activation(out=acv[:, cb, :], in_=pt[:, 0:TC_], func=ACTF.Identity, bias=cpar[:, 0, cb:cb + 1]),
                         reads=[pb, b_cpar], writes=[b_acv[cb]])
                    S.op("act", lambda e, pt=pt, cb=cb, asq=asq: e.activation(out=asq[:, cb, :], in_=pt[:, 0:TC_], func=ACTF.Square, bias=cpar[:, 0, cb:cb + 1]),
                         reads=[pb, b_cpar], writes=[b_asq[cb]])
                return acv, b_acv, asq, b_asq

            def c1_epilogue(ep):
                for (rt, rb, dst_rows, dbg_rows) in ep:
                    ot, ob = x1o.next()
                    _ln_aff(S, lnb, rt, rb, ot, ob, l1g, b_l1g, l1b, b_l1b)
                    S.dma("pool", dst_rows, ot[:], reads=[ob])
                    if debug:
                        S.dma("pool", dbg_rows, ot[:], reads=[ob])

            an2 = [T(es, "an2_%d" % i, [128, 4, TC_], BF16)[0] for i in range(2)]
            an2b = [[Buf("an2_%d_%d" % (i, c)) for c in range(4)] for i in range(2)]
            stt2 = [T(es, "stt2_%d" % i, [128, 4, TC_])[0] for i in range(2)]
            stt2b = [Buf("stt2_%d" % i) for i in range(2)]

            def c1_X(n):
                acv, b_acv, asq, b_asq = c1_conv(n)
                st_ = stt2[n % 2]
                b_st = stt2b[n % 2]
                an_ = an2[n % 2]
                b_an_ = an2b[n % 2]
                p1, p1b = PS.next()
                for cb in range(4):
                    S.op("pe", lambda e, p1=p1, cb=cb, acv=acv: e.matmul(p1[:, 0:TC_], ones_f[:], acv[:, cb, :], start=(cb == 0), stop=(cb == 3)),
                         reads=[b_ones, b_acv[cb]], writes=[p1b], signal=(cb == 3))
                p2, p2b = PS.next()
                for cb in range(4):
                    S.op("pe", lambda e, p2=p2, cb=cb, asq=asq: e.matmul(p2[:, 0:TC_], ones_f[:], asq[:, cb, :], start=(cb == 0), stop=(cb == 3)),
                         reads=[b_ones, b_asq[cb]], writes=[p2b], signal=(cb == 3))
                S.op("dve", lambda e, p1=p1: e.tensor_scalar_mul(st_[:, 0, :], p1[:, 0:TC_], 1.0 / CH), reads=[p1b], writes=[b_st])
                S.op("dve", lambda e: e.tensor_tensor(st_[:, 3, :], st_[:, 0, :], st_[:, 0, :], ALU.mult), reads=[b_st], writes=[b_st])
                S.op("dve", lambda e, p2=p2: e.scalar_tensor_tensor(st_[:, 1, :], p2[:, 0:TC_], 1.0 / CH, st_[:, 3, :], ALU.mult, ALU.subtract), reads=[p2b, b_st], writes=[b_st])
                S.op("act", lambda e: e.activation(out=st_[:, 2, :], in_=st_[:, 1, :], func=ACTF.Sqrt, bias=LN_EPS), reads=[b_st], writes=[b_st])
                S.op("dve", lambda e: e.reciprocal(st_[:, 2, :], st_[:, 2, :]), reads=[b_st], writes=[b_st])
                for cb in range(4):
                    S.op("dve", lambda e, cb=cb: e.tensor_tensor(acv[:, cb, :], acv[:, cb, :], st_[:, 0, :], ALU.subtract), reads=[b_acv[cb], b_st], writes=[b_acv[cb]])
                    S.op("dve", lambda e, cb=cb: e.tensor_tensor(acv[:, cb, :], acv[:, cb, :], st_[:, 2, :], ALU.mult), reads=[b_acv[cb], b_st], writes=[b_acv[cb]])
                    S.op("act", lambda e, cb=cb: e.activation(out=an_[:, cb, :], in_=acv[:, cb, :], func=ACTF.Silu, bias=cpar[:, 2, cb:cb + 1], scale=cpar[:, 1, cb:cb + 1]),
                         reads=[b_acv[cb], b_cpar], writes=[b_an_[cb]])

            c1_load(0)
            if len(c_chunks) > 1:
                c1_load(1)
            c1_X(0)
            pend_ep = None
            last_si = -1
            for n, (si, sq, ck) in enumerate(c_chunks):
                L = sq["L"]
                tok0 = sq["tok0"]
                c0 = ck * TC_
                g0 = tok0 + c0
                an_ = an2[n % 2]
                b_an_ = an2b[n % 2]
                sg, b_sg = sgl.next()
                S.dma("sp", sg[:], G_s[:, g0:g0 + TC_].rearrange("(gi p) t -> p gi t", p=128), writes=[b_sg])
                zt, zb_ = zbl.next()
                S.dma("sp", zt[:], ZB_s[:, g0:g0 + TC_].rearrange("(cb p) t -> p cb t", p=128), writes=[zb_])
                if n + 2 < len(c_chunks):
                    c1_load(n + 2)
                if n + 1 < len(c_chunks):
                    c1_X(n + 1)
                if si != last_si:
                    S.dma("sp", g1bc[:], MODROW[si, 0:D].partition_broadcast(128), writes=[b_g1bc])
                    last_si = si
                for db in range(8):
                    pa, pab = PS.next()
                    for cb in range(4):
                        S.op("pe", lambda e, pa=pa, cb=cb, db=db, an_=an_: e.matmul(pa[:, 0:TC_], Wpw[:, cb, db * 128:(db + 1) * 128], an_[:, cb, :], start=(cb == 0), stop=(cb == 3)),
                             reads=[b_Wpw, b_an_[cb]], writes=[pab], signal=(cb == 3))
                    pbb, pbbb = PS.next()
                    for cb in range(4):
                        S.op("pe", lambda e, pbb=pbb, cb=cb, db=db, zt=zt: e.matmul(pbb[:, 0:TC_], Whyo[:, cb, db * 128:(db + 1) * 128], zt[:, cb, :], start=(cb == 0), stop=(cb == 3)),
                             reads=[b_Whyo, zb_], writes=[pbbb], signal=(cb == 3))
                    t1, t1b = tmp1.next()
                    t2, t2b = tmp2.next()
                    S.op("dve", lambda e, t1=t1, pa=pa, db=db, sg=sg: e.tensor_tensor(t1[:], pa[:, 0:TC_], sg[:, db, :], ALU.mult), reads=[pab, b_sg], writes=[t1b])
                    S.op("dve", lambda e, t2=t2, pbb=pbb, db=db, sg=sg: e.tensor_tensor(t2[:], pbb[:, 0:TC_], sg[:, 8 + db, :], ALU.mult), reads=[pbbb, b_sg], writes=[t2b])
                    S.op("pool", lambda e, t1=t1, t2=t2, db=db: e.tensor_tensor(mt[:, db, :], t1[:], t2[:], ALU.add), reads=[t1b, t2b], writes=[b_mt8[db]])
                ep = []
                for tt in range(NTT):
                    xt, xb = xr.next()
                    S.dma("sp", xt[:], xrows(sq, c0 + tt * 128, 128), writes=[xb])
                    rt, rb = rr.next()
                    for half in range(2):
                        pm, pmb = PS.next()
                        for kb in range(8):
                            S.op("pe", lambda e, pm=pm, kb=kb, tt=tt, half=half: e.matmul(pm[:], mt[:, kb, tt * 128:(tt + 1) * 128], Wout[:, kb, half * 512:(half + 1) * 512], start=(kb == 0), stop=(kb == 7)),
                                 reads=[b_mt8[kb], b_Wout], writes=[pmb], signal=(kb == 7))
                        S.op("dve", lambda e, rt=rt, pm=pm, half=half: e.tensor_tensor(rt[:, half * 512:(half + 1) * 512], pm[:], g1bc[:, half * 512:(half + 1) * 512], ALU.mult),
                             reads=[pmb, b_g1bc], writes=[rb], disjoint=(half > 0))
                    S.op("dve", lambda e, rt=rt, xt=xt: e.scalar_tensor_tensor(rt[:], xt[:], ALPHA, rt[:], ALU.mult, ALU.add), reads=[xb, rb], writes=[rb])
                    ep.append((rt, rb, yrows(sq, c0 + tt * 128, 128), X1DBG[g0 + tt * 128:g0 + (tt + 1) * 128, :] if debug else None))
                if pend_ep is not None:
                    c1_epilogue(pend_ep)
                pend_ep = ep
            if pend_ep is not None:
                c1_epilogue(pend_ep)
            end_stage()

        with ExitStack() as es:
            W1, b_W1 = T(es, "W1", [128, 8, DFF], BF16)
            W2, b_W2 = T(es, "W2", [128, 32, D], BF16)
            nb0 = len(stage_bufs)
            with ExitStack() as es2:
                stg = R(es2, "stgd", [128, 8, 256], F32, 2)
                stg2 = R(es2, "stgd2", [128, 32, 64], F32, 2)
                load_w_bf16(W1, b_W1, mlp_w1, D, DFF, stg, 256)
                load_w_bf16(W2, b_W2, mlp_w2, DFF, D, stg2, 64)
                sub_stage(nb0)
            b1T, b_b1T = T(es, "b1T", [128, 32])
            g2bc, b_g2bc = T(es, "g2bc", [128, D])
            l2g, b_l2g = T(es, "l2g", [128, D])
            l2b, b_l2b = T(es, "l2b", [128, D])
            b2f, b_b2f = T(es, "b2f", [1, D])
            b2r, b_b2r = T(es, "b2r", [1, D], BF16)
            onesr, b_onesr = T(es, "onesr", [1, 128], BF16)
            xr = R(es, "xrd", [128, D], F32, 2 * NTT)
            lnb = ln_rings(es, "c2", NTT, NTT)
            lnb2 = dict(st=R(es, "std", [128, 2, 6], F32, 2), mv=R(es, "mvd", [128, 4], F32, 2))
            hTt = [T(es, "hTd%d" % i, [128, 8, TC_], BF16)[0] for i in range(2)]
            hTb = [[Buf("hTd%d_%d" % (i, t)) for t in range(NTT)] for i in range(2)]
            fT, _ = T(es, "fT", [128, 32, TC_], BF16)
            fTb = [Buf("fT%d" % i) for i in range(32)]
            rl = R(es, "rl", [128, TC_], F32, 2)
            rr = R(es, "rrd", [128, D], F32, 1)
            yo = R(es, "yo", [128, D], F32, 2)
            S.dma("sp", b1T[:], mlp_b1T, writes=[b_b1T])
            S.dma("sp", l2g[:], ln2_g.partition_broadcast(128), writes=[b_l2g])
            S.dma("sp", l2b[:], ln2_b.partition_broadcast(128), writes=[b_l2b])
            S.dma("sp", b2f[:], mlp_b2.rearrange("(a d) -> a d", a=1), writes=[b_b2f])
            S.op("dve", lambda e: e.tensor_copy(b2r[:], b2f[:]), reads=[b_b2f], writes=[b_b2r])
            S.op("pool", lambda e: e.memset(onesr[:], 1.0), writes=[b_onesr])

            def c2_phaseA(n):
                si, sq, ck = c_chunks[n]
                tiles = []
                for tt in range(NTT):
                    xt, xb = xr.next()
                    S.dma("sp", xt[:], yrows(sq, ck * TC_ + tt * 128, 128), writes=[xb])
                    tiles.append((xt[:], xb))
                return tiles, ln_phaseA(lnb, tiles)

            def c2_phaseB(n, xns):
                si, sq, ck = c_chunks[n]
                hTc = hTt[n % 2]
                ln_phaseB(xns, lambda t, kb, hTc=hTc: hTc[:, kb, t * 128:(t + 1) * 128], hTb[n % 2], si, 24, 32)

            tiles_cur, xns0 = c2_phaseA(0)
            c2_phaseB(0, xns0)
            last_si = -1
            for n, (si, sq, ck) in enumerate(c_chunks):
                c0 = ck * TC_
                if si != last_si:
                    S.dma("sp", g2bc[:], MODROW[si, D:2 * D].partition_broadcast(128), writes=[b_g2bc])
                    last_si = si
                hTc = hTt[n % 2]
                hb_ = hTb[n % 2]
                nxt = None
                for fb in range(32):
                    if fb == 8 and n + 1 < len(c_chunks):
                        nxt = c2_phaseA(n + 1)
                    pt, pb = PS.next()
                    for kb in range(8):
                        S.op("pe", lambda e, pt=pt, kb=kb, fb=fb, hTc=hTc: e.matmul(pt[:, 0:TC_], W1[:, kb, fb * 128:(fb + 1) * 128], hTc[:, kb, :], start=(kb == 0), stop=(kb == 7)),
                             reads=[b_W1] + hb_, writes=[pb], signal=(kb == 7))
                    rt_, rtb = rl.next()
                    S.op("act", lambda e, rt_=rt_, pt=pt, fb=fb: e.activation(out=rt_[:], in_=pt[:, 0:TC_], func=ACTF.Relu, bias=b1T[:, fb:fb + 1]), reads=[pb, b_b1T], writes=[rtb])
                    eng = "dve" if fb % 2 == 0 else "pool"
                    S.op(eng, lambda e, rt_=rt_, fb=fb: e.tensor_tensor(fT[:, fb, :], rt_[:], rt_[:], ALU.mult), reads=[rtb], writes=[fTb[fb]])
                if nxt is not None:
                    c2_phaseB(n + 1, nxt[1])
                for tt in range(NTT):
                    xt, xb = tiles_cur[tt]
                    rt, rb = rr.next()
                    for half in range(2):
                        pm, pmb = PS.next()
                        for fb in range(32):
                            S.op("pe", lambda e, pm=pm, fb=fb, tt=tt, half=half: e.matmul(pm[:], fT[:, fb, tt * 128:(tt + 1) * 128], W2[:, fb, half * 512:(half + 1) * 512], start=(fb == 0), stop=False),
                                 reads=[fTb[fb], b_W2], writes=[pmb], signal=False)
                        S.op("pe", lambda e, pm=pm, half=half: e.matmul(pm[:], onesr[:], b2r[:, half * 512:(half + 1) * 512], start=False, stop=True),
                             reads=[b_onesr, b_b2r], writes=[pmb], signal=True)
                        S.op("dve", lambda e, rt=rt, pm=pm, half=half: e.tensor_tensor(rt[:, half * 512:(half + 1) * 512], pm[:], g2bc[:, half * 512:(half + 1) * 512], ALU.mult),
                             reads=[pmb, b_g2bc], writes=[rb], disjoint=(half > 0))
                    S.op("dve", lambda e, rt=rt, xt=xt: e.scalar_tensor_tensor(rt[:], xt, ALPHA, rt[:], ALU.mult, ALU.add), reads=[xb, rb], writes=[rb])
                    ot, ob = yo.next()
                    _ln_aff(S, lnb2, rt, rb, ot, ob, l2g, b_l2g, l2b, b_l2b)
                    S.dma("pool", yrows(sq, c0 + tt * 128, 128), ot[:], reads=[ob])
                if nxt is not None:
                    tiles_cur = nxt[0]
            end_stage()
    return nc


def _ln_aff(S, lnb, rt, rb, ot, ob, g, gb, b, bb):
    st, stb = lnb["st"].next()
    mv, mvb = lnb["mv"].next()
    for i in range(2):
        S.op("dve", lambda e, i=i: e.bn_stats(st[:, i, :], rt[:, i * 512:(i + 1) * 512]), reads=[rb], writes=[stb])
    S.op("dve", lambda e: e.bn_aggr(mv[:, 0:2], st[:].rearrange("p a b -> p (a b)")), reads=[stb], writes=[mvb])
    S.op("act", lambda e: e.activation(out=mv[:, 2:3], in_=mv[:, 1:2], func=ACTF.Sqrt, bias=LN_EPS), reads=[mvb], writes=[mvb])
    S.op("dve", lambda e: e.reciprocal(mv[:, 2:3], mv[:, 2:3]), reads=[mvb], writes=[mvb])
    S.op("dve", lambda e: e.scalar_tensor_tensor(mv[:, 3:4], mv[:, 0:1], -1.0, mv[:, 2:3], ALU.mult, ALU.mult), reads=[mvb], writes=[mvb])
    S.op("act", lambda e: e.activation(out=ot[:], in_=rt[:], func=ACTF.Identity, bias=mv[:, 3:4], scale=mv[:, 2:3]), reads=[rb, mvb], writes=[ob])
    S.op("dve", lambda e: e.tensor_tensor(ot[:], ot[:], g[:], ALU.mult), reads=[ob, gb], writes=[ob])
    S.op("dve", lambda e: e.tensor_tensor(ot[:], ot[:], b[:], ALU.add), reads=[ob, bb], writes=[ob])


_TABLE_CACHE = {}


def _tables(L):
    if L in _TABLE_CACHE:
        return _TABLE_CACHE[L]
    KB = L // 128
    n = np.arange(L, dtype=np.int64)
    m = (n[:, None] * n[None, :]) % (2 * L)
    ang = m.astype(np.float64) * (math.pi / L)
    C = np.cos(ang)
    Sn = np.sin(ang)
    alt = np.where(n % 2 == 0, 1.0, -1.0)
    SF = Sn.copy()
    SF[:, 0] = alt
    SI = Sn.copy()
    SI[0, :] = alt

    def lay(M):
        return np.ascontiguousarray(M.reshape(KB, 128, KB, 128).transpose(2, 1, 0, 3)).astype(ml_dtypes.bfloat16)

    t = np.linspace(0.0, 1.0, L, dtype=np.float32)[:, None]
    w = ((2.0 * math.pi / L) * np.arange(L, dtype=np.float32))[:, None].astype(np.float32)
    bands = np.linspace(1e-4, 15, 16, dtype=np.float32)[None, :]
    z = np.concatenate([t, np.cos(w * bands), -np.sin(w * bands)], axis=-1).astype(np.float32)
    max_decay = math.log(1e-2) / 0.3
    min_decay = math.log(1e-2) / 1.5
    deltas = np.abs(np.linspace(min_decay, max_decay, CH, dtype=np.float32))
    decF = np.exp(-t * deltas[None, :]).astype(np.float32)
    decB = np.exp(-t * deltas[::-1][None, :]).astype(np.float32)
    out = dict(zT=np.ascontiguousarray(z.T), decF=decF, decB=decB, TC=lay(C), TSF=lay(SF), TSI=lay(SI))
    _TABLE_CACHE[L] = out
    return out


def _dw_pack(w):
    wp = np.zeros((32, 512), np.float32)
    wp[:31] = w
    return np.ascontiguousarray(wp.reshape(8, 4, 16, 32).transpose(1, 3, 2, 0).reshape(128, 16, 8))


def _colT(v, nblk):
    return np.ascontiguousarray(np.asarray(v, np.float32).reshape(nblk, 128).T)


def run(cfg, inputs, ncores, debug=False):
    NP, LP, NS, LS = cfg["NP"], cfg["LP"], cfg["NS"], cfg["LS"]
    f = lambda k: np.ascontiguousarray(np.asarray(inputs[k], dtype=np.float32))
    shared = dict(
        altc=np.ascontiguousarray(np.stack([np.where(np.arange(128) % 2 == 0, 1.0, -1.0)] * 2, axis=1).astype(np.float32)),
        w_ada=f("w_ada")[0], b_ada=f("b_ada")[0], b_adaT=_colT(f("b_ada")[0], 48),
        w_in=f("w_in")[0],
        dw_wT=np.ascontiguousarray(f("conv_dw_w")[0].reshape(31, 4, 128).transpose(2, 1, 0)),
        dw_pk=_dw_pack(f("conv_dw_w")[0]),
        dw_bT=_colT(f("conv_dw_b")[0], 4), cln_gT=_colT(f("conv_ln_g")[0], 4), cln_bT=_colT(f("conv_ln_b")[0], 4),
        pw_w=f("conv_pw_w")[0], hs_w=f("hy_short_w")[0], hs_b=f("hy_short_b")[0], hs_bT=_colT(f("hy_short_b")[0], 12),
        hs_wT=np.ascontiguousarray(f("hy_short_w")[0].reshape(3, 12, 128).transpose(2, 1, 0)),
        f_w1=f("hy_ffn_w1")[0], f_b1=f("hy_ffn_b1")[0].reshape(64, 1), f_fr=f("hy_sin_freq")[0].reshape(64, 1),
        f_w2=f("hy_ffn_w2")[0], f_b2=f("hy_ffn_b2")[0].reshape(64, 1), f_w3=f("hy_ffn_w3")[0],
        hy_skip=f("hy_skip")[0], hyo_w=f("hy_out_w")[0], w_out=f("w_out")[0],
        ln1_g=f("ln1_g")[0], ln1_b=f("ln1_b")[0], mlp_w1=f("mlp_w1")[0], mlp_b1T=_colT(f("mlp_b1")[0], 32),
        mlp_w2=f("mlp_w2")[0], mlp_b2=f("mlp_b2")[0], ln2_g=f("ln2_g")[0], ln2_b=f("ln2_b")[0],
    )
    Ls = sorted(set(([LP] if NP else []) + ([LS] if NS else [])))
    for L in Ls:
        for k, v in _tables(L).items():
            shared["%s_%d" % (k, L)] = v
    xp = f("x_prompt") if NP else None
    xs = f("x_sample") if NS else None
    cp = f("c_prompt") if NP else None
    cs = f("c_sample") if NS else None
    in_maps = []
    for c in range(ncores):
        m = dict(shared)
        cl = []
        if NP:
            m["x_p"] = xp[c * NP:(c + 1) * NP].reshape(NP * LP, D)
            cl.append(cp[c * NP:(c + 1) * NP])
        if NS:
            m["x_s"] = xs[c * NS:(c + 1) * NS].reshape(NS * LS, D)
            cl.append(cs[c * NS:(c + 1) * NS])
        call = np.concatenate(cl, axis=0)
        m["cT"] = np.ascontiguousarray(call.reshape(-1, 8, 128).transpose(2, 1, 0))
        in_maps.append(m)
    nc = build(cfg, debug=debug)
    res = run_bass_kernel_spmd(nc, in_maps, core_ids=list(range(ncores)))
    outs = []
    if NP:
        outs.append(np.concatenate([np.asarray(r["y_p"], np.float32).reshape(NP, LP, D) for r in res.results], axis=0))
    if NS:
        outs.append(np.concatenate([np.asarray(r["y_s"], np.float32).reshape(NS, LS, D) for r in res.results], axis=0))
    return tuple(outs), res


def kernel(**inputs):
    cfg = dict(NP=4, LP=2048, NS=2, LS=4096)
    outs, _ = run(cfg, inputs, NCORES)
    return outs
```

```python
import math
from contextlib import ExitStack

import numpy as np
import ml_dtypes

import concourse.bass as bass
import concourse.mybir as mybir
from concourse.bass_utils import run_bass_kernel_spmd

F32 = mybir.dt.float32
BF16 = mybir.dt.bfloat16
ACTF = mybir.ActivationFunctionType
ALU = mybir.AluOpType

D = 1024
DFF = 4096
CH = 512
NCORES = 8
LN_EPS = 1e-5
ALPHA = 2.0 ** 0.25
MAGIC = 12582912.0
TWO_PI = 2.0 * math.pi

ENGS = ("pe", "act", "dve", "pool", "sp")
import os as _os0
NODISJOINT = bool(_os0.environ.get("NODISJOINT"))
CHECK = bool(_os0.environ.get("SCHED_CHECK"))


class Buf:
    __slots__ = ("name", "w", "r", "pr", "lsem", "ssem")

    def __init__(self, name):
        self.name = name
        self.w = []
        self.r = []
        self.pr = []
        self.lsem = None
        self.ssem = None


class Sched:
    def __init__(self, nc):
        self.nc = nc
        self.ops = {e: [] for e in ENGS}
        self.cnt = {e: 0 for e in ENGS}
        self.seen = {e: {} for e in ENGS}
        self.dcnt = {}
        self.free_dma_sems = []
        self.sems = {}
        self.pending_nosig = {e: False for e in ENGS}
        self.nblk = 0
        self.log = []
        self.chk_sem = {}
        self.chk_semvc = {}
        self.chk_vc = {e: {} for e in ENGS}
        self.chk_acc = {}

    def register(self, eng_sems, dma_sems):
        for e, s in zip(ENGS, eng_sems):
            self.sems[e] = s
        self.free_sw_sems = []
        for i, s in enumerate(dma_sems):
            k = "d%d" % i
            self.sems[k] = s
            self.dcnt[k] = 0
            (self.free_sw_sems if i < 14 else self.free_dma_sems).append(k)

    def release(self, bufs):
        for b in bufs:
            if b.lsem is not None:
                self.free_dma_sems.append(b.lsem)
                b.lsem = None
            if b.ssem is not None:
                self.free_sw_sems.append(b.ssem)
                b.ssem = None

    def _dma_sem(self, buf, queue="sp"):
        if queue == "pool":
            if buf.ssem is None:
                if not self.free_sw_sems:
                    raise RuntimeError("out of SW-DMA semaphores")
                buf.ssem = self.free_sw_sems.pop(0)
            return buf.ssem
        if buf.lsem is None:
            if not self.free_dma_sems:
                raise RuntimeError("out of DMA semaphores")
            buf.lsem = self.free_dma_sems.pop(0)
        return buf.lsem

    def _waits(self, eng, reads, writes, disjoint=False):
        need = {}
        for b in reads:
            for (s, v) in b.w:
                need[s] = max(need.get(s, 0), v)
        for b in writes:
            if not disjoint:
                for (s, v) in b.w:
                    need[s] = max(need.get(s, 0), v)
            for (s, v) in b.r:
                need[s] = max(need.get(s, 0), v)
            for (s, v) in b.pr:
                need[s] = max(need.get(s, 0), v)
        out = []
        for s, v in need.items():
            if s == eng and eng == "pe":
                continue
            if self.seen[eng].get(s, 0) >= v:
                continue
            self.seen[eng][s] = v
            out.append((s, v))
        return out

    def _note(self, tok, reads, writes, disjoint):
        for b in reads:
            b.r.append(tok)
            if len(b.r) > 48:
                b.r = _compact(b.r)
        for b in writes:
            if disjoint:
                if b.r:
                    b.pr = b.r
                    b.r = []
                    b.w = [tok]
                else:
                    b.w.append(tok)
                    if len(b.w) > 48:
                        b.w = _compact(b.w)
            else:
                b.pr = _compact(b.r + b.w)
                b.w = [tok]
                b.r = []

    def op(self, eng, fn, reads=(), writes=(), signal=True, disjoint=False):
        if NODISJOINT:
            disjoint = False
        waits = self._waits(eng, reads, writes, disjoint)
        if signal:
            self.cnt[eng] += 1
            tok = (eng, self.cnt[eng])
            self.pending_nosig[eng] = False
            inc = (eng, 1)
        else:
            tok = (eng, self.cnt[eng] + 1)
            self.pending_nosig[eng] = True
            inc = None
        self.ops[eng].append((waits, fn, inc))
        if CHECK:
            self.log.append(("op", eng, list(waits), inc, [b for b in reads], [b for b in writes], disjoint, len(self.ops[eng]) - 1))
        self._note(tok, reads, writes, disjoint)
        return tok

    def dma(self, queue, out_ap, in_ap, reads=(), writes=(), sembuf=None, disjoint=False, **kw):
        waits = self._waits(queue, reads, writes, disjoint)
        if sembuf is None:
            sembuf = writes[0] if writes else reads[0]
        k = self._dma_sem(sembuf, queue)
        self.dcnt[k] += 16
        tok = (k, self.dcnt[k])
        fn = lambda e, o=out_ap, i=in_ap, kw=kw: e.dma_start(out=o, in_=i, **kw)
        self.ops[queue].append((waits, fn, (k, 16)))
        if CHECK:
            self.log.append(("dma", queue, list(waits), (k, 16), [b for b in reads], [b for b in writes], disjoint, len(self.ops[queue]) - 1))
        self._note(tok, reads, writes, disjoint)
        return tok

    def check(self):
        queues = {e: [] for e in ENGS}
        for ent in self.log:
            queues[ent[1]].append(ent)
        full = {e: [] for e in ENGS}
        for e in ENGS:
            li = {ent[7]: ent for ent in queues[e]}
            for i, (waits, fn, inc) in enumerate(self.ops[e]):
                if i in li:
                    full[e].append(li[i])
                else:
                    full[e].append(("bar", e, list(waits), inc, [], [], False, i))
        ptr = {e: 0 for e in ENGS}
        sem = self.chk_sem
        semvc = self.chk_semvc
        vc = self.chk_vc
        pos = getattr(self, "chk_pos", {e: 0 for e in ENGS})
        nerr = 0

        def join(a, b):
            for k, v in b.items():
                if a.get(k, 0) < v:
                    a[k] = v

        progress = True
        while progress:
            progress = False
            for e in ENGS:
                while ptr[e] < len(full[e]):
                    kind, _, waits, inc, reads, writes, disjoint, _i = full[e][ptr[e]]
                    ok = all(sem.get(s, 0) >= v for s, v in waits)
                    if not ok:
                        break
                    cur = dict(vc[e])
                    for s_, v in waits:
                        key = (s_, v)
                        if key in semvc:
                            join(cur, semvc[key])
                        else:
                            cands = [kk for kk in semvc if kk[0] == s_ and kk[1] <= v]
                            for kk in cands:
                                join(cur, semvc[kk])
                    pos[e] += 1
                    cur[e] = pos[e]
                    vc[e] = cur
                    if kind != "bar":
                        if kind == "dma":
                            thread = inc[0]
                            done = dict(cur)
                            sem[thread] = sem.get(thread, 0) + 16
                            done[thread] = sem[thread]
                            semvc[(thread, sem[thread])] = done
                            acc_vc_start = cur
                            acc_key = (thread, sem[thread])
                        else:
                            if inc is not None:
                                sem[inc[0]] = sem.get(inc[0], 0) + 1
                                semvc[(inc[0], sem[inc[0]])] = cur
                            acc_key = (e, pos[e])
                            acc_vc_start = cur
                        for b in reads:
                            a = self.chk_acc.setdefault(id(b), dict(w=[], r=[], name=b.name))
                            for (k, p) in a["w"]:
                                if acc_vc_start.get(k, 0) < p and not (k == e == "pe"):
                                    nerr += 1
                                    if nerr < 20:
                                        print("RACE RAW buf=%s reader=%s@%d writer=%s@%d" % (b.name, e, pos[e], k, p))
                            a["r"].append(acc_key)
                            if len(a["r"]) > 200:
                                a["r"] = a["r"][-100:]
                        for b in writes:
                            a = self.chk_acc.setdefault(id(b), dict(w=[], r=[], name=b.name))
                            for (k, p) in a["r"]:
                                if acc_vc_start.get(k, 0) < p and not (k == e == "pe"):
                                    nerr += 1
                                    if nerr < 20:
                                        print("RACE WAR buf=%s writer=%s@%d reader=%s@%d" % (b.name, e, pos[e], k, p))
                            if not disjoint:
                                for (k, p) in a["w"]:
                                    if acc_vc_start.get(k, 0) < p and not (k == e == "pe"):
                                        nerr += 1
                                        if nerr < 20:
                                            print("RACE WAW buf=%s writer=%s@%d prev=%s@%d" % (b.name, e, pos[e], k, p))
                                a["w"] = [acc_key]
                                a["r"] = []
                            else:
                                if a["r"]:
                                    a["w"] = [acc_key]
                                    a["r"] = []
                                else:
                                    a["w"].append(acc_key)
                    ptr[e] += 1
                    progress = True
        for e in ENGS:
            if ptr[e] < len(full[e]):
                print("DEADLOCK: engine %s stuck at op %d/%d waits=%s semvals=%s" % (
                    e, ptr[e], len(full[e]), full[e][ptr[e]][2], {s_: sem.get(s_, 0) for s_, _ in full[e][ptr[e]][2]}))
                nerr += 1
        self.chk_pos = pos
        self.log = []
        print("[check] block ok" if nerr == 0 else "[check] %d problems" % nerr)

    def barrier(self):
        allw = [(e, self.cnt[e]) for e in ENGS if self.cnt[e] > 0]
        allw += [(k, v) for k, v in self.dcnt.items() if v > 0]
        for e in ENGS:
            assert not self.pending_nosig[e], "engine %s has trailing non-signalling op" % e
            waits = []
            for s, v in allw:
                if s == e:
                    continue
                if self.seen[e].get(s, 0) >= v:
                    continue
                self.seen[e][s] = v
                waits.append((s, v))
            if waits:
                self.ops[e].append((waits, None, None))

    def emit(self):
        nc = self.nc
        S = self
        if CHECK:
            self.check()

        def replay(e, name):
            for waits, fn, inc in S.ops[name]:
                for s, v in waits:
                    e.wait_ge(S.sems[s], v)
                if fn is not None:
                    ins = fn(e)
                    if inc is not None:
                        ins.then_inc(S.sems[inc[0]], inc[1])

        with nc.Block() as block:
            @block.tensor
            def _(e):
                replay(e, "pe")

            @block.scalar
            def _(e):
                replay(e, "act")

            @block.vector
            def _(e):
                replay(e, "dve")

            @block.gpsimd
            def _(e):
                replay(e, "pool")

            @block.sync
            def _(e):
                replay(e, "sp")
        self.ops = {e: [] for e in ENGS}


def _compact(toks):
    best = {}
    for s, v in toks:
        best[s] = max(best.get(s, 0), v)
    return list(best.items())


class Ring:
    def __init__(self, es, nc, name, shape, dt, n, psum=False):
        mk = nc.psum_tensor if psum else nc.sbuf_tensor
        self.t = [es.enter_context(mk("%s_%d" % (name, i), shape, dt)) for i in range(n)]
        self.b = [Buf("%s_%d" % (name, i)) for i in range(n)]
        self.n = n
        self.i = 0

    def next(self):
        k = self.i % self.n
        self.i += 1
        return self.t[k], self.b[k]


def build(cfg, debug=False):
    NP, LP, NS, LS = cfg["NP"], cfg["LP"], cfg["NS"], cfg["LS"]
    NSEQ = NP + NS
    NTOK = NP * LP + NS * LS
    Ls = sorted(set(([LP] if NP else []) + ([LS] if NS else [])))
    seqs = []
    t0 = 0
    for i in range(NP):
        seqs.append(dict(g="p", i=i, L=LP, tok0=t0, row0=i * LP))
        t0 += LP
    for i in range(NS):
        seqs.append(dict(g="s", i=i, L=LS, tok0=t0, row0=i * LS))
        t0 += LS

    nc = bass.Bass("TRN2", target_bir_lowering=False)

    def din(name, shape, dt=F32):
        return nc.dram_tensor(name, list(shape), dt, kind="ExternalInput").ap()

    def dout(name, shape, dt=F32):
        return nc.dram_tensor(name, list(shape), dt, kind="ExternalOutput").ap()

    def dscr(name, shape, dt):
        return nc.dram_tensor(name, list(shape), dt, kind=("ExternalOutput" if debug else "Internal")).ap()

    X = {}
    Y = {}
    if NP:
        X["p"] = din("x_p", [NP * LP, D])
        Y["p"] = dout("y_p", [NP * LP, D])
    if NS:
        X["s"] = din("x_s", [NS * LS, D])
        Y["s"] = dout("y_s", [NS * LS, D])
    cT = din("cT", [128, 8, NSEQ])
    w_ada = din("w_ada", [D, 6 * D])
    b_ada = din("b_ada", [6 * D])
    b_adaT = din("b_adaT", [128, 48])
    w_in = din("w_in", [D, 4608])
    dw_wT = din("dw_wT", [128, 4, 31])
    dw_pk = din("dw_pk", [128, 16, 8])
    dw_bT = din("dw_bT", [128, 4])
    cln_gT = din("cln_gT", [128, 4])
    cln_bT = din("cln_bT", [128, 4])
    pw_w = din("pw_w", [CH, D])
    hs_w = din("hs_w", [3, 1536])
    hs_b = din("hs_b", [1536])
    hs_bT = din("hs_bT", [128, 12])
    hs_wT = din("hs_wT", [128, 12, 3])
    f_w1 = din("f_w1", [33, 64])
    f_b1 = din("f_b1", [64, 1])
    f_fr = din("f_fr", [64, 1])
    f_w2 = din("f_w2", [64, 64])
    f_b2 = din("f_b2", [64, 1])
    f_w3 = din("f_w3", [64, 2048])
    hy_skip = din("hy_skip", [2, 512])
    hyo_w = din("hyo_w", [CH, D])
    w_out = din("w_out", [D, D])
    ln1_g = din("ln1_g", [D])
    ln1_b = din("ln1_b", [D])
    mlp_w1 = din("mlp_w1", [D, DFF])
    mlp_b1T = din("mlp_b1T", [128, 32])
    mlp_w2 = din("mlp_w2", [DFF, D])
    mlp_b2 = din("mlp_b2", [D])
    ln2_g = din("ln2_g", [D])
    ln2_b = din("ln2_b", [D])
    altc = din("altc", [128, 2])
    TAB = {}
    for L in Ls:
        KB = L // 128
        TAB[L] = dict(
            zT=din("zT_%d" % L, [33, L]),
            decF=din("decF_%d" % L, [L, CH]),
            decB=din("decB_%d" % L, [L, CH]),
            TC=din("TC_%d" % L, [KB, 128, KB, 128], BF16),
            TSF=din("TSF_%d" % L, [KB, 128, KB, 128], BF16),
            TSI=din("TSI_%d" % L, [KB, 128, KB, 128], BF16),
            HS=dscr("HS_%d" % L, [2, 2 * KB, 128, CH], F32),
        )
    V_s = dscr("V_s", [NTOK, CH], BF16)
    X1_s = dscr("X1_s", [NTOK, CH], BF16)
    X2_s = dscr("X2_s", [NTOK, CH], BF16)
    A_s = dscr("A_s", [CH, NTOK], BF16)
    ZB_s = dscr("ZB_s", [CH, NTOK], BF16)
    MODROW = dscr("MODROW", [NSEQ, 2 * D], F32)
    G_s = dscr("G_s", [2 * D, NTOK], BF16)
    X1DBG = dscr("X1DBG", [NTOK, D], F32) if debug else None

    def xrows(sq, t, n):
        return X[sq["g"]][sq["row0"] + t: sq["row0"] + t + n, :]

    def yrows(sq, t, n):
        return Y[sq["g"]][sq["row0"] + t: sq["row0"] + t + n, :]

    with ExitStack() as top:
        esems = [top.enter_context(nc.semaphore("eng%d" % i)) for i in range(5)]
        dsems = []
        for i in range(98):
            try:
                dsems.append(top.enter_context(nc.semaphore("dma%d" % i)))
            except KeyError:
                break
        S = Sched(nc)
        S.register(esems, dsems)

        PS = Ring(top, nc, "ps", [128, 512], F32, 8, psum=True)

        uniq = [0]

        def Tt(es, name, shape, dt=F32):
            uniq[0] += 1
            name = "%s_u%d" % (name, uniq[0])
            return es.enter_context(nc.sbuf_tensor(name, list(shape), dt)), Buf(name)

        ident_f, b_identf = Tt(top, "ident_f", [128, 128], F32)
        ident, b_ident = Tt(top, "ident", [128, 128], BF16)
        ones_f, b_ones = Tt(top, "ones_f", [128, 128], F32)
        modT, b_modT = Tt(top, "modT", [128, 48, NSEQ], F32)
        S.op("pool", lambda e: e.memset(ident_f[:], 0.0), writes=[b_identf])
        S.op("pool", lambda e: e.affine_select(out=ident_f[:], in_=ident_f[:], compare_op=ALU.not_equal, fill=1.0,
                                               base=0, pattern=[[-1, 128]], channel_multiplier=1),
             reads=[b_identf], writes=[b_identf])
        S.op("dve", lambda e: e.tensor_copy(ident[:], ident_f[:]), reads=[b_identf], writes=[b_ident])
        S.op("pool", lambda e: e.memset(ones_f[:], 1.0), writes=[b_ones])

        stage_bufs = []

        def end_stage():
            S.barrier()
            S.emit()
            S.release(stage_bufs)
            del stage_bufs[:]

        def T(es, name, shape, dt=F32):
            t, b = Tt(es, name, shape, dt)
            stage_bufs.append(b)
            return t, b

        def R(es, name, shape, dt, n):
            uniq[0] += 1
            r = Ring(es, nc, "%s_u%d" % (name, uniq[0]), shape, dt, n)
            stage_bufs.extend(r.b)
            return r

        with ExitStack() as es:
            sc, b_sc = T(es, "sc", [128, 8, NSEQ])
            badaT, b_badaT = T(es, "badaT", [128, 48])
            wr = R(es, "wada", [128, 8, 512], F32, 2)
            brow = R(es, "brow", [NSEQ, 512], F32, 2)
            grow = R(es, "grow", [NSEQ, 512], F32, 2)
            S.dma("sp", sc[:], cT, writes=[b_sc])
            S.dma("sp", badaT[:], b_adaT, writes=[b_badaT])
            S.op("act", lambda e: e.activation(out=sc[:], in_=sc[:], func=ACTF.Silu), reads=[b_sc], writes=[b_sc])
            for ck in range(12):
                wt, wb = wr.next()
                S.dma("sp", wt[:], w_ada[:, ck * 512:(ck + 1) * 512].rearrange("(kb p) n -> p kb n", p=128), writes=[wb])
                if ck in (4, 5, 10, 11):
                    pt, pb = PS.next()
                    for kb in range(8):
                        S.op("pe", lambda e, pt=pt, wt=wt, kb=kb: e.matmul(pt[0:NSEQ, :], sc[:, kb, :], wt[:, kb, :], start=(kb == 0), stop=(kb == 7)),
                             reads=[b_sc, wb], writes=[pb], signal=(kb == 7))
                    bt, bb = brow.next()
                    S.dma("sp", bt[:], b_ada[ck * 512:(ck + 1) * 512].partition_broadcast(NSEQ), writes=[bb])
                    gt, gb = grow.next()
                    S.op("dve", lambda e, gt=gt, pt=pt, bt=bt: e.tensor_tensor(gt[:], pt[0:NSEQ, :], bt[:], ALU.add),
                         reads=[pb, bb], writes=[gb])
                    col = (0 if ck < 6 else D) + (ck % 2) * 512
                    S.dma("pool", MODROW[:, col:col + 512], gt[:], reads=[gb])
                else:
                    for q in range(4):
                        blk = ck * 4 + q
                        pt, pb = PS.next()
                        for kb in range(8):
                            S.op("pe", lambda e, pt=pt, wt=wt, kb=kb, q=q: e.matmul(pt[:, 0:NSEQ], wt[:, kb, q * 128:(q + 1) * 128], sc[:, kb, :], start=(kb == 0), stop=(kb == 7)),
                                 reads=[b_sc, wb], writes=[pb], signal=(kb == 7))
                        one = 1.0 if ck in (2, 3, 8, 9) else 0.0
                        S.op("dve", lambda e, pt=pt, blk=blk, one=one: e.tensor_scalar(modT[:, blk, :], pt[:, 0:NSEQ], badaT[:, blk:blk + 1], one, ALU.add, ALU.add),
                             reads=[pb, b_badaT], writes=[b_modT])
            end_stage()

        def ln_phaseA(lnb, tiles):
            NT = len(tiles)
            st, stb = lnb["st"].next()
            mv, mvb = lnb["mv"].next()
            for t, (xt, xb) in enumerate(tiles):
                for i in range(2):
                    S.op("dve", lambda e, i=i, t=t, st=st, xt=xt: e.bn_stats(st[:, t, i, :], xt[:, i * 512:(i + 1) * 512]), reads=[xb], writes=[stb], disjoint=True)
                S.op("dve", lambda e, t=t, st=st, mv=mv: e.bn_aggr(mv[:, t, 0:2], st[:, t].rearrange("p a b -> p (a b)")), reads=[stb], writes=[mvb], disjoint=True)
            S.op("act", lambda e, mv=mv: e.activation(out=mv[:, 0:NT, 2:3], in_=mv[:, 0:NT, 1:2], func=ACTF.Sqrt, bias=LN_EPS), reads=[mvb], writes=[mvb])
            S.op("dve", lambda e, mv=mv: e.reciprocal(mv[:, 0:NT, 2:3], mv[:, 0:NT, 2:3]), reads=[mvb], writes=[mvb])
            S.op("dve", lambda e, mv=mv: e.scalar_tensor_tensor(mv[:, 0:NT, 3:4], mv[:, 0:NT, 0:1], -1.0, mv[:, 0:NT, 2:3], ALU.mult, ALU.mult), reads=[mvb], writes=[mvb])
            xns = []
            for t, (xt, xb) in enumerate(tiles):
                xn, xnb = lnb["xn"].next()
                S.op("act", lambda e, mv=mv, xn=xn, xt=xt, t=t: e.activation(out=xn[:], in_=xt, func=ACTF.Identity, bias=mv[:, t, 3:4], scale=mv[:, t, 2:3]),
                     reads=[xb, mvb], writes=[xnb])
                xns.append((xn, xnb))
            return xns

        def ln_phaseB(xns, dst_fn, dst_bufs, sidx, shift_blk, scale_blk):
            for t, (xn, xnb) in enumerate(xns):
                pt, pb = PS.next()
                pv = pt.bitcast(BF16)
                for kb in range(8):
                    S.op("pe", lambda e, kb=kb, pv=pv, xn=xn: e.transpose(pv[:, kb * 128:(kb + 1) * 128], xn[:, kb * 128:(kb + 1) * 128], ident[:]),
                         reads=[xnb, b_ident], writes=[pb], signal=(kb == 7))
                eng = "dve" if t % 2 == 0 else "act"
                for kb in range(8):
                    dv = dst_fn(t, kb)
                    sc_ap = modT[:, scale_blk + kb, sidx:sidx + 1]
                    sh_ap = modT[:, shift_blk + kb, sidx:sidx + 1]
                    if eng == "dve":
                        S.op("dve", lambda e, dv=dv, pv=pv, kb=kb, sc_ap=sc_ap, sh_ap=sh_ap: e.tensor_scalar(dv, pv[:, kb * 128:(kb + 1) * 128], sc_ap, sh_ap, ALU.mult, ALU.add),
                             reads=[pb, b_modT], writes=[dst_bufs[t]], disjoint=True)
                    else:
                        S.op("act", lambda e, dv=dv, pv=pv, kb=kb, sc_ap=sc_ap, sh_ap=sh_ap: e.activation(out=dv, in_=pv[:, kb * 128:(kb + 1) * 128], func=ACTF.Identity, bias=sh_ap, scale=sc_ap),
                             reads=[pb, b_modT], writes=[dst_bufs[t]], disjoint=True)

        def ln_rings(es, tag, nt, nxn):
            return dict(st=R(es, "st" + tag, [128, nt, 2, 6], F32, 2), mv=R(es, "mv" + tag, [128, nt, 4], F32, 2),
                        xn=R(es, "xn" + tag, [128, D], BF16, nxn))

        def load_w_bf16(dst, dst_b, src, K, N, stg, step, col0=0, scale_bc=None, scale_b=None):
            KBn = K // 128
            c = 0
            i = 0
            while c < N:
                n = min(step, N - c)
                stt, stb_ = stg.next()
                S.dma("sp", stt[:, 0:KBn, 0:n], src[:, c:c + n].rearrange("(kb p) n -> p kb n", p=128), writes=[stb_])
                if scale_bc is None:
                    eng = "act" if i % 2 == 0 else "dve"
                    if eng == "act":
                        S.op("act", lambda e, stt=stt, c=c, n=n: e.activation(out=dst[:, :, col0 + c:col0 + c + n], in_=stt[:, 0:KBn, 0:n], func=ACTF.Copy),
                             reads=[stb_], writes=[dst_b])
                    else:
                        S.op("dve", lambda e, stt=stt, c=c, n=n: e.tensor_copy(dst[:, :, col0 + c:col0 + c + n], stt[:, 0:KBn, 0:n]),
                             reads=[stb_], writes=[dst_b])
                else:
                    for kb in range(KBn):
                        eng = "dve" if kb % 2 == 0 else "pool"
                        S.op(eng, lambda e, stt=stt, c=c, n=n, kb=kb: e.tensor_tensor(dst[:, kb, col0 + c:col0 + c + n], stt[:, kb, 0:n], scale_bc[:, c:c + n], ALU.mult),
                             reads=[stb_, scale_b], writes=[dst_b])
                c += n
                i += 1

        def sub_stage(bufs_before):
            S.barrier()
            S.emit()
            S.release(stage_bufs[bufs_before:])
            del stage_bufs[bufs_before:]

        for L in Ls:
            KB = L // 128
            tab = TAB[L]
            with ExitStack() as es:
                HSUM, b_HSUM = T(es, "HSUM", [128, KB, 2, 512], BF16)
                HDIF, b_HDIF = T(es, "HDIF", [128, KB, 2, 512], BF16)
                w3s, b_w3s = T(es, "w3s", [64, 2048])
                hd2, b_hd2 = T(es, "hd2", [64, L])
                skr, b_skr = T(es, "skr", [1, 2, 512])
                alt_f, b_altf = T(es, "alt_f", [128, 2])
                alt, b_alt = T(es, "alt", [128, 2], BF16)
                S.dma("sp", w3s[:], f_w3, writes=[b_w3s])
                S.dma("sp", skr[:], hy_skip.rearrange("(a o) c -> a o c", a=1), writes=[b_skr])
                S.dma("sp", alt_f[:], altc, writes=[b_altf])
                S.op("dve", lambda e: e.tensor_copy(alt[:], alt_f[:]), reads=[b_altf], writes=[b_alt])
                nb0 = len(stage_bufs)
                with ExitStack() as es2:
                    zT, b_zT = T(es2, "zT", [33, L])
                    w1s, b_w1s = T(es2, "w1s", [33, 64])
                    w2s, b_w2s = T(es2, "w2s", [64, 64])
                    fpar, b_fpar = T(es2, "fpar", [64, 8])
                    hd1, b_hd1 = T(es2, "hd1", [64, L])
                    argr = R(es2, "argr", [64, 512], F32, 2)
                    kr = R(es2, "kr", [64, 512], F32, 2)
                    S.dma("sp", zT[:], tab["zT"], writes=[b_zT])
                    S.dma("sp", w1s[:], f_w1, writes=[b_w1s])
                    S.dma("sp", w2s[:], f_w2, writes=[b_w2s])
                    S.dma("sp", fpar[:, 0:1], f_b1, writes=[b_fpar])
                    S.dma("sp", fpar[:, 1:2], f_fr, writes=[b_fpar])
                    S.dma("sp", fpar[:, 2:3], f_b2, writes=[b_fpar])
                    S.op("dve", lambda e: e.tensor_tensor(fpar[:, 3:4], fpar[:, 0:1], fpar[:, 1:2], ALU.mult), reads=[b_fpar], writes=[b_fpar])
                    S.op("dve", lambda e: e.tensor_tensor(fpar[:, 4:5], fpar[:, 2:3], fpar[:, 1:2], ALU.mult), reads=[b_fpar], writes=[b_fpar])
                    for layer in range(2):
                        src, srcb = (zT, b_zT) if layer == 0 else (hd1, b_hd1)
                        wl, wlb = (w1s, b_w1s) if layer == 0 else (w2s, b_w2s)
                        dst, dstb = (hd1, b_hd1) if layer == 0 else (hd2, b_hd2)
                        kin = 33 if layer == 0 else 64
                        fb_col = 3 if layer == 0 else 4
                        for ck in range(L // 512):
                            pt, pb = PS.next()
                            S.op("pe", lambda e, pt=pt, wl=wl, src=src, ck=ck, kin=kin: e.matmul(pt[0:64, :], wl[0:kin, :], src[0:kin, ck * 512:(ck + 1) * 512], start=True, stop=True),
                                 reads=[wlb, srcb], writes=[pb])
                            at, ab = argr.next()
                            kt, kb_ = kr.next()
                            S.op("dve", lambda e, at=at, pt=pt, fb_col=fb_col: e.tensor_scalar(at[:], pt[0:64, :], fpar[:, 1:2], fpar[:, fb_col:fb_col + 1], ALU.mult, ALU.add),
                                 reads=[pb, b_fpar], writes=[ab])
                            S.op("dve", lambda e, at=at, kt=kt: e.tensor_scalar(kt[:], at[:], 1.0 / TWO_PI, MAGIC, ALU.mult, ALU.add), reads=[ab], writes=[kb_])
                            S.op("dve", lambda e, kt=kt: e.tensor_scalar(kt[:], kt[:], -MAGIC, -TWO_PI, ALU.add, ALU.mult), reads=[kb_], writes=[kb_])
                            S.op("dve", lambda e, at=at, kt=kt: e.tensor_tensor(at[:], at[:], kt[:], ALU.add), reads=[ab, kb_], writes=[ab])
                            S.op("act", lambda e, at=at, dst=dst, ck=ck: e.activation(out=dst[:, ck * 512:(ck + 1) * 512], in_=at[:], func=ACTF.Sin),
                                 reads=[ab], writes=[dstb])
                    sub_stage(nb0)
                with ExitStack() as es2:
                    decr = R(es2, "decr", [128, 2, 512], F32, 2)
                    hfr = R(es2, "hfr", [128, 4, 512], F32, 2)
                    for tb in range(KB):
                        dt_, db_ = decr.next()
                        S.dma("sp", dt_[:, 0, :], tab["decF"][tb * 128:(tb + 1) * 128, :], writes=[db_])
                        S.dma("sp", dt_[:, 1, :], tab["decB"][tb * 128:(tb + 1) * 128, :], writes=[db_], disjoint=True)
                        ht, hb = hfr.next()
                        for q in range(4):
                            pt, pb = PS.next()
                            S.op("pe", lambda e, pt=pt, tb=tb, q=q: e.matmul(pt[:], hd2[:, tb * 128:(tb + 1) * 128], w3s[:, q * 512:(q + 1) * 512], start=True, stop=True),
                                 reads=[b_hd2, b_w3s], writes=[pb])
                            S.op("dve", lambda e, ht=ht, pt=pt, dt_=dt_, q=q: e.tensor_tensor(ht[:, q, :], pt[:], dt_[:, q % 2, :], ALU.mult),
                                 reads=[pb, db_], writes=[hb])
                        if tb == 0:
                            for o in range(2):
                                S.op("dve", lambda e, ht=ht, o=o: e.memset(ht[0:1, 2 * o + 1, :], 0.0), writes=[hb])
                                S.op("dve", lambda e, ht=ht, o=o: e.tensor_tensor(ht[0:1, 2 * o, :], ht[0:1, 2 * o, :], skr[0:1, o, :], ALU.add),
                                     reads=[hb, b_skr], writes=[hb])
                        for o in range(2):
                            S.op("dve", lambda e, ht=ht, o=o, tb=tb: e.tensor_tensor(HSUM[:, tb, o, :], ht[:, 2 * o, :], ht[:, 2 * o + 1, :], ALU.add),
                                 reads=[hb], writes=[b_HSUM])
                            S.op("pool", lambda e, ht=ht, o=o, tb=tb: e.tensor_tensor(HDIF[:, tb, o, :], ht[:, 2 * o, :], ht[:, 2 * o + 1, :], ALU.subtract),
                                 reads=[hb], writes=[b_HDIF])
                    sub_stage(nb0)
                slabC = R(es, "slabC", [128, KB, 128], BF16, 2)
                slabS = R(es, "slabS", [128, KB, 128], BF16, 2)
                outr = R(es, "outr", [128, 512], F32, 4)
                sc_all = 1.0 / L
                for j in range(KB):
                    ct, cb_ = slabC.next()
                    stt, sb_ = slabS.next()
                    S.dma("sp", ct[:], tab["TC"][j], writes=[cb_])
                    S.dma("sp", stt[:], tab["TSF"][j], writes=[sb_])
                    for o in range(2):
                        pP, pPb = PS.next()
                        for kb in range(KB):
                            S.op("pe", lambda e, pP=pP, ct=ct, kb=kb, o=o, KB=KB: e.matmul(pP[:], ct[:, kb, :], HSUM[:, kb, o, :], start=(kb == 0), stop=(kb == KB - 1)),
                                 reads=[cb_, b_HSUM], writes=[pPb], signal=(kb == KB - 1))
                        pQ, pQb = PS.next()
                        for kb in range(KB):
                            S.op("pe", lambda e, pQ=pQ, stt=stt, kb=kb, o=o, KB=KB: e.matmul(pQ[:], stt[:, kb, :], HDIF[:, kb, o, :], start=(kb == 0), stop=(kb == KB - 1)),
                                 reads=[sb_, b_HDIF], writes=[pQb], signal=(kb == KB - 1))
                        oP, oPb = outr.next()
                        oQ, oQb = outr.next()
                        S.op("act", lambda e, oP=oP, pP=pP: e.activation(out=oP[:], in_=pP[:], func=ACTF.Copy, scale=sc_all), reads=[pPb], writes=[oPb])
                        S.op("dve", lambda e, oQ=oQ, pQ=pQ: e.tensor_scalar_mul(oQ[:], pQ[:], sc_all), reads=[pQb], writes=[oQb])
                        if j == 0:
                            pN, pNb = PS.next()
                            for kb in range(KB):
                                S.op("pe", lambda e, pN=pN, kb=kb, o=o, KB=KB: e.matmul(pN[0:1, :], alt[:, 0:1], HSUM[:, kb, o, :], start=(kb == 0), stop=(kb == KB - 1)),
                                     reads=[b_alt, b_HSUM], writes=[pNb], signal=(kb == KB - 1))
                            S.op("dve", lambda e, oP=oP: e.tensor_scalar_mul(oP[0:1, :], oP[0:1, :], 0.5), reads=[oPb], writes=[oPb])
                            S.op("dve", lambda e, oQ=oQ, pN=pN: e.tensor_scalar_mul(oQ[0:1, :], pN[0:1, :], 0.5 * sc_all), reads=[pNb, oQb], writes=[oQb])
                        S.dma("pool", tab["HS"][o, j], oP[:], reads=[oPb])
                        S.dma("pool", tab["HS"][o, KB + j], oQ[:], reads=[oQb])
                end_stage()

        all_chunks = [(si, sq, ck) for si, sq in enumerate(seqs) for ck in range(sq["L"] // 512)]
        with ExitStack() as es:
            Wa, b_Wa = T(es, "Wa", [128, 8, 1024], BF16)
            Wg, b_Wg = T(es, "Wg", [128, 8, 2048], BF16)
            nb0 = len(stage_bufs)
            with ExitStack() as es2:
                stg = R(es2, "stg", [128, 8, 256], F32, 3)
                load_w_bf16(Wa, b_Wa, w_in[:, 0:1024], D, 1024, stg, 256)
                load_w_bf16(Wg, b_Wg, w_in[:, 2560:4608], D, 2048, stg, 256)
                sub_stage(nb0)
            xr = R(es, "xr", [128, D], F32, 6)
            lnb = ln_rings(es, "a1", 4, 8)
            hTt = [T(es, "hTc%d" % i, [128, 8, 512], BF16)[0] for i in range(2)]
            hTb = [[Buf("hTc%d_%d" % (i, t)) for t in range(4)] for i in range(2)]
            sgr = R(es, "sgr", [128, 512], F32, 2)
            abuf = R(es, "abuf", [128, 4, 512], BF16, 2)
            gbuf = R(es, "gbuf", [128, 16, 512], BF16, 2)

            def a1_phaseA(n):
                si, sq, ck = all_chunks[n]
                tiles = []
                for tt in range(4):
                    xt, xb = xr.next()
                    S.dma("sp", xt[:], xrows(sq, ck * 512 + tt * 128, 128), writes=[xb])
                    tiles.append((xt[:], xb))
                return ln_phaseA(lnb, tiles)

            def a1_phaseB(n, xns):
                si, sq, ck = all_chunks[n]
                hTc = hTt[n % 2]
                ln_phaseB(xns, lambda t, kb, hTc=hTc: hTc[:, kb, t * 128:(t + 1) * 128], hTb[n % 2], si, 0, 8)

            a1_phaseB(0, a1_phaseA(0))
            for n, (si, sq, ck) in enumerate(all_chunks):
                c0 = ck * 512
                g0 = sq["tok0"] + c0
                hTc = hTt[n % 2]
                hb_ = hTb[n % 2]
                at, ab = abuf.next()
                for cb in range(4):
                    pv_, pvb = PS.next()
                    pg_, pgb = PS.next()
                    for kb in range(8):
                        S.op("pe", lambda e, pv_=pv_, kb=kb, cb=cb, hTc=hTc: e.matmul(pv_[:], Wa[:, kb, cb * 128:(cb + 1) * 128], hTc[:, kb, :], start=(kb == 0), stop=(kb == 7)),
                             reads=[b_Wa] + hb_, writes=[pvb], signal=(kb == 7))
                    for kb in range(8):
                        S.op("pe", lambda e, pg_=pg_, kb=kb, cb=cb, hTc=hTc: e.matmul(pg_[:], Wa[:, kb, 512 + cb * 128:512 + (cb + 1) * 128], hTc[:, kb, :], start=(kb == 0), stop=(kb == 7)),
                             reads=[b_Wa] + hb_, writes=[pgb], signal=(kb == 7))
                    sg, sgb = sgr.next()
                    S.op("act", lambda e, sg=sg, pg_=pg_: e.activation(out=sg[:], in_=pg_[:], func=ACTF.Sigmoid), reads=[pgb], writes=[sgb])
                    S.op("dve", lambda e, at=at, cb=cb, pv_=pv_, sg=sg: e.tensor_tensor(at[:, cb, :], pv_[:], sg[:], ALU.mult), reads=[pvb, sgb], writes=[ab], disjoint=True)
                S.dma("pool", A_s[:, g0:g0 + 512].rearrange("(cb p) t -> p cb t", p=128), at[:], reads=[ab])
                xns = a1_phaseA(n + 1) if n + 1 < len(all_chunks) else None
                gt, gb = gbuf.next()
                for gi in range(16):
                    if gi == 8 and xns is not None:
                        a1_phaseB(n + 1, xns)
                    pt, pb = PS.next()
                    for kb in range(8):
                        S.op("pe", lambda e, pt=pt, kb=kb, gi=gi, hTc=hTc: e.matmul(pt[:], Wg[:, kb, gi * 128:(gi + 1) * 128], hTc[:, kb, :], start=(kb == 0), stop=(kb == 7)),
                             reads=[b_Wg] + hb_, writes=[pb], signal=(kb == 7))
                    S.op("act", lambda e, pt=pt, gi=gi, gt=gt: e.activation(out=gt[:, gi, :], in_=pt[:], func=ACTF.Sigmoid), reads=[pb], writes=[gb], disjoint=True)
                S.dma("pool", G_s[:, g0:g0 + 512].rearrange("(gi p) t -> p gi t", p=128), gt[:], reads=[gb])
            end_stage()

        with ExitStack() as es:
            LMAX = max(sq["L"] for sq in seqs)
            hT, _ = T(es, "hT", [128, 8, LMAX + 2], BF16)
            hTtb = [Buf("hTt%d" % t) for t in range(LMAX // 128)]
            b_halo = Buf("halo")
            Why, b_Why = T(es, "Why", [128, 8, 1536], BF16)
            nb0 = len(stage_bufs)
            with ExitStack() as es2:
                stg = R(es2, "stg2", [128, 8, 256], F32, 3)
                load_w_bf16(Why, b_Why, w_in[:, 1024:2560], D, 1536, stg, 256)
                sub_stage(nb0)
            hswT, b_hswT = T(es, "hswT", [128, 12, 3])
            hsbT, b_hsbT = T(es, "hsbT", [128, 12])
            xr = R(es, "xr2", [128, D], F32, 4)
            lnb = ln_rings(es, "a2", 4, 8)
            halr = R(es, "halr", [128, 12, 2], F32, 2)
            ber = R(es, "ber", [128, 2, 12], F32, 2)
            u32 = R(es, "u32", [128, 512], F32, 3)
            ubf = R(es, "ubf", [128, 512], BF16, 4)
            obuf = R(es, "obufA", [128, 4, 512], BF16, 3)
            S.dma("sp", hswT[:], hs_wT, writes=[b_hswT])
            S.dma("sp", hsbT[:], hs_bT, writes=[b_hsbT])
            for si, sq in enumerate(seqs):
                L = sq["L"]
                NTL = L // 128

                def a2_phaseA(tl, sq=sq):
                    tiles = []
                    for t in tl:
                        xt, xb = xr.next()
                        S.dma("sp", xt[:], xrows(sq, t * 128, 128), writes=[xb])
                        tiles.append((xt[:], xb))
                    return ln_phaseA(lnb, tiles)

                def a2_phaseB(tl, xns, si=si):
                    ln_phaseB(xns, lambda t, kb, tl=tl: hT[:, kb, 1 + tl[t] * 128: 1 + (tl[t] + 1) * 128], [hTtb[t] for t in tl], si, 0, 8)

                S.op("pool", lambda e: e.memset(hT[:, :, 0:1], 0.0), writes=[b_halo])
                S.op("pool", lambda e, L=L: e.memset(hT[:, :, L + 1:L + 2], 0.0), writes=[b_halo], disjoint=True)
                groups = [[0]] + [[t for t in range(4 * g + 1, 4 * g + 5) if t < NTL] for g in range(L // 512)]
                groups = [g for g in groups if g]
                a2_phaseB(groups[0], a2_phaseA(groups[0]))
                a2_phaseB(groups[1], a2_phaseA(groups[1]))
                for ck in range(L // 512):
                    c0 = ck * 512
                    g0 = sq["tok0"] + c0
                    nxt = groups[ck + 2] if ck + 2 < len(groups) else None
                    xns = a2_phaseA(nxt) if nxt else None
                    tb0 = c0 // 128
                    rd_main = [hTtb[t] for t in range(tb0, tb0 + 4)] + [b_Why]
                    rd_halo = [hTtb[t] for t in (tb0 - 1, tb0 + 4) if 0 <= t < NTL] + [b_halo, b_Why]
                    ph, phb = PS.next()
                    for blk in range(12):
                        for kb in range(8):
                            S.op("pe", lambda e, ph=ph, blk=blk, kb=kb, c0=c0: e.matmul(ph[:, blk * 2:blk * 2 + 2], Why[:, kb, blk * 128:(blk + 1) * 128], hT[:, kb, c0:c0 + 514:513],
                                                                                    start=(kb == 0), stop=(kb == 7)),
                                 reads=rd_halo, writes=[phb], signal=(blk == 11 and kb == 7))
                    hl, hlb = halr.next()
                    S.op("dve", lambda e, hl=hl, ph=ph: e.tensor_copy(hl[:].rearrange("p a b -> p (a b)"), ph[:, 0:24]), reads=[phb], writes=[hlb])
                    be, beb = ber.next()
                    for side, tap in ((0, 0), (1, 2)):
                        S.op("dve", lambda e, be=be, hl=hl, side=side, tap=tap: e.tensor_tensor(be[:, side, :], hl[:, :, side], hswT[:, :, tap], ALU.mult),
                             reads=[hlb, b_hswT], writes=[beb], disjoint=(side > 0))
                        S.op("dve", lambda e, be=be, side=side: e.tensor_tensor(be[:, side, :], be[:, side, :], hsbT[:], ALU.add),
                             reads=[beb, b_hsbT], writes=[beb])
                    pending = [None]

                    def flush():
                        if pending[0] is None:
                            return
                        which_, cb_, dt_, db_, pT_, ot_, ob_ = pending[0]
                        pending[0] = None
                        for tt in range(4):
                            pt_, ptb_ = pT_[tt // 2]
                            pv = pt_.bitcast(BF16)
                            S.op("pe", lambda e, pv=pv, tt=tt, cb_=cb_, dt_=dt_: e.transpose(pv[:, (tt % 2) * 512 + cb_ * 128:(tt % 2) * 512 + (cb_ + 1) * 128], dt_[:, tt * 128:(tt + 1) * 128], ident[:]),
                                 reads=[db_, b_ident], writes=[ptb_])
                        if cb_ == 3:
                            for h2 in range(2):
                                pt_, ptb_ = pT_[h2]
                                pv = pt_.bitcast(BF16)
                                S.op("act", lambda e, ot_=ot_, pv=pv, h2=h2: e.activation(out=ot_[:, 2 * h2:2 * h2 + 2, :].rearrange("p a b -> p (a b)"), in_=pv[:, 0:1024], func=ACTF.Copy),
                                     reads=[ptb_], writes=[ob_], disjoint=True)
                            dst = (V_s, X1_s, X2_s)[which_]
                            S.dma("pool", dst[g0:g0 + 512, :].rearrange("(tt p) c -> p tt c", p=128), ot_[:], reads=[ob_])
                            if which_ == 1 and xns is not None:
                                a2_phaseB(nxt, xns)

                    for which in range(3):
                        ot, ob = obuf.next()
                        pT = None
                        for cb in range(4):
                            blk = which * 4 + cb
                            pm, pmb = PS.next()
                            for kb in range(8):
                                S.op("pe", lambda e, pm=pm, blk=blk, kb=kb, c0=c0: e.matmul(pm[:], Why[:, kb, blk * 128:(blk + 1) * 128], hT[:, kb, 1 + c0:1 + c0 + 512], start=(kb == 0), stop=(kb == 7)),
                                     reads=rd_main, writes=[pmb], signal=(kb == 7))
                            flush()
                            if cb == 0:
                                pT = [PS.next() for _ in range(2)]
                            u, ub = u32.next()
                            w0 = hswT[:, blk, 0:1]
                            w1 = hswT[:, blk, 1:2]
                            w2 = hswT[:, blk, 2:3]
                            S.op("act", lambda e, u=u, pm=pm, w1=w1, blk=blk: e.activation(out=u[:, 1:511], in_=pm[:, 1:511], func=ACTF.Identity, bias=hsbT[:, blk:blk + 1], scale=w1),
                                 reads=[pmb, b_hswT, b_hsbT], writes=[ub])
                            S.op("act", lambda e, u=u, pm=pm, w1=w1, blk=blk, be=be: e.activation(out=u[:, 0:1], in_=pm[:, 0:1], func=ACTF.Identity, bias=be[:, 0, blk:blk + 1], scale=w1),
                                 reads=[pmb, b_hswT, beb], writes=[ub], disjoint=True)
                            S.op("act", lambda e, u=u, pm=pm, w1=w1, blk=blk, be=be: e.activation(out=u[:, 511:512], in_=pm[:, 511:512], func=ACTF.Identity, bias=be[:, 1, blk:blk + 1], scale=w1),
                                 reads=[pmb, b_hswT, beb], writes=[ub], disjoint=True)
                            S.op("dve", lambda e, u=u, pm=pm, w0=w0: e.scalar_tensor_tensor(u[:, 1:512], pm[:, 0:511], w0, u[:, 1:512], ALU.mult, ALU.add),
                                 reads=[pmb, ub, b_hswT], writes=[ub])
                            dt_, db_ = ubf.next()
                            dv = dt_
                            S.op("dve", lambda e, u=u, pm=pm, w2=w2, dv=dv: e.scalar_tensor_tensor(dv[:, 0:511], pm[:, 1:512], w2, u[:, 0:511], ALU.mult, ALU.add),
                                 reads=[pmb, ub, b_hswT], writes=[db_])
                            S.op("dve", lambda e, u=u, dv=dv: e.tensor_copy(dv[:, 511:512], u[:, 511:512]), reads=[ub], writes=[db_], disjoint=True)
                            pending[0] = (which, cb, dt_, db_, pT, ot, ob)
                    flush()
            end_stage()

        with ExitStack() as es:
            LMAX = max(sq["L"] for sq in seqs)
            KBM = LMAX // 128
            vb, b_vb = T(es, "vb", [128, KBM, 512], BF16)
            Yb, b_Yb = T(es, "Yb", [128, 2 * KBM, 512], BF16)
            slabC = R(es, "slabCb", [128, KBM, 128], BF16, 2)
            slabS = R(es, "slabSb", [128, KBM, 128], BF16, 2)
            hsr = R(es, "hsr", [128, 2, 512], F32, 2)
            abr = R(es, "abr", [128, 2, 512], F32, 2)
            tmr = R(es, "tmr", [128, 4, 512], F32, 2)
            x1r = R(es, "x1r", [128, 512], BF16, 3)
            ztk = R(es, "ztk", [128, 512], BF16, 3)
            zbr = R(es, "zbr", [128, 4, 512], BF16, 2)
            for si, sq in enumerate(seqs):
                L = sq["L"]
                KB = L // 128
                tab = TAB[L]
                tok0 = sq["tok0"]
                S.dma("sp", vb[:, 0:KB, :], V_s[tok0:tok0 + L, :].rearrange("(kb p) c -> p kb c", p=128), writes=[b_vb])
                for order in range(2):
                    for j in range(KB):
                        ct, cb_ = slabC.next()
                        stt, sb_ = slabS.next()
                        S.dma("sp", ct[:, 0:KB, :], tab["TC"][j], writes=[cb_])
                        S.dma("sp", stt[:, 0:KB, :], tab["TSF"][j], writes=[sb_])
                        ht, hb = hsr.next()
                        S.dma("sp", ht[:, 0, :], tab["HS"][order, j], writes=[hb])
                        S.dma("sp", ht[:, 1, :], tab["HS"][order, KB + j], writes=[hb], disjoint=True)
                        pA, pAb = PS.next()
                        for kb in range(KB):
                            S.op("pe", lambda e, pA=pA, ct=ct, kb=kb, KB=KB: e.matmul(pA[:], ct[:, kb, :], vb[:, kb, :], start=(kb == 0), stop=(kb == KB - 1)),
                                 reads=[cb_, b_vb], writes=[pAb], signal=(kb == KB - 1))
                        pB, pBb = PS.next()
                        for kb in range(KB):
                            S.op("pe", lambda e, pB=pB, stt=stt, kb=kb, KB=KB: e.matmul(pB[:], stt[:, kb, :], vb[:, kb, :], start=(kb == 0), stop=(kb == KB - 1)),
                                 reads=[sb_, b_vb], writes=[pBb], signal=(kb == KB - 1))
                        ab_t, ab_b = abr.next()
                        S.op("act", lambda e, ab_t=ab_t, pA=pA: e.activation(out=ab_t[:, 0, :], in_=pA[:], func=ACTF.Copy), reads=[pAb], writes=[ab_b])
                        S.op("act", lambda e, ab_t=ab_t, pB=pB: e.activation(out=ab_t[:, 1, :], in_=pB[:], func=ACTF.Copy), reads=[pBb], writes=[ab_b])
                        tm, tmb = tmr.next()
                        S.op("dve", lambda e, tm=tm, ab_t=ab_t, ht=ht: e.tensor_tensor(tm[:, 0, :], ab_t[:, 0, :], ht[:, 0, :], ALU.mult), reads=[ab_b, hb], writes=[tmb])
                        S.op("pool", lambda e, tm=tm, ab_t=ab_t, ht=ht: e.tensor_tensor(tm[:, 1, :], ab_t[:, 1, :], ht[:, 1, :], ALU.mult), reads=[ab_b, hb], writes=[tmb])
                        S.op("pool", lambda e, tm=tm, ab_t=ab_t, ht=ht: e.tensor_tensor(tm[:, 2, :], ab_t[:, 0, :], ht[:, 1, :], ALU.mult), reads=[ab_b, hb], writes=[tmb])
                        S.op("dve", lambda e, tm=tm, ab_t=ab_t, ht=ht: e.tensor_tensor(tm[:, 3, :], ab_t[:, 1, :], ht[:, 0, :], ALU.mult), reads=[ab_b, hb], writes=[tmb])
                        S.op("dve", lambda e, tm=tm, j=j: e.tensor_tensor(Yb[:, j, :], tm[:, 0, :], tm[:, 1, :], ALU.subtract), reads=[tmb], writes=[b_Yb])
                        S.op("pool", lambda e, tm=tm, j=j, KB=KB: e.tensor_tensor(Yb[:, KB + j, :], tm[:, 2, :], tm[:, 3, :], ALU.add), reads=[tmb], writes=[b_Yb])
                        if j == 0:
                            S.op("dve", lambda e, tm=tm: e.tensor_copy(Yb[0:1, 0, :], tm[0:1, 0, :]), reads=[tmb], writes=[b_Yb])
                            S.op("dve", lambda e, tm=tm, KB=KB: e.tensor_copy(Yb[0:1, KB, :], tm[0:1, 1, :]), reads=[tmb], writes=[b_Yb])
                    if order == 0:
                        for tb in range(KB):
                            ct, cb_ = slabC.next()
                            stt, sb_ = slabS.next()
                            S.dma("sp", ct[:, 0:KB, :], tab["TC"][tb], writes=[cb_])
                            S.dma("sp", stt[:, 0:KB, :], tab["TSI"][tb], writes=[sb_])
                            x1t, x1b = x1r.next()
                            S.dma("sp", x1t[:], X1_s[tok0 + tb * 128: tok0 + (tb + 1) * 128, :], writes=[x1b])
                            py, pyb = PS.next()
                            for fb in range(KB):
                                S.op("pe", lambda e, py=py, ct=ct, fb=fb: e.matmul(py[:], ct[:, fb, :], Yb[:, fb, :], start=(fb == 0), stop=False),
                                     reads=[cb_, b_Yb], writes=[pyb], signal=False)
                            for fb in range(KB):
                                S.op("pe", lambda e, py=py, stt=stt, fb=fb, KB=KB: e.matmul(py[:], stt[:, fb, :], Yb[:, KB + fb, :], start=False, stop=(fb == KB - 1)),
                                     reads=[sb_, b_Yb], writes=[pyb], signal=(fb == KB - 1))
                            S.op("dve", lambda e, py=py, x1t=x1t, tb=tb: e.tensor_tensor(vb[:, tb, :], py[:], x1t[:], ALU.mult), reads=[pyb, x1b], writes=[b_vb])
                    else:
                        pend = [None]

                        def flush_t(tok0=tok0):
                            if pend[0] is None:
                                return
                            tb_, zk, zkb, zt_, ztb_ = pend[0]
                            pend[0] = None
                            pz, pzb = PS.next()
                            pzv = pz.bitcast(BF16)
                            for cb in range(4):
                                S.op("pe", lambda e, pzv=pzv, cb=cb, zk=zk: e.transpose(pzv[:, cb * 128:(cb + 1) * 128], zk[:, cb * 128:(cb + 1) * 128], ident[:]),
                                     reads=[zkb, b_ident], writes=[pzb], signal=(cb == 3))
                            q = tb_ % 4
                            S.op("act", lambda e, pzv=pzv, zt_=zt_, q=q: e.activation(out=zt_[:, :, q * 128:(q + 1) * 128], in_=pzv[:, 0:512].rearrange("p (a b) -> p a b", a=4), func=ACTF.Copy),
                                 reads=[pzb], writes=[ztb_], disjoint=True)
                            if q == 3:
                                g0 = tok0 + (tb_ // 4) * 512
                                S.dma("pool", ZB_s[:, g0:g0 + 512].rearrange("(cb p) t -> p cb t", p=128), zt_[:], reads=[ztb_])

                        zt, zb_ = None, None
                        for tb in range(KB):
                            if tb % 4 == 0:
                                zt, zb_ = zbr.next()
                            ct, cb_ = slabC.next()
                            stt, sb_ = slabS.next()
                            S.dma("sp", ct[:, 0:KB, :], tab["TC"][tb], writes=[cb_])
                            S.dma("sp", stt[:, 0:KB, :], tab["TSI"][tb], writes=[sb_])
                            x2t, x2b = x1r.next()
                            S.dma("sp", x2t[:], X2_s[tok0 + tb * 128: tok0 + (tb + 1) * 128, :], writes=[x2b])
                            py, pyb = PS.next()
                            for fb in range(KB):
                                S.op("pe", lambda e, py=py, ct=ct, fb=fb: e.matmul(py[:], ct[:, fb, :], Yb[:, fb, :], start=(fb == 0), stop=False),
                                     reads=[cb_, b_Yb], writes=[pyb], signal=False)
                            for fb in range(KB):
                                S.op("pe", lambda e, py=py, stt=stt, fb=fb, KB=KB: e.matmul(py[:], stt[:, fb, :], Yb[:, KB + fb, :], start=False, stop=(fb == KB - 1)),
                                     reads=[sb_, b_Yb], writes=[pyb], signal=(fb == KB - 1))
                            flush_t()
                            zk, zkb = ztk.next()
                            S.op("dve", lambda e, py=py, x2t=x2t, zk=zk: e.tensor_tensor(zk[:], py[:], x2t[:], ALU.mult), reads=[pyb, x2b], writes=[zkb])
                            pend[0] = (tb, zk, zkb, zt, zb_)
                        flush_t()
            end_stage()

        TC_ = 256
        NTT = TC_ // 128
        c_chunks = [(si, sq, ck) for si, sq in enumerate(seqs) for ck in range(sq["L"] // TC_)]
        with ExitStack() as es:
            Wpw, b_Wpw = T(es, "Wpw", [128, 4, D], BF16)
            Whyo, b_Whyo = T(es, "Whyo", [128, 4, D], BF16)
            Wout, b_Wout = T(es, "Wout", [128, 8, D], BF16)
            Lw, b_Lw = T(es, "Lw", [128, 16, 8, 32], BF16)
            nb0 = len(stage_bufs)
            with ExitStack() as es2:
                stg = R(es2, "stgc", [128, 8, 256], F32, 3)
                wsel, b_wsel = T(es2, "wsel", [128, 16, 8])
                E4, b_E4 = T(es2, "E4", [128, 32])
                S.dma("sp", wsel[:], dw_pk, writes=[b_wsel])
                S.op("dve", lambda e: e.tensor_tensor(E4[:], ident_f[:, 0:32], ident_f[:, 32:64], ALU.add), reads=[b_identf], writes=[b_E4])
                S.op("dve", lambda e: e.tensor_tensor(E4[:], E4[:], ident_f[:, 64:96], ALU.add), reads=[b_identf, b_E4], writes=[b_E4])
                S.op("dve", lambda e: e.tensor_tensor(E4[:], E4[:], ident_f[:, 96:128], ALU.add), reads=[b_identf, b_E4], writes=[b_E4])
                load_w_bf16(Wpw, b_Wpw, pw_w, CH, D, stg, 256)
                load_w_bf16(Whyo, b_Whyo, hyo_w, CH, D, stg, 256)
                load_w_bf16(Wout, b_Wout, w_out, D, D, stg, 256)
                for cg in range(16):
                    for g in range(8):
                        eng = "dve" if (cg * 8 + g) % 2 == 0 else "pool"
                        S.op(eng, lambda e, cg=cg, g=g: e.tensor_scalar_mul(Lw[:, cg, g, :], E4[:], wsel[:, cg, g:g + 1]),
                             reads=[b_E4, b_wsel], writes=[b_Lw], disjoint=True)
                sub_stage(nb0)
            cpar, b_cpar = T(es, "cpar", [128, 3, 4])
            g1bc, b_g1bc = T(es, "g1bc", [128, D])
            l1g, b_l1g = T(es, "l1g", [128, D])
            l1b, b_l1b = T(es, "l1b", [128, D])
            xr = R(es, "xrc", [128, D], F32, 3)
            lnb = dict(st=R(es, "stc", [128, 2, 6], F32, 2), mv=R(es, "mvc", [128, 4], F32, 2))
            sgl = R(es, "sgl", [128, 16, TC_], BF16, 2)
            ah = R(es, "ah", [128, 16, TC_ + 30], BF16, 3)
            zbl = R(es, "zbl", [128, 4, TC_], BF16, 2)
            acvr = R(es, "acv", [128, 4, TC_], F32, 2)
            asqr = R(es, "asq", [128, 4, TC_], F32, 2)
            acv_bufs = [[Buf("acv%d_%d" % (i, c)) for c in range(4)] for i in range(2)]
            asq_bufs = [[Buf("asq%d_%d" % (i, c)) for c in range(4)] for i in range(2)]
            b_an4 = [Buf("an%d" % c) for c in range(4)]
            b_mt8 = [Buf("mt%d" % c) for c in range(8)]
            stt_, b_stt = T(es, "stats", [128, 4, TC_])
            an, b_an = T(es, "an", [128, 4, TC_], BF16)
            mt, b_mt = T(es, "mt", [128, 8, TC_], BF16)
            tmp1 = R(es, "tmp1", [128, TC_], F32, 2)
            tmp2 = R(es, "tmp2", [128, TC_], F32, 2)
            rr = R(es, "rr", [128, D], F32, 4)
            x1o = R(es, "x1o", [128, D], F32, 2)
            S.dma("sp", cpar[:, 0, :], dw_bT, writes=[b_cpar])
            S.dma("sp", cpar[:, 1, :], cln_gT, writes=[b_cpar], disjoint=True)
            S.dma("sp", cpar[:, 2, :], cln_bT, writes=[b_cpar], disjoint=True)
            S.dma("sp", l1g[:], ln1_g.partition_broadcast(128), writes=[b_l1g])
            S.dma("sp", l1b[:], ln1_b.partition_broadcast(128), writes=[b_l1b])

            loaded = {}

            def c1_load(n):
                si, sq, ck = c_chunks[n]
                L = sq["L"]
                tok0 = sq["tok0"]
                c0 = ck * TC_
                at, ab = ah.next()
                W_ = TC_ + 28
                edge = (c0 - 15 < 0) or (c0 - 15 + 3 + W_ > L)
                if edge:
                    S.op("pool", lambda e, at=at: e.memset(at[:], 0.0), writes=[ab])
                for j in range(4):
                    s0 = c0 - 15 + j
                    lo = max(s0, 0)
                    hi = min(s0 + W_, L)
                    S.dma("sp", at[32 * j:32 * (j + 1), :, lo - s0: hi - s0], A_s[:, tok0 + lo: tok0 + hi].rearrange("(cg c) t -> c cg t", c=32),
                          writes=[ab], disjoint=(not edge))
                loaded[n] = (at, ab)

            def c1_conv(n):
                at, ab = loaded.pop(n)
                acv, _ = acvr.next()
                asq, _ = asqr.next()
                k_ = (acvr.i - 1) % 2
                b_acv = acv_bufs[k_]
                b_asq = asq_bufs[k_]
                for cb in range(4):
                    pt, pb = PS.next()
                    for g in range(8):
                        for i in range(4):
                            cg = cb * 4 + i
                            S.op("pe", lambda e, pt=pt, cg=cg, g=g, i=i, at=at: e.matmul(pt[32 * i:32 * (i + 1), 0:TC_], Lw[:, cg, g, :], at[:, cg, 4 * g:4 * g + TC_],
                                                                                    start=(g == 0), stop=(g == 7), skip_group_check=True, tile_position=(0, 32 * i)),
                                 reads=[b_Lw, ab], writes=[pb], signal=(g == 7 and i == 3))
                    S.op("act", lambda e, pt=pt, cb=cb, acv=acv: e.activation(out=acv[:, cb, :], in_=pt[:, 0:TC_], func=ACTF.Identity, bias=cpar[:, 0, cb:cb + 1]),
                         reads=[pb, b_cpar], writes=[b_acv[cb]])
                    S.op("act", lambda e, pt=pt, cb=cb, asq=asq: e.activation(out=asq[:, cb, :], in_=pt[:, 0:TC_], func=ACTF.Square, bias=cpar[:, 0, cb:cb + 1]),
                         reads=[pb, b_cpar], writes=[b_asq[cb]])
                return acv, b_acv, asq, b_asq

            def c1_epilogue(ep):
                for (rt, rb, dst_rows, dbg_rows) in ep:
                    ot, ob = x1o.next()
                    _ln_aff(S, lnb, rt, rb, ot, ob, l1g, b_l1g, l1b, b_l1b)
                    S.dma("pool", dst_rows, ot[:], reads=[ob])
                    if debug:
                        S.dma("pool", dbg_rows, ot[:], reads=[ob])

            an2 = [T(es, "an2_%d" % i, [128, 4, TC_], BF16)[0] for i in range(2)]
            an2b = [[Buf("an2_%d_%d" % (i, c)) for c in range(4)] for i in range(2)]
            stt2 = [T(es, "stt2_%d" % i, [128, 4, TC_])[0] for i in range(2)]
            stt2b = [Buf("stt2_%d" % i) for i in range(2)]

            def c1_X(n):
                acv, b_acv, asq, b_asq = c1_conv(n)
                st_ = stt2[n % 2]
                b_st = stt2b[n % 2]
                an_ = an2[n % 2]
                b_an_ = an2b[n % 2]
                p1, p1b = PS.next()
                for cb in range(4):
                    S.op("pe", lambda e, p1=p1, cb=cb, acv=acv: e.matmul(p1[:, 0:TC_], ones_f[:], acv[:, cb, :], start=(cb == 0), stop=(cb == 3)),
                         reads=[b_ones, b_acv[cb]], writes=[p1b], signal=(cb == 3))
                p2, p2b = PS.next()
                for cb in range(4):
                    S.op("pe", lambda e, p2=p2, cb=cb, asq=asq: e.matmul(p2[:, 0:TC_], ones_f[:], asq[:, cb, :], start=(cb == 0), stop=(cb == 3)),
                         reads=[b_ones, b_asq[cb]], writes=[p2b], signal=(cb == 3))
                S.op("dve", lambda e, p1=p1: e.tensor_scalar_mul(st_[:, 0, :], p1[:, 0:TC_], 1.0 / CH), reads=[p1b], writes=[b_st])
                S.op("dve", lambda e: e.tensor_tensor(st_[:, 3, :], st_[:, 0, :], st_[:, 0, :], ALU.mult), reads=[b_st], writes=[b_st])
                S.op("dve", lambda e, p2=p2: e.scalar_tensor_tensor(st_[:, 1, :], p2[:, 0:TC_], 1.0 / CH, st_[:, 3, :], ALU.mult, ALU.subtract), reads=[p2b, b_st], writes=[b_st])
                S.op("act", lambda e: e.activation(out=st_[:, 2, :], in_=st_[:, 1, :], func=ACTF.Sqrt, bias=LN_EPS), reads=[b_st], writes=[b_st])
                S.op("dve", lambda e: e.reciprocal(st_[:, 2, :], st_[:, 2, :]), reads=[b_st], writes=[b_st])
                for cb in range(4):
                    S.op("dve", lambda e, cb=cb: e.tensor_tensor(acv[:, cb, :], acv[:, cb, :], st_[:, 0, :], ALU.subtract), reads=[b_acv[cb], b_st], writes=[b_acv[cb]])
                    S.op("dve", lambda e, cb=cb: e.tensor_tensor(acv[:, cb, :], acv[:, cb, :], st_[:, 2, :], ALU.mult), reads=[b_acv[cb], b_st], writes=[b_acv[cb]])
                    S.op("act", lambda e, cb=cb: e.activation(out=an_[:, cb, :], in_=acv[:, cb, :], func=ACTF.Silu, bias=cpar[:, 2, cb:cb + 1], scale=cpar[:, 1, cb:cb + 1]),
                         reads=[b_acv[cb], b_cpar], writes=[b_an_[cb]])

            c1_load(0)
            if len(c_chunks) > 1:
                c1_load(1)
            c1_X(0)
            pend_ep = None
            last_si = -1
            for n, (si, sq, ck) in enumerate(c_chunks):
                L = sq["L"]
                tok0 = sq["tok0"]
                c0 = ck * TC_
                g0 = tok0 + c0
                an_ = an2[n % 2]
                b_an_ = an2b[n % 2]
                sg, b_sg = sgl.next()
                S.dma("sp", sg[:], G_s[:, g0:g0 + TC_].rearrange("(gi p) t -> p gi t", p=128), writes=[b_sg])
                zt, zb_ = zbl.next()
                S.dma("sp", zt[:], ZB_s[:, g0:g0 + TC_].rearrange("(cb p) t -> p cb t", p=128), writes=[zb_])
                if n + 2 < len(c_chunks):
                    c1_load(n + 2)
                if n + 1 < len(c_chunks):
                    c1_X(n + 1)
                if si != last_si:
                    S.dma("sp", g1bc[:], MODROW[si, 0:D].partition_broadcast(128), writes=[b_g1bc])
                    last_si = si
                for db in range(8):
                    pa, pab = PS.next()
                    for cb in range(4):
                        S.op("pe", lambda e, pa=pa, cb=cb, db=db, an_=an_: e.matmul(pa[:, 0:TC_], Wpw[:, cb, db * 128:(db + 1) * 128], an_[:, cb, :], start=(cb == 0), stop=(cb == 3)),
                             reads=[b_Wpw, b_an_[cb]], writes=[pab], signal=(cb == 3))
                    pbb, pbbb = PS.next()
                    for cb in range(4):
                        S.op("pe", lambda e, pbb=pbb, cb=cb, db=db, zt=zt: e.matmul(pbb[:, 0:TC_], Whyo[:, cb, db * 128:(db + 1) * 128], zt[:, cb, :], start=(cb == 0), stop=(cb == 3)),
                             reads=[b_Whyo, zb_], writes=[pbbb], signal=(cb == 3))
                    t1, t1b = tmp1.next()
                    t2, t2b = tmp2.next()
                    S.op("dve", lambda e, t1=t1, pa=pa, db=db, sg=sg: e.tensor_tensor(t1[:], pa[:, 0:TC_], sg[:, db, :], ALU.mult), reads=[pab, b_sg], writes=[t1b])
                    S.op("dve", lambda e, t2=t2, pbb=pbb, db=db, sg=sg: e.tensor_tensor(t2[:], pbb[:, 0:TC_], sg[:, 8 + db, :], ALU.mult), reads=[pbbb, b_sg], writes=[t2b])
                    S.op("pool", lambda e, t1=t1, t2=t2, db=db: e.tensor_tensor(mt[:, db, :], t1[:], t2[:], ALU.add), reads=[t1b, t2b], writes=[b_mt8[db]])
                ep = []
                for tt in range(NTT):
                    xt, xb = xr.next()
                    S.dma("sp", xt[:], xrows(sq, c0 + tt * 128, 128), writes=[xb])
                    rt, rb = rr.next()
                    for half in range(2):
                        pm, pmb = PS.next()
                        for kb in range(8):
                            S.op("pe", lambda e, pm=pm, kb=kb, tt=tt, half=half: e.matmul(pm[:], mt[:, kb, tt * 128:(tt + 1) * 128], Wout[:, kb, half * 512:(half + 1) * 512], start=(kb == 0), stop=(kb == 7)),
                                 reads=[b_mt8[kb], b_Wout], writes=[pmb], signal=(kb == 7))
                        S.op("dve", lambda e, rt=rt, pm=pm, half=half: e.tensor_tensor(rt[:, half * 512:(half + 1) * 512], pm[:], g1bc[:, half * 512:(half + 1) * 512], ALU.mult),
                             reads=[pmb, b_g1bc], writes=[rb], disjoint=(half > 0))
                    S.op("dve", lambda e, rt=rt, xt=xt: e.scalar_tensor_tensor(rt[:], xt[:], ALPHA, rt[:], ALU.mult, ALU.add), reads=[xb, rb], writes=[rb])
                    ep.append((rt, rb, yrows(sq, c0 + tt * 128, 128), X1DBG[g0 + tt * 128:g0 + (tt + 1) * 128, :] if debug else None))
                if pend_ep is not None:
                    c1_epilogue(pend_ep)
                pend_ep = ep
            if pend_ep is not None:
                c1_epilogue(pend_ep)
            end_stage()

        with ExitStack() as es:
            W1, b_W1 = T(es, "W1", [128, 8, DFF], BF16)
            W2, b_W2 = T(es, "W2", [128, 32, D], BF16)
            nb0 = len(stage_bufs)
            with ExitStack() as es2:
                stg = R(es2, "stgd", [128, 8, 256], F32, 2)
                stg2 = R(es2, "stgd2", [128, 32, 64], F32, 2)
                load_w_bf16(W1, b_W1, mlp_w1, D, DFF, stg, 256)
                load_w_bf16(W2, b_W2, mlp_w2, DFF, D, stg2, 64)
                sub_stage(nb0)
            b1T, b_b1T = T(es, "b1T", [128, 32])
            g2bc, b_g2bc = T(es, "g2bc", [128, D])
            l2g, b_l2g = T(es, "l2g", [128, D])
            l2b, b_l2b = T(es, "l2b", [128, D])
            b2f, b_b2f = T(es, "b2f", [1, D])
            b2r, b_b2r = T(es, "b2r", [1, D], BF16)
            onesr, b_onesr = T(es, "onesr", [1, 128], BF16)
            xr = R(es, "xrd", [128, D], F32, 2 * NTT)
            lnb = ln_rings(es, "c2", NTT, NTT)
            lnb2 = dict(st=R(es, "std", [128, 2, 6], F32, 2), mv=R(es, "mvd", [128, 4], F32, 2))
            hTt = [T(es, "hTd%d" % i, [128, 8, TC_], BF16)[0] for i in range(2)]
            hTb = [[Buf("hTd%d_%d" % (i, t)) for t in range(NTT)] for i in range(2)]
            fT, _ = T(es, "fT", [128, 32, TC_], BF16)
            fTb = [Buf("fT%d" % i) for i in range(32)]
            rl = R(es, "rl", [128, TC_], F32, 2)
            rr = R(es, "rrd", [128, D], F32, 1)
            yo = R(es, "yo", [128, D], F32, 2)
            S.dma("sp", b1T[:], mlp_b1T, writes=[b_b1T])
            S.dma("sp", l2g[:], ln2_g.partition_broadcast(128), writes=[b_l2g])
            S.dma("sp", l2b[:], ln2_b.partition_broadcast(128), writes=[b_l2b])
            S.dma("sp", b2f[:], mlp_b2.rearrange("(a d) -> a d", a=1), writes=[b_b2f])
            S.op("dve", lambda e: e.tensor_copy(b2r[:], b2f[:]), reads=[b_b2f], writes=[b_b2r])
            S.op("pool", lambda e: e.memset(onesr[:], 1.0), writes=[b_onesr])

            def c2_phaseA(n):
                si, sq, ck = c_chunks[n]
                tiles = []
                for tt in range(NTT):
                    xt, xb = xr.next()
                    S.dma("sp", xt[:], yrows(sq, ck * TC_ + tt * 128, 128), writes=[xb])
                    tiles.append((xt[:], xb))
                return tiles, ln_phaseA(lnb, tiles)

            def c2_phaseB(n, xns):
                si, sq, ck = c_chunks[n]
                hTc = hTt[n % 2]
                ln_phaseB(xns, lambda t, kb, hTc=hTc: hTc[:, kb, t * 128:(t + 1) * 128], hTb[n % 2], si, 24, 32)

            tiles_cur, xns0 = c2_phaseA(0)
            c2_phaseB(0, xns0)
            last_si = -1
            for n, (si, sq, ck) in enumerate(c_chunks):
                c0 = ck * TC_
                if si != last_si:
                    S.dma("sp", g2bc[:], MODROW[si, D:2 * D].partition_broadcast(128), writes=[b_g2bc])
                    last_si = si
                hTc = hTt[n % 2]
                hb_ = hTb[n % 2]
                nxt = None
                for fb in range(32):
                    if fb == 8 and n + 1 < len(c_chunks):
                        nxt = c2_phaseA(n + 1)
                    pt, pb = PS.next()
                    for kb in range(8):
                        S.op("pe", lambda e, pt=pt, kb=kb, fb=fb, hTc=hTc: e.matmul(pt[:, 0:TC_], W1[:, kb, fb * 128:(fb + 1) * 128], hTc[:, kb, :], start=(kb == 0), stop=(kb == 7)),
                             reads=[b_W1] + hb_, writes=[pb], signal=(kb == 7))
                    rt_, rtb = rl.next()
                    S.op("act", lambda e, rt_=rt_, pt=pt, fb=fb: e.activation(out=rt_[:], in_=pt[:, 0:TC_], func=ACTF.Relu, bias=b1T[:, fb:fb + 1]), reads=[pb, b_b1T], writes=[rtb])
                    eng = "dve" if fb % 2 == 0 else "pool"
                    S.op(eng, lambda e, rt_=rt_, fb=fb: e.tensor_tensor(fT[:, fb, :], rt_[:], rt_[:], ALU.mult), reads=[rtb], writes=[fTb[fb]])
                if nxt is not None:
                    c2_phaseB(n + 1, nxt[1])
                for tt in range(NTT):
                    xt, xb = tiles_cur[tt]
                    rt, rb = rr.next()
                    for half in range(2):
                        pm, pmb = PS.next()
                        for fb in range(32):
                            S.op("pe", lambda e, pm=pm, fb=fb, tt=tt, half=half: e.matmul(pm[:], fT[:, fb, tt * 128:(tt + 1) * 128], W2[:, fb, half * 512:(half + 1) * 512], start=(fb == 0), stop=False),
                                 reads=[fTb[fb], b_W2], writes=[pmb], signal=False)
                        S.op("pe", lambda e, pm=pm, half=half: e.matmul(pm[:], onesr[:], b2r[:, half * 512:(half + 1) * 512], start=False, stop=True),
                             reads=[b_onesr, b_b2r], writes=[pmb], signal=True)
                        S.op("dve", lambda e, rt=rt, pm=pm, half=half: e.tensor_tensor(rt[:, half * 512:(half + 1) * 512], pm[:], g2bc[:, half * 512:(half + 1) * 512], ALU.mult),
                             reads=[pmb, b_g2bc], writes=[rb], disjoint=(half > 0))
                    S.op("dve", lambda e, rt=rt, xt=xt: e.scalar_tensor_tensor(rt[:], xt, ALPHA, rt[:], ALU.mult, ALU.add), reads=[xb, rb], writes=[rb])
                    ot, ob = yo.next()
                    _ln_aff(S, lnb2, rt, rb, ot, ob, l2g, b_l2g, l2b, b_l2b)
                    S.dma("pool", yrows(sq, c0 + tt * 128, 128), ot[:], reads=[ob])
                if nxt is not None:
                    tiles_cur = nxt[0]
            end_stage()
    return nc


def _ln_aff(S, lnb, rt, rb, ot, ob, g, gb, b, bb):
    st, stb = lnb["st"].next()
    mv, mvb = lnb["mv"].next()
    for i in range(2):
        S.op("dve", lambda e, i=i: e.bn_stats(st[:, i, :], rt[:, i * 512:(i + 1) * 512]), reads=[rb], writes=[stb])
    S.op("dve", lambda e: e.bn_aggr(mv[:, 0:2], st[:].rearrange("p a b -> p (a b)")), reads=[stb], writes=[mvb])
    S.op("act", lambda e: e.activation(out=mv[:, 2:3], in_=mv[:, 1:2], func=ACTF.Sqrt, bias=LN_EPS), reads=[mvb], writes=[mvb])
    S.op("dve", lambda e: e.reciprocal(mv[:, 2:3], mv[:, 2:3]), reads=[mvb], writes=[mvb])
    S.op("dve", lambda e: e.scalar_tensor_tensor(mv[:, 3:4], mv[:, 0:1], -1.0, mv[:, 2:3], ALU.mult, ALU.mult), reads=[mvb], writes=[mvb])
    S.op("act", lambda e: e.activation(out=ot[:], in_=rt[:], func=ACTF.Identity, bias=mv[:, 3:4], scale=mv[:, 2:3]), reads=[rb, mvb], writes=[ob])
    S.op("dve", lambda e: e.tensor_tensor(ot[:], ot[:], g[:], ALU.mult), reads=[ob, gb], writes=[ob])
    S.op("dve", lambda e: e.tensor_tensor(ot[:], ot[:], b[:], ALU.add), reads=[ob, bb], writes=[ob])


_TABLE_CACHE = {}


def _tables(L):
    if L in _TABLE_CACHE:
        return _TABLE_CACHE[L]
    KB = L // 128
    n = np.arange(L, dtype=np.int64)
    m = (n[:, None] * n[None, :]) % (2 * L)
    ang = m.astype(np.float64) * (math.pi / L)
    C = np.cos(ang)
    Sn = np.sin(ang)
    alt = np.where(n % 2 == 0, 1.0, -1.0)
    SF = Sn.copy()
    SF[:, 0] = alt
    SI = Sn.copy()
    SI[0, :] = alt

    def lay(M):
        return np.ascontiguousarray(M.reshape(KB, 128, KB, 128).transpose(2, 1, 0, 3)).astype(ml_dtypes.bfloat16)

    t = np.linspace(0.0, 1.0, L, dtype=np.float32)[:, None]
    w = ((2.0 * math.pi / L) * np.arange(L, dtype=np.float32))[:, None].astype(np.float32)
    bands = np.linspace(1e-4, 15, 16, dtype=np.float32)[None, :]
    z = np.concatenate([t, np.cos(w * bands), -np.sin(w * bands)], axis=-1).astype(np.float32)
    max_decay = math.log(1e-2) / 0.3
    min_decay = math.log(1e-2) / 1.5
    deltas = np.abs(np.linspace(min_decay, max_decay, CH, dtype=np.float32))
    decF = np.exp(-t * deltas[None, :]).astype(np.float32)
    decB = np.exp(-t * deltas[::-1][None, :]).astype(np.float32)
    out = dict(zT=np.ascontiguousarray(z.T), decF=decF, decB=decB, TC=lay(C), TSF=lay(SF), TSI=lay(SI))
    _TABLE_CACHE[L] = out
    return out


def _dw_pack(w):
    wp = np.zeros((32, 512), np.float32)
    wp[:31] = w
    return np.ascontiguousarray(wp.reshape(8, 4, 16, 32).transpose(1, 3, 2, 0).reshape(128, 16, 8))


def _colT(v, nblk):
    return np.ascontiguousarray(np.asarray(v, np.float32).reshape(nblk, 128).T)


def run(cfg, inputs, ncores, debug=False):
    NP, LP, NS, LS = cfg["NP"], cfg["LP"], cfg["NS"], cfg["LS"]
    f = lambda k: np.ascontiguousarray(np.asarray(inputs[k], dtype=np.float32))
    shared = dict(
        altc=np.ascontiguousarray(np.stack([np.where(np.arange(128) % 2 == 0, 1.0, -1.0)] * 2, axis=1).astype(np.float32)),
        w_ada=f("w_ada")[0], b_ada=f("b_ada")[0], b_adaT=_colT(f("b_ada")[0], 48),
        w_in=f("w_in")[0],
        dw_wT=np.ascontiguousarray(f("conv_dw_w")[0].reshape(31, 4, 128).transpose(2, 1, 0)),
        dw_pk=_dw_pack(f("conv_dw_w")[0]),
        dw_bT=_colT(f("conv_dw_b")[0], 4), cln_gT=_colT(f("conv_ln_g")[0], 4), cln_bT=_colT(f("conv_ln_b")[0], 4),
        pw_w=f("conv_pw_w")[0], hs_w=f("hy_short_w")[0], hs_b=f("hy_short_b")[0], hs_bT=_colT(f("hy_short_b")[0], 12),
        hs_wT=np.ascontiguousarray(f("hy_short_w")[0].reshape(3, 12, 128).transpose(2, 1, 0)),
        f_w1=f("hy_ffn_w1")[0], f_b1=f("hy_ffn_b1")[0].reshape(64, 1), f_fr=f("hy_sin_freq")[0].reshape(64, 1),
        f_w2=f("hy_ffn_w2")[0], f_b2=f("hy_ffn_b2")[0].reshape(64, 1), f_w3=f("hy_ffn_w3")[0],
        hy_skip=f("hy_skip")[0], hyo_w=f("hy_out_w")[0], w_out=f("w_out")[0],
        ln1_g=f("ln1_g")[0], ln1_b=f("ln1_b")[0], mlp_w1=f("mlp_w1")[0], mlp_b1T=_colT(f("mlp_b1")[0], 32),
        mlp_w2=f("mlp_w2")[0], mlp_b2=f("mlp_b2")[0], ln2_g=f("ln2_g")[0], ln2_b=f("ln2_b")[0],
    )
    Ls = sorted(set(([LP] if NP else []) + ([LS] if NS else [])))
    for L in Ls:
        for k, v in _tables(L).items():
            shared["%s_%d" % (k, L)] = v
    xp = f("x_prompt") if NP else None
    xs = f("x_sample") if NS else None
    cp = f("c_prompt") if NP else None
    cs = f("c_sample") if NS else None
    in_maps = []
    for c in range(ncores):
        m = dict(shared)
        cl = []
        if NP:
            m["x_p"] = xp[c * NP:(c + 1) * NP].reshape(NP * LP, D)
            cl.append(cp[c * NP:(c + 1) * NP])
        if NS:
            m["x_s"] = xs[c * NS:(c + 1) * NS].reshape(NS * LS, D)
            cl.append(cs[c * NS:(c + 1) * NS])
        call = np.concatenate(cl, axis=0)
        m["cT"] = np.ascontiguousarray(call.reshape(-1, 8, 128).transpose(2, 1, 0))
        in_maps.append(m)
    nc = build(cfg, debug=debug)
    res = run_bass_kernel_spmd(nc, in_maps, core_ids=list(range(ncores)))
    outs = []
    if NP:
        outs.append(np.concatenate([np.asarray(r["y_p"], np.float32).reshape(NP, LP, D) for r in res.results], axis=0))
    if NS:
        outs.append(np.concatenate([np.asarray(r["y_s"], np.float32).reshape(NS, LS, D) for r in res.results], axis=0))
    return tuple(outs), res


def kernel(**inputs):
    cfg = dict(NP=4, LP=2048, NS=2, LS=4096)
    outs, _ = run(cfg, inputs, NCORES)
    return outs
```
